# Optimizing a Trainium2 kernel written in Bass

```python
import functools
import jax
import jax.numpy as jnp
from jax import lax
import numpy as np

D_MODEL = 1024
BATCH = 8
SEQ = 2048
DEPTH = 1
DEC_BATCH = 128
DEC_SEQ = 4
PAST_LEN = 16384
PAGE_SIZE = 128

N_META = 16
MIX_WIDTH = D_MODEL
POOL_WIDTH = MIX_WIDTH // 2
POOL_WINDOWS = (2, 4, 8, 16)
POOL_GROUPS = len(POOL_WINDOWS)
POOL_GC = POOL_WIDTH // POOL_GROUPS
POOL_CTX = max(POOL_WINDOWS) - 1
MLSTM_WIDTH = MIX_WIDTH - POOL_WIDTH
N_HEADS = 4
HEAD_DIM = MLSTM_WIDTH // N_HEADS
CHUNK = 64
D_FF = 4 * D_MODEL
EPS = 1e-6
IN_COLS = POOL_WIDTH + 4 * MLSTM_WIDTH + 2 * N_HEADS

kernel_name = "hymba_pool_mlstm_decoder_step"


def rmsnorm(x, g):
    x32 = x.astype(jnp.float32)
    y = x32 * lax.rsqrt(jnp.mean(x32 * x32, axis=-1, keepdims=True) + EPS) * g.astype(jnp.float32)
    return y.astype(x.dtype)


def mixer_inputs(x, norm_g, w_in, b_gate):
    B, T, _ = x.shape
    p = jnp.einsum('btd,dc->btc', rmsnorm(x, norm_g), w_in)
    splits = [POOL_WIDTH, POOL_WIDTH + MLSTM_WIDTH, POOL_WIDTH + 2 * MLSTM_WIDTH,
              POOL_WIDTH + 3 * MLSTM_WIDTH, POOL_WIDTH + 4 * MLSTM_WIDTH]
    u, q, k, v, o, gates = jnp.split(p, splits, axis=-1)

    def heads(z):
        return z.reshape(B, T, N_HEADS, HEAD_DIM).transpose(0, 2, 1, 3).astype(jnp.float32)

    gates = gates.astype(jnp.float32) + b_gate.astype(jnp.float32)
    ig = gates[..., :N_HEADS].transpose(0, 2, 1)
    lf = jax.nn.log_sigmoid(gates[..., N_HEADS:]).transpose(0, 2, 1)
    return u, heads(q) * (HEAD_DIM ** -0.5), heads(k), heads(v), o, ig, lf


def pool_mix(u_ext, pos, w_pool, pool_scale):
    T = u_ext.shape[1] - POOL_CTX
    u32 = u_ext.astype(jnp.float32)
    cs = jnp.pad(jnp.cumsum(u32, axis=1), ((0, 0), (1, 0), (0, 0)))
    u_tok = u32[:, POOL_CTX:]
    outs = []
    for g, w in enumerate(POOL_WINDOWS):
        sl = slice(g * POOL_GC, (g + 1) * POOL_GC)
        wsum = cs[:, POOL_CTX + 1:, sl] - cs[:, POOL_CTX + 1 - w:POOL_CTX + 1 - w + T, sl]
        cnt = jnp.minimum(pos + 1, w).astype(jnp.float32)[None, :, None]
        outs.append(jnp.einsum('btc,cd->btd', wsum / cnt - u_tok[:, :, sl], w_pool[g].astype(jnp.float32)))
    return jnp.concatenate(outs, axis=-1) * pool_scale.astype(jnp.float32)


def mlstm_chunk(carry, inp):
    C0, n0, m0 = carry
    q, k, v, ig, lf = inp
    T = q.shape[2]
    b = jnp.cumsum(lf, axis=-1)
    g = b + m0[..., None]
    mask = jnp.tril(jnp.ones((T, T), dtype=bool))
    dmat = jnp.where(mask, b[..., :, None] - b[..., None, :] + ig[..., None, :], -jnp.inf)
    m = jnp.maximum(g, jnp.max(dmat, axis=-1))
    inter = jnp.exp(g - m)
    qk = jnp.einsum('bhtk,bhsk->bhts', q, k) * jnp.exp(dmat - m[..., None])
    num = inter[..., None] * jnp.einsum('bhtk,bhkv->bhtv', q, C0) + jnp.einsum('bhts,bhsv->bhtv', qk, v)
    den = inter * jnp.einsum('bhtk,bhk->bht', q, n0) + jnp.sum(qk, axis=-1)
    h = num / jnp.maximum(jnp.abs(den), jnp.exp(-m))[..., None]
    m_last = m[..., -1]
    decay0 = jnp.exp(b[..., -1] + m0 - m_last)
    ws = jnp.exp(b[..., -1:] - b + ig - m_last[..., None])
    C1 = decay0[..., None, None] * C0 + jnp.einsum('bhs,bhsk,bhsv->bhkv', ws, k, v)
    n1 = decay0[..., None] * n0 + jnp.einsum('bhs,bhsk->bhk', ws, k)
    return (C1, n1, m_last), h


def mlstm_prompt(q, k, v, ig, lf):
    B, H, L, K = q.shape
    carry = (jnp.zeros((B, H, K, K), jnp.float32), jnp.zeros((B, H, K), jnp.float32),
             jnp.zeros((B, H), jnp.float32))
    carry, h_meta = mlstm_chunk(carry, (q[:, :, :N_META], k[:, :, :N_META], v[:, :, :N_META],
                                        ig[:, :, :N_META], lf[:, :, :N_META]))
    nc = (L - N_META) // CHUNK

    def to_chunks(z):
        z = z[:, :, N_META:]
        return jnp.moveaxis(z.reshape((B, H, nc, CHUNK) + z.shape[3:]), 2, 0)

    carry, h_rest = lax.scan(mlstm_chunk, carry, (to_chunks(q), to_chunks(k), to_chunks(v),
                                                  to_chunks(ig), to_chunks(lf)))
    h_rest = jnp.moveaxis(h_rest, 0, 2).reshape(B, H, nc * CHUNK, K)
    return jnp.concatenate([h_meta, h_rest], axis=2), carry


def mlstm_sample(q, k, v, ig, lf, c0, n0, m0):
    carry = (c0.astype(jnp.float32), n0.astype(jnp.float32), m0.astype(jnp.float32))
    carry, h = mlstm_chunk(carry, (q, k, v, ig, lf))
    return h, carry


def mixer_output(pool_out, h, o, head_gain, w_out, dtype):
    B, H, T, V = h.shape
    h = h * lax.rsqrt(jnp.mean(h * h, axis=-1, keepdims=True) + EPS) * head_gain.astype(jnp.float32)[None, :, None, :]
    h = h.transpose(0, 2, 1, 3).reshape(B, T, MLSTM_WIDTH) * jax.nn.sigmoid(o.astype(jnp.float32))
    mix = jnp.concatenate([pool_out, h], axis=-1).astype(dtype)
    return jnp.einsum('btc,cd->btd', mix, w_out)


def channel_mixer(x, g, w_up, w_down):
    a = jnp.square(jax.nn.relu(jnp.einsum('btd,df->btf', rmsnorm(x, g), w_up)))
    return jnp.einsum('btf,fd->btd', a, w_down)


def block(x, pool_ctx, pos, run_mlstm, norm1, w_in, b_gate, w_pool, pool_scale, head_gain, w_out,
          norm2, w_up, w_down):
    u, q, k, v, o, ig, lf = mixer_inputs(x, norm1, w_in, b_gate)
    u_ext = jnp.concatenate([pool_ctx.astype(u.dtype), u], axis=1)
    pool_out = pool_mix(u_ext, pos, w_pool, pool_scale)
    h, mstate = run_mlstm(q, k, v, ig, lf)
    x = x + mixer_output(pool_out, h, o, head_gain, w_out, x.dtype)
    x = x + channel_mixer(x, norm2, w_up, w_down)
    return x, u_ext[:, -POOL_CTX:], mstate


def setup_inputs(seed: int = 0) -> dict:
    key = jax.random.key(seed)
    ks = jax.random.split(key, 20)
    f32 = jnp.float32
    nrm = lambda k, s, sc: jax.random.normal(k, s, f32) * sc
    b_forget = jnp.linspace(3.0, 6.0, N_HEADS)[None, :] + nrm(ks[9], (DEPTH, N_HEADS), 0.1)
    b_input = nrm(ks[10], (DEPTH, N_HEADS), 0.1)
    return {
        'x_prompt': nrm(ks[0], (BATCH, SEQ, D_MODEL), 1.0),
        'x_sample': nrm(ks[1], (DEC_BATCH, DEC_SEQ, D_MODEL), 1.0),
        'state_pool': nrm(ks[2], (DEPTH, DEC_BATCH, POOL_CTX, POOL_WIDTH), 1.0),
        'state_C': nrm(ks[3], (DEPTH, DEC_BATCH, N_HEADS, HEAD_DIM, HEAD_DIM), 0.3),
        'state_n': nrm(ks[4], (DEPTH, DEC_BATCH, N_HEADS, HEAD_DIM), 0.3),
        'state_m': nrm(ks[5], (DEPTH, DEC_BATCH, N_HEADS), 0.5),
        'meta_tokens': nrm(ks[6], (N_META, D_MODEL), 1.0),
        'norm1': 1.0 + nrm(ks[7], (DEPTH, D_MODEL), 0.05),
        'w_in': nrm(ks[8], (DEPTH, D_MODEL, IN_COLS), D_MODEL ** -0.5),
        'b_gate': jnp.concatenate([b_input, b_forget], axis=-1),
        'w_pool': nrm(ks[11], (DEPTH, POOL_GROUPS, POOL_GC, POOL_GC), POOL_GC ** -0.5),
        'pool_scale': 1.0 + nrm(ks[12], (DEPTH, POOL_WIDTH), 0.1),
        'head_gain': 1.0 + nrm(ks[13], (DEPTH, N_HEADS, HEAD_DIM), 0.05),
        'w_out': nrm(ks[14], (DEPTH, MIX_WIDTH, D_MODEL), MIX_WIDTH ** -0.5),
        'norm2': 1.0 + nrm(ks[15], (DEPTH, D_MODEL), 0.05),
        'w_up': nrm(ks[16], (DEPTH, D_MODEL, D_FF), D_MODEL ** -0.5),
        'w_down': nrm(ks[17], (DEPTH, D_FF, D_MODEL), D_FF ** -0.5),
        'norm_f': 1.0 + nrm(ks[18], (D_MODEL,), 0.05),
    }


def reference(x_prompt, x_sample, state_pool, state_C, state_n, state_m, meta_tokens, norm1, w_in,
              b_gate, w_pool, pool_scale, head_gain, w_out, norm2, w_up, w_down, norm_f):
    dtype = x_prompt.dtype
    B = x_prompt.shape[0]
    xp = jnp.concatenate([jnp.broadcast_to(meta_tokens.astype(dtype)[None], (B, N_META, D_MODEL)),
                          x_prompt], axis=1)
    xs = x_sample
    pos_p = jnp.arange(xp.shape[1], dtype=jnp.int32)
    pos_s = PAST_LEN + jnp.arange(xs.shape[1], dtype=jnp.int32)
    zero_ctx = jnp.zeros((B, POOL_CTX, POOL_WIDTH), dtype)
    pool_p, c_p, n_p, m_p, pool_s, c_s, n_s, m_s = [], [], [], [], [], [], [], []
    for l in range(DEPTH):
        w = (norm1[l], w_in[l], b_gate[l], w_pool[l], pool_scale[l], head_gain[l], w_out[l],
             norm2[l], w_up[l], w_down[l])
        xp, pp, (cp, np_, mp) = block(xp, zero_ctx, pos_p, mlstm_prompt, *w)
        run_s = functools.partial(mlstm_sample, c0=state_C[l], n0=state_n[l], m0=state_m[l])
        xs, ps, (cs, ns, ms) = block(xs, state_pool[l], pos_s, run_s, *w)
        pool_p.append(pp); c_p.append(cp); n_p.append(np_); m_p.append(mp)
        pool_s.append(ps); c_s.append(cs); n_s.append(ns); m_s.append(ms)
    y_prompt = rmsnorm(xp[:, N_META:], norm_f)
    y_sample = rmsnorm(xs, norm_f)
    return (y_prompt, y_sample, jnp.stack(pool_p), jnp.stack(c_p), jnp.stack(n_p), jnp.stack(m_p),
            jnp.stack(pool_s), jnp.stack(c_s), jnp.stack(n_s), jnp.stack(m_s))
```

```python
import numpy as np
from contextlib import ExitStack
import concourse.bass as bass
import concourse.mybir as mybir
from concourse.bass_utils import run_bass_kernel_spmd

F32 = mybir.dt.float32
BF16 = mybir.dt.bfloat16
ALU = mybir.AluOpType
AF = mybir.ActivationFunctionType

NCORES = 8
D = 1024
SEQ = 2048
NT = 17
SP = 16
DFF = 4096
INC = 2568
EPS = 1e-6
WIN = (2, 4, 8, 16)
SCHED = 'list'
JUNK_EVERY = 0
OPLOG = None
PE_RATE = 1400.0
BETA = 0.0


class Tk:
    HOP = 0.02

    def __init__(self, nc, es):
        self.nc = nc
        self.es = es
        self.eng = {'pe': nc.tensor, 'act': nc.scalar, 'dve': nc.vector, 'pool': nc.gpsimd, 'sp': nc.sync}
        self.sems = {}
        self.cnt = {}
        for e in self.eng:
            self.sems['E' + e] = es.enter_context(nc.semaphore('s_' + e))
            self.cnt['E' + e] = 0
        self.lastw = {}
        self.readers = {}
        self.waited = {}
        self.nwaits = 0
        self.attach_waits = True
        self.oplog = None
        self.efree = {e: 0.0 for e in self.eng}
        self.wready = {}
        self.rready = {}

    def estimate(self, eng, R, W):
        t = self.efree.get(eng, 0.0)
        h = self.HOP
        for k in R:
            t = max(t, self.wready.get(k, 0.0) + h)
        for k in W:
            t = max(t, self.wready.get(k, 0.0) + h, self.rready.get(k, 0.0) + h)
        return t

    @staticmethod
    def _fsize(ins):
        try:
            ap = ins.ins.outs[0].ap
            n = 1
            for st_, cnt in ap[1:]:
                n *= cnt
            return n
        except Exception:
            return 128

    def _model(self, eng, R, W, ins, is_dma):
        start = self.estimate(eng, R, W)
        n = self._fsize(ins)
        if is_dma:
            self.efree[eng] = start + 0.1
            end = start + 2.5
        else:
            if eng == 'pe':
                cost = max(0.04, n / PE_RATE)
            elif eng == 'act':
                cost = 0.22 + n / 1200.0
            elif eng == 'dve':
                cost = 0.12 + n / 960.0
            else:
                cost = 0.15 + n / 480.0
            end = start + cost
            self.efree[eng] = end
        for k in R:
            if end > self.rready.get(k, 0.0):
                self.rready[k] = end
        for k in W:
            self.wready[k] = end
        if self.oplog is not None:
            self.oplog.append((start, end, eng, is_dma, tuple(R), tuple(W)))

    def chan(self, name):
        k = 'C' + name
        if k not in self.sems:
            self.sems[k] = self.es.enter_context(self.nc.semaphore('c_' + name))
            self.cnt[k] = 0
        return k

    def _collect(self, R, W):
        deps = []
        for k in R:
            t = self.lastw.get(k)
            if t is not None:
                deps.append(('raw', t))
            if len(k) == 2 and k[0] == 'B' and k[1].isdigit():
                for t in self.readers.get(k, ()):
                    deps.append(('war', t))
        for k in W:
            t = self.lastw.get(k)
            if t is not None:
                deps.append(('waw', t))
            for t in self.readers.get(k, ()):
                deps.append(('war', t))
        return deps

    def _wait(self, eng, deps, is_dma):
        best = {}
        for kind, (sk, val, prod) in deps:
            if not is_dma and prod == eng:
                if eng == 'pe':
                    continue
                if kind == 'war':
                    continue
            if val > best.get(sk, 0):
                best[sk] = val
        need = []
        for sk, val in best.items():
            if self.waited.get((eng, sk), 0) >= val:
                continue
            need.append((sk, val))
            self.waited[(eng, sk)] = val
            self.nwaits += 1
        attach = need.pop() if (need and self.attach_waits) else None
        for sk, val in need:
            self.eng[eng].wait_ge(self.sems[sk], val)
        return attach

    def _record(self, tok, R, W):
        for k in R:
            self.readers.setdefault(k, []).append(tok)
        for k in W:
            self.lastw[k] = tok
            self.readers[k] = []

    def op(self, eng, fn, R=(), W=(), signal=True):
        att = self._wait(eng, self._collect(R, W), False)
        ins = fn()
        if att is not None:
            ins._wait_ge(self.sems[att[0]], att[1])
        sk = 'E' + eng
        if signal:
            self.cnt[sk] += 1
            ins.then_inc(self.sems[sk], 1)
            tok = (sk, self.cnt[sk], eng)
        else:
            tok = (sk, self.cnt[sk] + 1, eng)
        self._record(tok, R, W)
        self._model(eng, R, W, ins, False)
        return ins

    def dma(self, queue, fn, R, W, chan):
        ck = self.chan(chan)
        att = self._wait(queue, self._collect(R, W), True)
        if att is not None:
            self.eng[queue].wait_ge(self.sems[att[0]], att[1])
        ins = fn()
        self.cnt[ck] += 16
        ins.then_inc(self.sems[ck], 16)
        tok = (ck, self.cnt[ck], 'dma')
        self._record(tok, R, W)
        self._model(queue, R, W, ins, True)
        return ins

    def seal(self, chan, keys):
        ck = self.chan(chan)
        for k in keys:
            self.lastw[k] = (ck, self.cnt[ck], 'dma')

    def barrier(self):
        for e in self.eng:
            for sk, c in self.cnt.items():
                if c > 0 and self.waited.get((e, sk), 0) < c:
                    self.eng[e].wait_ge(self.sems[sk], c)
                    self.waited[(e, sk)] = c
        self.lastw.clear()
        self.readers.clear()

    def finish(self):
        for sk, c in self.cnt.items():
            if c > 0 and self.waited.get(('sp', sk), 0) < c:
                self.nc.sync.wait_ge(self.sems[sk], c)
                self.waited[('sp', sk)] = c


class _Stop(Exception):
    pass


def build_nc(upto=None):
    nc = bass.Bass("TRN2", target_bir_lowering=False)

    def din(name, shape):
        return nc.dram_tensor(name, list(shape), F32, kind="ExternalInput").ap()

    def dout(name, shape):
        return nc.dram_tensor(name, list(shape), F32, kind="ExternalOutput").ap()

    xp = din("xp", [SEQ, D]); xs = din("xs", [64, D]); meta = din("meta", [16, D])
    spool = din("spool", [16, 15, 512]); sC = din("sC", [16, 4, 128, 128])
    sn = din("sn", [64, 128]); sm_ = din("sm", [16, 4])
    norm1 = din("norm1", [D]); w_in = din("w_in", [D, INC]); b_gate = din("b_gate", [8])
    w_pool = din("w_pool", [4, 128, 128]); pool_scale = din("pool_scale", [512])
    head_gain = din("head_gain", [512]); w_out = din("w_out", [D, D]); norm2 = din("norm2", [D])
    w_up = din("w_up", [D, DFF]); w_down = din("w_down", [DFF, D]); norm_f = din("norm_f", [D])

    yp = dout("yp", [SEQ, D]); ys = dout("ys", [64, D])
    pool_p = dout("pool_p", [15, 512]); C_p = dout("C_p", [4, 128, 128]); n_p = dout("n_p", [4, 128])
    m_p = dout("m_p", [1, 4])
    pool_s = dout("pool_s", [16, 15, 512]); C_s = dout("C_s", [16, 4, 128, 128])
    n_s = dout("n_s", [16, 4, 128]); m_s = dout("m_s", [16, 4])
    x2s = nc.dram_tensor("x2s", [NT * 128, D], F32, kind="Internal").ap()
    dbg_out = {}

    with ExitStack() as es:
        E = es.enter_context
        tk = Tk(nc, es)
        if OPLOG is not None:
            tk.oplog = OPLOG

        def sb(es_, name, shape, dt=F32):
            return es_.enter_context(nc.sbuf_tensor(name, list(shape), dt))

        def A(fn, R=(), W=()): return tk.op('act', fn, R, W)
        def V(fn, R=(), W=()): return tk.op('dve', fn, R, W)
        def G(fn, R=(), W=()): return tk.op('pool', fn, R, W)
        def P(fn, R=(), W=(), sig=True): return tk.op('pe', fn, R, W, signal=sig)

        B = [E(nc.psum_tensor("B%d" % i, [128, 512], F32)) for i in range(8)]
        B0b = B[0][:].bitcast(BF16)
        B1b = B[1][:].bitcast(BF16)

        ident_bf = sb(es, "ident_bf", [128, 128], BF16)
        gcol2 = sb(es, "gcol2", [128, 8])
        gfB = sb(es, "gfB", [128, D])
        ARENA_W = 12288
        arena = sb(es, "arena", [128, ARENA_W])
        ms = ExitStack()
        ident_f = sb(ms, "ident_f", [128, 128])
        sel = sb(ms, "sel", [4, 4, 128])
        maskC = sb(ms, "maskC", [128, 128])
        maskS = sb(ms, "maskS", [128, 128])
        maskcols = sb(ms, "maskcols", [128, 17])
        selmask = sb(ms, "selmask", [128, 16, 64], BF16)
        E1 = sb(ms, "E1", [16, 128])
        E2 = sb(ms, "E2", [1, 128])
        zeros4 = sb(ms, "zeros4", [4, 128])
        ones_bf = sb(ms, "ones_bf", [128, 1], BF16)
        gcol1 = sb(ms, "gcol1", [128, 8])
        hgB = sb(ms, "hgB", [128, 512])
        pscol = sb(ms, "pscol", [128, 4])
        bi_col = sb(ms, "bi_col", [4, 1]); bf_col = sb(ms, "bf_col", [4, 1])
        rc_meta = sb(ms, "rc_meta", [128, 4, 16])

        NX = 3
        xt = [sb(ms, "xt%d" % i, [128, D]) for i in range(NX)]
        tk.op('pool', lambda: nc.gpsimd.memset(xt[0][:], 0.0), (), ['xt0'])
        tk.dma('sp', lambda: nc.sync.dma_start(out=xt[0][0:64, :], in_=xs[:, :]), [], ['xt0'], 'x0a')
        tk.dma('sp', lambda: nc.sync.dma_start(out=xt[0][64:80, :], in_=meta[:, :]), ['xt0'], ['xt0'], 'x0')
        def mk_ident(t, key):
            G(lambda: nc.gpsimd.memset(t[:], 1.0), W=[key])
            G(lambda: nc.gpsimd.affine_select(out=t[:], in_=t[:], pattern=[[-1, 128]], compare_op=ALU.is_equal,
                                              fill=0.0, base=0, channel_multiplier=1), R=[key], W=[key])
        mk_ident(ident_bf, 'ident_bf'); mk_ident(ident_f, 'ident_f')
        G(lambda: nc.gpsimd.memset(sel[:], 1.0), W=['sel'])
        G(lambda: nc.gpsimd.affine_select(out=sel[:], in_=sel[:], pattern=[[-1, 4], [0, 128]], compare_op=ALU.is_equal,
                                          fill=0.0, base=0, channel_multiplier=1), R=['sel'], W=['sel'])
        G(lambda: nc.gpsimd.memset(maskC[:], 1.0), W=['maskC'])
        G(lambda: nc.gpsimd.affine_select(out=maskC[:], in_=maskC[:], pattern=[[1, 128]], compare_op=ALU.is_ge,
                                          fill=0.0, base=0, channel_multiplier=-1), R=['maskC'], W=['maskC'])
        G(lambda: nc.gpsimd.memset(selmask[:], 1.0), W=['selmask'])
        G(lambda: nc.gpsimd.affine_select(out=selmask[:], in_=selmask[:], pattern=[[-4, 16], [1, 64]], compare_op=ALU.is_ge,
                                          fill=0.0, base=0, channel_multiplier=0), R=['selmask'], W=['selmask'])
        G(lambda: nc.gpsimd.affine_select(out=selmask[:], in_=selmask[:], pattern=[[4, 16], [-1, 64]], compare_op=ALU.is_ge,
                                          fill=0.0, base=3, channel_multiplier=0), R=['selmask'], W=['selmask'])
        G(lambda: nc.gpsimd.memset(E1[:], 1.0), W=['E1'])
        G(lambda: nc.gpsimd.affine_select(out=E1[:], in_=E1[:], pattern=[[1, 128]], compare_op=ALU.is_ge,
                                          fill=0.0, base=0, channel_multiplier=-4), R=['E1'], W=['E1'])
        G(lambda: nc.gpsimd.affine_select(out=E1[:], in_=E1[:], pattern=[[-1, 128]], compare_op=ALU.is_ge,
                                          fill=0.0, base=3, channel_multiplier=4), R=['E1'], W=['E1'])
        G(lambda: nc.gpsimd.memset(E2[:], 0.0), W=['E2'])
        G(lambda: nc.gpsimd.memset(E2[:, 64:80], 1.0), R=['E2'], W=['E2'])
        G(lambda: nc.gpsimd.memset(zeros4[:], 0.0), W=['zeros4'])
        G(lambda: nc.gpsimd.memset(ones_bf[:], 1.0), W=['ones_bf'])
        for g, w in enumerate(WIN):
            G(lambda: nc.gpsimd.memset(rc_meta[:, g, :], 1.0 / w), R=['rc_meta'], W=['rc_meta'])
            for pos in range(w - 1):
                G(lambda: nc.gpsimd.memset(rc_meta[:, g, pos:pos + 1], 1.0 / (pos + 1)), R=['rc_meta'], W=['rc_meta'])
        P(lambda: nc.tensor.matmul(B[3][:, 0:128], lhsT=E1[:, :], rhs=E1[:, :], start=True, stop=False),
          R=['E1'], W=['B3'], sig=False)
        P(lambda: nc.tensor.matmul(B[3][:, 0:128], lhsT=E2[:, :], rhs=E2[:, :], start=False, stop=True),
          R=['E2'], W=['B3'])
        V(lambda: nc.vector.tensor_tensor(out=maskS[:], in0=B[3][:, 0:128], in1=maskC[:], op=ALU.mult),
          R=['B3', 'maskC'], W=['maskS'])
        P(lambda: nc.tensor.transpose(out=B[3][:, 128:144], in_=E1[:, :], identity=ident_f[0:16, 0:16]),
          R=['E1', 'ident_f'], W=['B3'])
        P(lambda: nc.tensor.transpose(out=B[3][:, 144:145], in_=E2[:, :], identity=ident_f[0:1, 0:1]),
          R=['E2', 'ident_f'], W=['B3'])
        V(lambda: nc.vector.tensor_copy(out=maskcols[:], in_=B[3][:, 128:145]), R=['B3'], W=['maskcols'])

        w_in_sb = sb(ms, "w_in_sb", [128, 8, INC], BF16)
        w_in_v = w_in.rearrange("(k p) c -> p k c", p=128)
        stg = [arena[:, j * INC:(j + 1) * INC] for j in range(4)]
        for k in range(8):
            j = k % 4
            eng_q = 'sp' if k % 2 == 0 else 'act'
            qobj = nc.sync if k % 2 == 0 else nc.scalar
            tk.dma(eng_q, lambda: qobj.dma_start(out=stg[j], in_=w_in_v[:, k, :]), [], ['stg%d' % j], 'stg%d' % j)
            V(lambda: nc.vector.tensor_copy(out=w_in_sb[:, k, :], in_=stg[j]), R=['stg%d' % j], W=['w_in'])
        cl = []
        def cload(t, src, key):
            tk.dma('sp', lambda: nc.sync.dma_start(out=t, in_=src), [], [key], 'const')
            cl.append(key)
        with nc.allow_non_contiguous_dma(reason="small constant loads"):
            cload(gcol1[:], norm1.rearrange("(k p) -> p k", p=128), 'gcol1')
            cload(gcol2[:], norm2.rearrange("(k p) -> p k", p=128), 'gcol2')
            cload(pscol[:], pool_scale.rearrange("(g p) -> p g", p=128), 'pscol')
            cload(bi_col[:], b_gate[0:4].rearrange("(h o) -> h o", o=1), 'bi_col')
            cload(bf_col[:], b_gate[4:8].rearrange("(h o) -> h o", o=1), 'bf_col')
        cload(gfB[:], norm_f.partition_broadcast(128), 'gfB')
        cload(hgB[:], head_gain.partition_broadcast(128), 'hgB')
        tk.seal('const', cl)

        stop = [False]

        def ckpt(name):
            if upto == name:
                stop[0] = True
            return stop[0]

        def x2key(i): return 'x2s%d' % i

        try:
            ckpt('setup')
            with ms:
                w_out_sb = sb(ms, "w_out_sb", [128, 8, D], BF16)
                w_pool_sb = sb(ms, "w_pool_sb", [128, 4, 128], BF16)
                tk.dma('pool', lambda: nc.gpsimd.dma_start(out=w_pool_sb[:], in_=w_pool.rearrange("g c d -> c g d")),
                       [], ['w_pool'], 'w_pool')
                w_out_v = w_out.rearrange("(k p) c -> p k c", p=128)
                for k2 in range(2):
                    tk.dma('pool', lambda: nc.gpsimd.dma_start(out=w_out_sb[:, 4 * k2:4 * k2 + 4, :], in_=w_out_v[:, 4 * k2:4 * k2 + 4, :]),
                           [], ['w_out'], 'w_out')
                tk.seal('w_out', ['w_out'])

                xn = [sb(ms, "xn%d" % i, [128, D], BF16) for i in range(1)]
                xr = [sb(ms, "xr%d" % i, [128, D]) for i in range(2)]
                numS = [sb(ms, "numS%d" % i, [128, 512]) for i in range(2)]
                st = [sb(ms, "st%d" % i, [128, 8]) for i in range(2)]
                xnT = [sb(ms, "xnT%d" % i, [128, 8, 128], BF16) for i in range(2)]
                qT = [sb(ms, "qT%d" % i, [128, 4, 128], BF16) for i in range(2)]
                kT = [sb(ms, "kT%d" % i, [128, 4, 128], BF16) for i in range(2)]
                vt = [sb(ms, "vt%d" % i, [128, 512], BF16) for i in range(2)]
                og = [sb(ms, "og%d" % i, [128, 512]) for i in range(3)]
                utile = [sb(ms, "utile%d" % i, [128, 4, 143]) for i in range(2)]
                Wa = sb(ms, "Wa", [128, 4, 143]); Wb = sb(ms, "Wb", [128, 4, 143])
                zt = [sb(ms, "zt%d" % i, [128, 4, 128], BF16) for i in range(4)]
                mixT = [sb(ms, "mixT%d" % i, [128, 8, 128], BF16) for i in range(2)]
                NR = 9
                rows = [sb(ms, "rows%d" % i, [4, NR, 128]) for i in range(2)]
                cols = [sb(ms, "cols%d" % i, [128, 24]) for i in range(2)]
                Dt = sb(ms, "Dt", [128, 4, 128]); Dm = sb(ms, "Dm", [128, 4, 128])
                interB = sb(ms, "interB", [128, 4, 128])
                ut = Dm[:].rearrange("p h t -> p (h t)")
                spt = [Dt[:].rearrange("p h t -> p (h t)"), interB[:].rearrange("p h t -> p (h t)")]
                Pt = sb(ms, "Pt", [128, 4, 128], BF16)
                qs = sb(ms, "qs", [128, 4, 128], BF16)
                kw = sb(ms, "kw", [128, 4, 128], BF16)
                mixh = sb(ms, "mixh", [128, 512], BF16)
                hj = sb(ms, "hj", [128, 128], BF16)
                Cf = sb(ms, "Cf", [128, 4, 128]); nf = sb(ms, "nf", [128, 4])
                Cb = [sb(ms, "Cb%d" % i, [128, 4, 128], BF16) for i in range(2)]
                nb = [sb(ms, "nb%d" % i, [128, 4], BF16) for i in range(2)]
                usamp = arena[:, 8192:9408].rearrange("p (a b) -> p a b", b=19)
                Was = arena[:, 9408:10624].rearrange("p (a b) -> p a b", b=19)
                Wbs = arena[:, 10624:11840].rearrange("p (a b) -> p a b", b=19)
                umeta = sb(ms, "umeta", [128, 4, 31])
                C0b = arena[:, 0:4096].bitcast(BF16).rearrange("p (a b) -> p a b", b=128)
                C0f = [arena[:, 4096:5120].rearrange("p (a b) -> p a b", b=128), arena[:, 5120:6144].rearrange("p (a b) -> p a b", b=128),
                       arena[:, 8192:9216].rearrange("p (a b) -> p a b", b=128), arena[:, 9216:10240].rearrange("p (a b) -> p a b", b=128)]
                qz = arena[:, 6144:8192].bitcast(BF16).rearrange("p (a b c) -> p a b c", b=4, c=64)
                ktm = sb(ms, "ktm", [128, 4, 128], BF16)
                kwz = [arena[:, 11840:12096].bitcast(BF16).rearrange("p (a b) -> p a b", b=128), sb(ms, "kwz1", [128, 4, 128], BF16)]
                wsm = sb(ms, "wsm", [128, 17, 4])
                m0r = sb(ms, "m0r", [4, 16]); decr = sb(ms, "decr", [4, 16]); msr = sb(ms, "msr", [4, 16])
                decS = sb(ms, "decS", [128, 4, 16])
                n0t = sb(ms, "n0t", [64, 128]); n0f = sb(ms, "n0f", [128, 64]); n0b = sb(ms, "n0b", [128, 64], BF16)
                n1s = sb(ms, "n1s", [128, 4, 16]); n1t = sb(ms, "n1t", [64, 128])
                npt = sb(ms, "npt", [4, 128]); mpr = sb(ms, "mpr", [4, 1])

                G(lambda: nc.gpsimd.memset(Cf[:], 0.0), W=['Cf'])
                G(lambda: nc.gpsimd.memset(nf[:], 0.0), W=['nf'])

                order = [SP] + list(range(16))

                def load_into(buf, key, i, chan):
                    if i == SP:
                        G(lambda: nc.gpsimd.memset(buf[:], 0.0), W=[key])
                        tk.dma('sp', lambda: nc.sync.dma_start(out=buf[0:64, :], in_=xs[:, :]), [], [key], chan + 'a')
                        tk.dma('sp', lambda: nc.sync.dma_start(out=buf[64:80, :], in_=meta[:, :]), [key], [key], chan)
                    else:
                        tk.dma('sp', lambda: nc.sync.dma_start(out=buf[:], in_=xp[128 * i:128 * i + 128, :]), [], [key], chan)

                def load_x(pos_):
                    s = pos_ % NX
                    load_into(xt[s], 'xt%d' % s, order[pos_], 'x%d' % s)

                def rstd_chain(stt, ss_col, out_col, n, keyR, keyW):
                    yield ('act', keyR, keyW)
                    A(lambda: nc.scalar.activation(out=stt[:, out_col:out_col + 1], in_=stt[:, ss_col:ss_col + 1], func=AF.Ln,
                                                   scale=1.0 / n, bias=EPS), R=keyR, W=keyW)
                    yield ('act', keyW, keyW)
                    A(lambda: nc.scalar.activation(out=stt[:, out_col:out_col + 1], in_=stt[:, out_col:out_col + 1], func=AF.Exp,
                                                   scale=-0.5), R=keyW, W=keyW)

                b0_live = [False]
                njunk = [0]

                def junk_mm():
                    if JUNK_EVERY <= 0 or b0_live[0]:
                        return
                    njunk[0] += 1
                    if njunk[0] % JUNK_EVERY:
                        return
                    P(lambda: nc.tensor.matmul(B[0][:, :], lhsT=ident_bf[:], rhs=w_in_sb[:, 0, 0:512], start=True, stop=True),
                      R=['ident_bf', 'w_in'], W=['B0'], sig=False)

                def norm_T(xtile, xkey, slot, gcol, gkey):
                    sk = 'st%d' % slot
                    yield ('act', [xkey], ['xn0', sk])
                    A(lambda: nc.scalar.activation(out=xn[0][:], in_=xtile[:], func=AF.Square, accum_out=st[slot][:, 0:1]),
                      R=[xkey], W=['xn0', sk])
                    yield from rstd_chain(st[slot], 0, 1, D, [sk], [sk])
                    yield ('act', [xkey, sk], ['xn0'])
                    A(lambda: nc.scalar.activation(out=xn[0][:], in_=xtile[:], func=AF.Copy, scale=st[slot][:, 1:2]),
                      R=[xkey, sk], W=['xn0'])
                    b0_live[0] = True
                    for k in range(8):
                        P(lambda: nc.tensor.transpose(out=B0b[:, k * 128:(k + 1) * 128], in_=xn[0][:, k * 128:(k + 1) * 128],
                                                      identity=ident_bf[:]),
                          R=['xn0', 'ident_bf'], W=['B0'], sig=(k == 7))
                    yield ('dve', ['B0', gkey], ['xnT%d' % slot])
                    V(lambda: nc.vector.tensor_tensor(out=xnT[slot][:], in0=B0b.rearrange("p (k t) -> p k t", k=8),
                                                      in1=gcol[:].unsqueeze(2).to_broadcast([128, 8, 128]), op=ALU.mult),
                      R=['B0', gkey], W=['xnT%d' % slot])
                    b0_live[0] = False

                def pool_sums(U, Wa_, Wb_, rpg, L, keyU, keyWa, keyWb):
                    yield ('pool', [keyU], [keyWa])
                    G(lambda: nc.gpsimd.tensor_tensor(out=Wa_[:, :, 1:L], in0=U[:, :, 1:L], in1=U[:, :, 0:L - 1], op=ALU.add),
                      R=[keyU], W=[keyWa])
                    yield ('pool', [keyWa], [keyWb])
                    G(lambda: nc.gpsimd.tensor_tensor(out=Wb_[:, rpg:4 * rpg, 3:L], in0=Wa_[:, rpg:4 * rpg, 3:L],
                                                      in1=Wa_[:, rpg:4 * rpg, 1:L - 2], op=ALU.add), R=[keyWa], W=[keyWb])
                    yield ('pool', [keyWb], [keyWa])
                    G(lambda: nc.gpsimd.tensor_tensor(out=Wa_[:, 2 * rpg:4 * rpg, 7:L], in0=Wb_[:, 2 * rpg:4 * rpg, 7:L],
                                                      in1=Wb_[:, 2 * rpg:4 * rpg, 3:L - 4], op=ALU.add), R=[keyWb], W=[keyWa])
                    yield ('pool', [keyWa], [keyWb])
                    G(lambda: nc.gpsimd.tensor_tensor(out=Wb_[:, 3 * rpg:4 * rpg, 15:L], in0=Wa_[:, 3 * rpg:4 * rpg, 15:L],
                                                      in1=Wa_[:, 3 * rpg:4 * rpg, 7:L - 8], op=ALU.add), R=[keyWa], W=[keyWb])
                    return [Wa_, Wb_, Wa_, Wb_]

                def mixer_tile(i, pos):
                    spc = (i == SP)
                    s2 = pos % 2
                    xs_ = pos % NX
                    xkey = 'xt%d' % xs_
                    rk = 'rows%d' % s2
                    ck = 'cols%d' % s2
                    rw = rows[s2]
                    prev_rw = rows[1 - s2]
                    R_XF, R_T, R_LF, R_F, R_A, R_M, R_AREL, R_MREL, R_WS = range(9)
                    R_MNEG, R_EMN, R_MEND = R_XF, R_T, R_LF

                    yield from norm_T(xt[xs_], xkey, s2, gcol1, 'gcol1')
                    yield 'SPLIT'
                    xk = 'xnT%d' % s2
                    X = xnT[s2]

                    if ckpt('t%da' % pos):
                        return
                    yield None
                    for part, c0 in ((0, 2560), (1, 2564)):
                        for k in range(8):
                            P(lambda: nc.tensor.matmul(B[3][0:4, part * 128:(part + 1) * 128], lhsT=w_in_sb[:, k, c0:c0 + 4],
                                                       rhs=X[:, k, :], start=(k == 0), stop=(k == 7)),
                              R=[xk, 'w_in'], W=['B3'], sig=(k == 7))

                    if ckpt('t%db' % pos):
                        return
                    yield None
                    if spc:
                        yield ('pool', (), [rk])
                        G(lambda: nc.gpsimd.memset(rw[:], 0.0), W=[rk])
                    yield ('dve', ['B3', 'bf_col'], [rk])
                    V(lambda: nc.vector.tensor_scalar(out=rw[:, R_XF, :], in0=B[3][0:4, 128:256], scalar1=bf_col[:, 0:1],
                                                      scalar2=None, op0=ALU.add), R=['B3', 'bf_col'], W=[rk])
                    yield ('dve', [rk], [rk])
                    V(lambda: nc.vector.scalar_tensor_tensor(out=rw[:, R_T, :], in0=rw[:, R_XF, :], scalar=-1.0, in1=rw[:, R_XF, :],
                                                             op0=ALU.mult, op1=ALU.max), R=[rk], W=[rk])
                    yield ('act', [rk], [rk])
                    A(lambda: nc.scalar.activation(out=rw[:, R_T, :], in_=rw[:, R_T, :], func=AF.Exp, scale=-1.0), R=[rk], W=[rk])
                    yield ('act', [rk], [rk])
                    A(lambda: nc.scalar.activation(out=rw[:, R_T, :], in_=rw[:, R_T, :], func=AF.Ln, bias=1.0), R=[rk], W=[rk])
                    yield ('dve', [rk], [rk])
                    V(lambda: nc.vector.scalar_tensor_tensor(out=rw[:, R_LF, :], in0=rw[:, R_XF, :], scalar=0.0, in1=rw[:, R_T, :],
                                                             op0=ALU.min, op1=ALU.subtract), R=[rk], W=[rk])
                    if not spc:
                        if i == 0:
                            Fc = prev_rw[:, R_F, 79:80]; Mc = prev_rw[:, R_M, 79:80]
                        else:
                            Fc = prev_rw[:, R_F, 127:128]; Mc = prev_rw[:, R_M, 127:128]
                        pk = 'rows%d' % (1 - s2)
                        yield ('dve', [rk, pk, 'zeros4'], [rk])
                        V(lambda: nc.vector.tensor_tensor_scan(out=rw[:, R_F, :], data0=rw[:, R_LF, :], data1=zeros4[:, :],
                                                               initial=Fc, op0=ALU.add, op1=ALU.add), R=[rk, pk, 'zeros4'], W=[rk])
                        yield ('dve', ['B3', 'bi_col', rk], [rk])
                        V(lambda: nc.vector.scalar_tensor_tensor(out=rw[:, R_A, :], in0=B[3][0:4, 0:128], scalar=bi_col[:, 0:1],
                                                                 in1=rw[:, R_F, :], op0=ALU.add, op1=ALU.subtract),
                          R=['B3', 'bi_col', rk], W=[rk])
                        yield ('dve', [rk, pk], [rk])
                        V(lambda: nc.vector.tensor_tensor_scan(out=rw[:, R_M, :], data0=rw[:, R_A, :], data1=rw[:, R_A, :],
                                                               initial=Mc, op0=ALU.max, op1=ALU.max), R=[rk, pk], W=[rk])
                        yield ('dve', [rk, pk], [rk])
                        V(lambda: nc.vector.tensor_scalar(out=rw[:, R_AREL, :], in0=rw[:, R_A, :], scalar1=Mc, scalar2=None,
                                                          op0=ALU.subtract), R=[rk, pk], W=[rk])
                        yield ('dve', [rk, pk], [rk])
                        V(lambda: nc.vector.tensor_scalar(out=rw[:, R_MREL, :], in0=rw[:, R_M, :], scalar1=Mc, scalar2=None,
                                                          op0=ALU.subtract), R=[rk, pk], W=[rk])
                        yield ('dve', [rk], [rk])
                        V(lambda: nc.vector.tensor_scalar(out=rw[:, R_WS, :], in0=rw[:, R_AREL, :], scalar1=rw[:, R_MREL, 127:128],
                                                          scalar2=None, op0=ALU.subtract), R=[rk], W=[rk])
                    else:
                        yield ('dve', [rk, 'zeros4'], [rk])
                        V(lambda: nc.vector.tensor_tensor_scan(out=rw[:, R_F, 64:80], data0=rw[:, R_LF, 64:80], data1=zeros4[:, 0:16],
                                                               initial=0.0, op0=ALU.add, op1=ALU.add), R=[rk, 'zeros4'], W=[rk])
                        lf3 = rw[:, R_LF, 0:64].rearrange("h (j t) -> h j t", t=4)
                        F3 = rw[:, R_F, 0:64].rearrange("h (j t) -> h j t", t=4)
                        yield ('dve', [rk], [rk])
                        V(lambda: nc.vector.tensor_copy(out=F3[:, :, 0], in_=lf3[:, :, 0]), R=[rk], W=[rk])
                        for t in range(1, 4):
                            yield ('dve', [rk], [rk])
                            V(lambda: nc.vector.tensor_tensor(out=F3[:, :, t], in0=F3[:, :, t - 1], in1=lf3[:, :, t], op=ALU.add),
                              R=[rk], W=[rk])
                        yield ('dve', ['B3', 'bi_col', rk], [rk])
                        V(lambda: nc.vector.scalar_tensor_tensor(out=rw[:, R_A, 0:80], in0=B[3][0:4, 0:80], scalar=bi_col[:, 0:1],
                                                                 in1=rw[:, R_F, 0:80], op0=ALU.add, op1=ALU.subtract),
                          R=['B3', 'bi_col', rk], W=[rk])
                        yield ('dve', [rk], [rk])
                        V(lambda: nc.vector.tensor_tensor_scan(out=rw[:, R_M, 64:80], data0=rw[:, R_A, 64:80], data1=rw[:, R_A, 64:80],
                                                               initial=0.0, op0=ALU.max, op1=ALU.max), R=[rk], W=[rk])
                        A3 = rw[:, R_A, 0:64].rearrange("h (j t) -> h j t", t=4)
                        M3 = rw[:, R_M, 0:64].rearrange("h (j t) -> h j t", t=4)
                        yield ('dve', [rk, 'm0r'], [rk])
                        V(lambda: nc.vector.tensor_tensor(out=M3[:, :, 0], in0=m0r[:, :], in1=A3[:, :, 0], op=ALU.max),
                          R=[rk, 'm0r'], W=[rk])
                        for t in range(1, 4):
                            yield ('dve', [rk], [rk])
                            V(lambda: nc.vector.tensor_tensor(out=M3[:, :, t], in0=M3[:, :, t - 1], in1=A3[:, :, t], op=ALU.max),
                              R=[rk], W=[rk])
                        m0b = m0r[:, :].unsqueeze(2).to_broadcast([4, 16, 4])
                        AR3 = rw[:, R_AREL, 0:64].rearrange("h (j t) -> h j t", t=4)
                        MR3 = rw[:, R_MREL, 0:64].rearrange("h (j t) -> h j t", t=4)
                        ME3 = rw[:, R_MEND, 0:64].rearrange("h (j t) -> h j t", t=4)
                        yield ('dve', [rk, 'm0r'], [rk])
                        V(lambda: nc.vector.tensor_tensor(out=AR3, in0=A3, in1=m0b, op=ALU.subtract), R=[rk, 'm0r'], W=[rk])
                        yield ('dve', [rk, 'm0r'], [rk])
                        V(lambda: nc.vector.tensor_tensor(out=MR3, in0=M3, in1=m0b, op=ALU.subtract), R=[rk, 'm0r'], W=[rk])
                        yield ('dve', [rk], [rk])
                        V(lambda: nc.vector.tensor_copy(out=rw[:, R_AREL, 64:80], in_=rw[:, R_A, 64:80]), R=[rk], W=[rk])
                        yield ('dve', [rk], [rk])
                        V(lambda: nc.vector.tensor_copy(out=rw[:, R_MREL, 64:80], in_=rw[:, R_M, 64:80]), R=[rk], W=[rk])
                        yield ('dve', [rk], [rk])
                        V(lambda: nc.vector.tensor_copy(out=ME3, in_=MR3[:, :, 3:4].to_broadcast([4, 16, 4])), R=[rk], W=[rk])
                        yield ('dve', [rk], [rk])
                        V(lambda: nc.vector.tensor_copy(out=rw[:, R_MEND, 64:80], in_=rw[:, R_MREL, 79:80].to_broadcast([4, 16])),
                          R=[rk], W=[rk])
                        yield ('dve', [rk], [rk])
                        V(lambda: nc.vector.tensor_tensor(out=rw[:, R_WS, :], in0=rw[:, R_AREL, :], in1=rw[:, R_MEND, :],
                                                          op=ALU.subtract), R=[rk], W=[rk])
                        yield ('act', [rk], ['decr'])
                        A(lambda: nc.scalar.activation(out=decr[:, :], in_=MR3[:, :, 3], func=AF.Exp, scale=-1.0),
                          R=[rk], W=['decr'])
                    yield ('dve', [rk], [rk])
                    V(lambda: nc.vector.scalar_tensor_tensor(out=rw[:, R_MNEG, :], in0=rw[:, R_F, :], scalar=-1.0, in1=rw[:, R_M, :],
                                                             op0=ALU.mult, op1=ALU.subtract), R=[rk], W=[rk])
                    yield ('act', [rk], [rk])
                    A(lambda: nc.scalar.activation(out=rw[:, R_EMN, :], in_=rw[:, R_MNEG, :], func=AF.Exp), R=[rk], W=[rk])
                    yield ('act', [rk], [rk])
                    A(lambda: nc.scalar.activation(out=rw[:, R_WS, :], in_=rw[:, R_WS, :], func=AF.Exp), R=[rk], W=[rk])
                    if spc:
                        MN3 = rw[:, R_MNEG, 0:64].rearrange("h (j t) -> h j t", t=4)
                        yield ('dve', [rk], ['msr'])
                        V(lambda: nc.vector.tensor_scalar(out=msr[:, :], in0=MN3[:, :, 3], scalar1=-1.0, scalar2=None, op0=ALU.mult),
                          R=[rk], W=['msr'])
                        with nc.allow_non_contiguous_dma(reason="tiny"):
                            yield ('sp', ['msr'], ['m_s'])
                            tk.dma('sp', lambda: nc.sync.dma_start(out=m_s.rearrange("j h -> h j"), in_=msr[:, :]), ['msr'], ['m_s'], 'm_s')
                    if i == 15:
                        yield ('dve', [rk], ['mpr'])
                        V(lambda: nc.vector.tensor_scalar(out=mpr[:, :], in0=rw[:, R_MNEG, 127:128], scalar1=-1.0, scalar2=None,
                                                          op0=ALU.mult), R=[rk], W=['mpr'])
                        with nc.allow_non_contiguous_dma(reason="tiny"):
                            yield ('sp', ['mpr'], ['m_p'])
                            tk.dma('sp', lambda: nc.sync.dma_start(out=m_p.rearrange("o h -> h o"), in_=mpr[:, :]), ['mpr'], ['m_p'], 'm_p')
                    if ckpt('t%dc' % pos):
                        return
                    yield None
                    def fm_round(c0):
                        for c in range(4):
                            for k in range(8):
                                P(lambda: nc.tensor.matmul(B[2][:, c * 128:(c + 1) * 128], lhsT=w_in_sb[:, k, c0 + c * 128:c0 + (c + 1) * 128],
                                                           rhs=X[:, k, :], start=(k == 0), stop=(k == 7)),
                                  R=[xk, 'w_in'], W=['B2'], sig=(c == 3 and k == 7))
                        yield None
                    B2v = B[2][:].rearrange("p (g t) -> p g t", g=4)
                    uk = 'utile%d' % s2
                    so = pos % 3
                    ok_ = 'og%d' % so

                    def tm_round(c0, bank):
                        for k in range(8):
                            P(lambda: nc.tensor.matmul(B[bank][:, :], lhsT=X[:, k, :], rhs=w_in_sb[:, k, c0:c0 + 512],
                                                       start=(k == 0), stop=(k == 7)), R=[xk, 'w_in'], W=['B%d' % bank], sig=(k == 7))

                    def evac_u():
                        if not spc:
                            yield ('act', ['B2'], [uk])
                            A(lambda: nc.scalar.activation(out=utile[s2][:, :, 15:143], in_=B2v, func=AF.Copy), R=['B2'], W=[uk])
                        else:
                            for g in range(4):
                                yield ('act', ['B2'], ['usamp'])
                                A(lambda: nc.scalar.activation(out=usamp[:, 16 * g:16 * g + 16, 15:19],
                                                               in_=B[2][:, g * 128:g * 128 + 64].rearrange("p (j t) -> p j t", t=4),
                                                               func=AF.Copy), R=['B2'], W=['usamp'])
                            yield ('pool', (), ['umeta'])
                            G(lambda: nc.gpsimd.memset(umeta[:, :, 0:15], 0.0), W=['umeta'])
                            yield ('act', ['B2'], ['umeta'])
                            A(lambda: nc.scalar.activation(out=umeta[:, :, 15:31], in_=B2v[:, :, 64:80], func=AF.Copy), R=['B2'], W=['umeta'])

                    yield from fm_round(0)
                    tm_round(1536, 4)
                    yield from evac_u()
                    yield from fm_round(512)
                    yield ('act', ['B4'], ['vt%d' % s2])
                    A(lambda: nc.scalar.activation(out=vt[s2][:], in_=B[4][:, :], func=AF.Copy), R=['B4'], W=['vt%d' % s2])
                    tm_round(2048, 5)
                    yield ('act', ['B2'], ['qT%d' % s2])
                    A(lambda: nc.scalar.activation(out=qT[s2][:], in_=B2v, func=AF.Copy, scale=float(128 ** -0.5)),
                      R=['B2'], W=['qT%d' % s2])
                    yield from fm_round(1024)
                    yield ('act', ['B5'], [ok_])
                    A(lambda: nc.scalar.activation(out=og[so][:], in_=B[5][:, :], func=AF.Exp, scale=-1.0), R=['B5'], W=[ok_])
                    yield ('act', [ok_], [ok_])
                    A(lambda: nc.scalar.activation(out=og[so][:], in_=og[so][:], func=AF.Ln, bias=1.0), R=[ok_], W=[ok_])
                    yield ('dve', ['B2'], ['kT%d' % s2])
                    V(lambda: nc.vector.tensor_copy(out=kT[s2][:], in_=B2v), R=['B2'], W=['kT%d' % s2])
                    yield ('act', [ok_], [ok_])
                    A(lambda: nc.scalar.activation(out=og[so][:], in_=og[so][:], func=AF.Exp, scale=-1.0), R=[ok_], W=[ok_])
                    yield ('pool', [ok_, 'hgB'], [ok_])
                    G(lambda: nc.gpsimd.tensor_tensor(out=og[so][:], in0=og[so][:], in1=hgB[:], op=ALU.mult), R=[ok_, 'hgB'], W=[ok_])
                    if i == 15 or spc:
                        for k in range(8):
                            P(lambda: nc.tensor.matmul(B[4][:, :], lhsT=X[:, k, :], rhs=w_in_sb[:, k, 0:512],
                                                       start=(k == 0), stop=(k == 7)), R=[xk, 'w_in'], W=['B4'], sig=(k == 7))
                        yield ('act', ['B4'], ['Dm'])
                        A(lambda: nc.scalar.activation(out=ut, in_=B[4][:, :], func=AF.Copy), R=['B4'], W=['Dm'])
                        if spc:
                            yield ('sp', ['Dm'], ['pool_s_b'])
                            tk.dma('sp', lambda: nc.sync.dma_start(out=pool_s[:, 11:15, :], in_=ut[0:64, :]), ['Dm'], ['pool_s_b'], 'pool_s_b')
                        else:
                            yield ('sp', ['Dm'], ['pool_p'])
                            tk.dma('sp', lambda: nc.sync.dma_start(out=pool_p[:, :], in_=ut[113:128, :]), ['Dm'], ['pool_p'], 'pool_p')

                    if ckpt('t%de' % pos):
                        return
                    yield None
                    zk = 'zt%d' % (pos % 4)
                    Z = zt[pos % 4]
                    if not spc:
                        U = utile[s2]
                        if i == 0:
                            yield ('pool', ['umeta'], [uk])
                            G(lambda: nc.gpsimd.tensor_copy(out=U[:, :, 0:15], in_=umeta[:, :, 16:31]), R=['umeta'], W=[uk])
                        else:
                            yield ('pool', ['utile%d' % (1 - s2)], [uk])
                            G(lambda: nc.gpsimd.tensor_copy(out=U[:, :, 0:15], in_=utile[1 - s2][:, :, 128:143]),
                              R=['utile%d' % (1 - s2)], W=[uk])
                        Ws = yield from pool_sums(U, Wa, Wb, 1, 143, uk, 'Wa', 'Wb')
                        for g in range(4):
                            yield ('dve', ['Wa', 'Wb', uk], [zk])
                            V(lambda: nc.vector.scalar_tensor_tensor(out=Z[:, g, :], in0=Ws[g][:, g, 15:143], scalar=1.0 / WIN[g],
                                                                     in1=U[:, g, 15:143], op0=ALU.mult, op1=ALU.subtract),
                              R=['Wa', 'Wb', uk], W=[zk])
                    else:
                        for g in range(4):
                            P(lambda: nc.tensor.transpose(out=B[4 + g // 2][:, (g % 2) * 240:(g % 2) * 240 + 128],
                                                          in_=spt[0][:, g * 128:(g + 1) * 128], identity=ident_f[:]),
                              R=['Dt', 'ident_f'], W=['B%d' % (4 + g // 2)], sig=False)
                            P(lambda: nc.tensor.transpose(out=B[4 + g // 2][:, (g % 2) * 240 + 128:(g % 2) * 240 + 240],
                                                          in_=spt[1][0:112, g * 128:(g + 1) * 128], identity=ident_f[0:112, 0:112]),
                              R=['interB', 'ident_f'], W=['B%d' % (4 + g // 2)], sig=True)
                        for g in range(4):
                            yield ('dve', ['B%d' % (4 + g // 2)], ['usamp'])
                            V(lambda: nc.vector.tensor_copy(out=usamp[:, 16 * g:16 * g + 16, 0:15],
                                                            in_=B[4 + g // 2][:, (g % 2) * 240:(g % 2) * 240 + 240].rearrange("p (j i) -> p j i", i=15)),
                              R=['B%d' % (4 + g // 2)], W=['usamp'])
                        yield ('pool', (), [zk])
                        G(lambda: nc.gpsimd.memset(Z[:], 0.0), W=[zk])
                        Ws = yield from pool_sums(usamp, Was, Wbs, 16, 19, 'usamp', 'Was', 'Wbs')
                        for g in range(4):
                            yield ('pool', ['Was', 'Wbs'], ['Was', 'Wbs'])
                            G(lambda: nc.gpsimd.tensor_scalar(out=Ws[g][:, 16 * g:16 * g + 16, 15:19], in0=Ws[g][:, 16 * g:16 * g + 16, 15:19],
                                                              scalar1=1.0 / WIN[g], scalar2=None, op0=ALU.mult), R=['Was', 'Wbs'], W=['Was', 'Wbs'])
                            yield ('pool', ['Was', 'Wbs', 'usamp'], [zk])
                            G(lambda: nc.gpsimd.tensor_tensor(out=Z[:, g, 0:64].rearrange("p (j t) -> p j t", t=4),
                                                              in0=Ws[g][:, 16 * g:16 * g + 16, 15:19],
                                                              in1=usamp[:, 16 * g:16 * g + 16, 15:19], op=ALU.subtract),
                              R=['Was', 'Wbs', 'usamp'], W=[zk])
                        Wm = yield from pool_sums(umeta, Wa[:, :, 0:31], Wb[:, :, 0:31], 1, 31, 'umeta', 'Wa', 'Wb')
                        for g in range(4):
                            yield ('pool', ['Wa', 'Wb', 'rc_meta'], ['Wa', 'Wb'])
                            G(lambda: nc.gpsimd.tensor_tensor(out=Wm[g][:, g, 15:31], in0=Wm[g][:, g, 15:31], in1=rc_meta[:, g, :], op=ALU.mult),
                              R=['Wa', 'Wb', 'rc_meta'], W=['Wa', 'Wb'])
                            yield ('pool', ['Wa', 'Wb', 'umeta'], [zk])
                            G(lambda: nc.gpsimd.tensor_tensor(out=Z[:, g, 64:80], in0=Wm[g][:, g, 15:31], in1=umeta[:, g, 15:31], op=ALU.subtract),
                              R=['Wa', 'Wb', 'umeta'], W=[zk])
                    mk = 'mixT%d' % s2
                    if ckpt('t%df' % pos):
                        return
                    yield 'SPLIT'
                    for n_, r_ in enumerate((R_AREL, R_EMN, R_WS)):
                        P(lambda: nc.tensor.transpose(out=B[3][:, 256 + 4 * n_:260 + 4 * n_], in_=rw[:, r_, :], identity=ident_f[0:4, 0:4]),
                          R=[rk, 'ident_f'], W=['B3'], sig=(n_ == 2))
                    cl_ = cols[s2]
                    yield ('dve', ['B3'], [ck])
                    V(lambda: nc.vector.tensor_copy(out=cl_[:, 0:12], in_=B[3][:, 256:268]), R=['B3'], W=[ck])
                    for h in range(4):
                        P(lambda: nc.tensor.matmul(B[6][:, h * 128:(h + 1) * 128], lhsT=sel[:, h, :], rhs=rw[:, R_MREL, :],
                                                   start=True, stop=True), R=[rk, 'sel'], W=['B6'], sig=(h == 3))

                    B6v = B[6][:].rearrange("p (h t) -> p h t", h=4)
                    B7v = B[7][:].rearrange("p (h t) -> p h t", h=4)
                    for h in range(4):
                        P(lambda: nc.tensor.matmul(B[7][:, h * 128:(h + 1) * 128], lhsT=kT[s2][:, h, :], rhs=qT[s2][:, h, :], start=True, stop=True),
                          R=['kT%d' % s2, 'qT%d' % s2], W=['B7'], sig=(h == 3))
                    for h in range(4):
                        yield ('act', ['B6', ck], ['Dt'])
                        A(lambda: nc.scalar.activation(out=Dt[:, h, :], in_=B[6][:, h * 128:(h + 1) * 128], func=AF.Exp, scale=-1.0,
                                                       bias=cl_[:, h:h + 1]), R=['B6', ck], W=['Dt'])
                    yield ('act', ['B6'], ['interB'])
                    A(lambda: nc.scalar.activation(out=interB[:], in_=B6v, func=AF.Exp, scale=-1.0), R=['B6'], W=['interB'])
                    msk = maskS if spc else maskC
                    yield ('pool', ['Dt', 'maskS', 'maskC'], ['Dm'])
                    G(lambda: nc.gpsimd.tensor_tensor(out=Dm[:], in0=Dt[:], in1=msk[:].unsqueeze(1).to_broadcast([128, 4, 128]), op=ALU.mult),
                      R=['Dt', 'maskS', 'maskC'], W=['Dm'])
                    yield ('dve', ['B7', 'Dm'], ['Pt'])
                    V(lambda: nc.vector.tensor_tensor(out=Pt[:], in0=B7v, in1=Dm[:], op=ALU.mult), R=['B7', 'Dm'], W=['Pt'])
                    yield ('dve', ['qT%d' % s2, 'interB'], ['qs'])
                    V(lambda: nc.vector.tensor_tensor(out=qs[:], in0=qT[s2][:], in1=interB[:], op=ALU.mult), R=['qT%d' % s2, 'interB'], W=['qs'])
                    cb = Cb[s2]; nbb = nb[s2]
                    cbk = 'Cb%d' % s2
                    if spc:
                        for j in range(16):
                            yield ('pool', ['qs', 'selmask'], ['qz'])
                            G(lambda: nc.gpsimd.tensor_tensor(out=qz[:, j, :, :], in0=qs[:, :, 0:64],
                                                              in1=selmask[:, j, :].unsqueeze(1).to_broadcast([128, 4, 64]), op=ALU.mult),
                              R=['qs', 'selmask'], W=['qz'])
                        P(lambda: nc.tensor.transpose(out=B[3][:, 300:364], in_=n0t[:, :], identity=ident_f[0:64, 0:64]),
                          R=['n0t', 'ident_f'], W=['B3'])
                        yield ('dve', ['B3'], ['n0f'])
                        V(lambda: nc.vector.tensor_copy(out=n0f[:], in_=B[3][:, 300:364]), R=['B3'], W=['n0f'])
                        yield ('act', ['B3'], ['n0b'])
                        A(lambda: nc.scalar.activation(out=n0b[:], in_=B[3][:, 300:364], func=AF.Copy), R=['B3'], W=['n0b'])
                    if ckpt('t%dg' % pos):
                        return
                    yield None
                    for h in range(4):
                        hc = slice(h * 128, (h + 1) * 128)
                        if not spc:
                            P(lambda: nc.tensor.matmul(B[6][:, hc], lhsT=Pt[:, h, :], rhs=vt[s2][:, hc], start=True, stop=False),
                              R=['Pt', 'vt%d' % s2], W=['B6'], sig=False)
                            P(lambda: nc.tensor.matmul(B[6][:, hc], lhsT=qs[:, h, :], rhs=cb[:, h, :], start=False, stop=True),
                              R=['qs', cbk], W=['B6'], sig=True)
                            P(lambda: nc.tensor.matmul(B[3][:, 272 + h:273 + h], lhsT=Pt[:, h, :], rhs=ones_bf[:, 0:1], start=True, stop=False),
                              R=['Pt', 'ones_bf'], W=['B3'], sig=False)
                            P(lambda: nc.tensor.matmul(B[3][:, 272 + h:273 + h], lhsT=qs[:, h, :], rhs=nbb[:, h:h + 1], start=False, stop=True),
                              R=['qs', cbk], W=['B3'], sig=True)
                        else:
                            P(lambda: nc.tensor.matmul(B[6][64:128, hc], lhsT=Pt[:, h, 64:128], rhs=vt[s2][:, hc], start=True, stop=True),
                              R=['Pt', 'vt%d' % s2], W=['B6'], sig=False)
                            P(lambda: nc.tensor.matmul(B[3][64:128, 272 + h:273 + h], lhsT=Pt[:, h, 64:128], rhs=ones_bf[:, 0:1], start=True, stop=True),
                              R=['Pt', 'ones_bf'], W=['B3'], sig=False)
                            for j in range(16):
                                P(lambda: nc.tensor.matmul(B[6][0:64, hc], lhsT=qz[:, j, h, :], rhs=C0b[:, j * 4 + h, :],
                                                           start=(j == 0), stop=False), R=['qz', 'C0b'], W=['B6'], sig=False)
                            P(lambda: nc.tensor.matmul(B[6][0:64, hc], lhsT=Pt[:, h, 0:64], rhs=vt[s2][:, hc], start=False, stop=True),
                              R=['Pt', 'vt%d' % s2], W=['B6'], sig=True)
                            for j in range(16):
                                P(lambda: nc.tensor.matmul(B[3][0:64, 272 + h:273 + h], lhsT=qz[:, j, h, :], rhs=n0b[:, j * 4 + h:j * 4 + h + 1],
                                                           start=(j == 0), stop=False), R=['qz', 'n0b'], W=['B3'], sig=False)
                            P(lambda: nc.tensor.matmul(B[3][0:64, 272 + h:273 + h], lhsT=Pt[:, h, 0:64], rhs=ones_bf[:, 0:1], start=False, stop=True),
                              R=['Pt', 'ones_bf'], W=['B3'], sig=True)
                    if ckpt('t%dh' % pos):
                        return
                    yield None
                    nS = numS[s2]
                    nSk = 'numS%d' % s2
                    yield ('act', ['B6'], [nSk])
                    A(lambda: nc.scalar.activation(out=nS[:], in_=B[6][:, :], func=AF.Copy), R=['B6'], W=[nSk])
                    c_ = cl_
                    yield ('dve', ['B3'], [ck])
                    V(lambda: nc.vector.tensor_copy(out=c_[:, 12:16], in_=B[3][:, 272:276]), R=['B3'], W=[ck])
                    if ckpt('t%di' % pos):
                        return
                    yield None
                    for h in range(4):
                        P(lambda: nc.tensor.transpose(out=B1b[:, 512 + h * 128:512 + (h + 1) * 128], in_=kT[s2][:, h, :], identity=ident_bf[:]),
                          R=['kT%d' % s2, 'ident_bf'], W=['B1'], sig=(h == 3))
                    if ckpt('t%di1' % pos):
                        return
                    yield None
                    B1k = B1b[:, 512:1024].rearrange("p (h t) -> p h t", h=4)
                    if spc:
                        for h in range(4):
                            yield ('dve', ['maskcols', ck], ['wsm'])
                            V(lambda: nc.vector.tensor_scalar(out=wsm[:, :, h], in0=maskcols[:, :], scalar1=c_[:, 8 + h:9 + h], scalar2=None,
                                                              op0=ALU.mult), R=['maskcols', ck], W=['wsm'])
                        yield ('act', ['B1'], ['ktm'])
                        A(lambda: nc.scalar.activation(out=ktm[:], in_=B1k, func=AF.Copy), R=['B1'], W=['ktm'])
                        wsx = wsm[:, 16, :]
                        wk = ['wsm']
                    else:
                        wsx = c_[:, 8:12]
                        wk = [ck]
                    if ckpt('t%di2' % pos):
                        return
                    yield None
                    yield ('dve', ['B1'] + wk, ['kw'])
                    V(lambda: nc.vector.tensor_tensor(out=kw[:], in0=B1k, in1=wsx.unsqueeze(2).to_broadcast([128, 4, 128]), op=ALU.mult),
                      R=['B1'] + wk, W=['kw'])
                    if ckpt('t%di3' % pos):
                        return
                    yield None
                    for h in range(4):
                        P(lambda: nc.tensor.matmul(B[7][:, h * 128:(h + 1) * 128], lhsT=kw[:, h, :], rhs=vt[s2][:, h * 128:(h + 1) * 128],
                                                   start=True, stop=True), R=['kw', 'vt%d' % s2], W=['B7'], sig=False)
                        P(lambda: nc.tensor.matmul(B[3][:, 280 + h:281 + h], lhsT=kw[:, h, :], rhs=ones_bf[:, 0:1], start=True, stop=True),
                          R=['kw', 'ones_bf'], W=['B3'], sig=(h == 3))
                    if ckpt('t%di4' % pos):
                        return
                    yield None
                    for h in range(4):
                        yield ('dve', ['Cf', 'interB', 'B7'], ['Cf'])
                        V(lambda: nc.vector.scalar_tensor_tensor(out=Cf[:, h, :], in0=Cf[:, h, :], scalar=interB[:, h, 127:128],
                                                                 in1=B[7][:, h * 128:(h + 1) * 128], op0=ALU.mult, op1=ALU.add),
                          R=['Cf', 'interB', 'B7'], W=['Cf'])
                    if ckpt('t%di5' % pos):
                        return
                    yield None
                    yield ('dve', ['nf', 'interB'], ['nf'])
                    V(lambda: nc.vector.tensor_tensor(out=nf[:], in0=nf[:], in1=interB[:, :, 127], op=ALU.mult), R=['nf', 'interB'], W=['nf'])
                    yield ('dve', ['nf', 'B3'], ['nf'])
                    V(lambda: nc.vector.tensor_tensor(out=nf[:], in0=nf[:], in1=B[3][:, 280:284], op=ALU.add), R=['nf', 'B3'], W=['nf'])
                    if ckpt('t%di6' % pos):
                        return
                    yield None
                    ncb = Cb[1 - s2]; nnb = nb[1 - s2]
                    yield ('act', ['Cf'], ['Cb%d' % (1 - s2)])
                    A(lambda: nc.scalar.activation(out=ncb[:], in_=Cf[:], func=AF.Copy), R=['Cf'], W=['Cb%d' % (1 - s2)])
                    yield ('act', ['nf'], ['Cb%d' % (1 - s2)])
                    A(lambda: nc.scalar.activation(out=nnb[:], in_=nf[:], func=AF.Copy), R=['nf'], W=['Cb%d' % (1 - s2)])
                    if i == 15:
                        yield ('sp', ['Cf'], ['C_p'])
                        tk.dma('sp', lambda: nc.sync.dma_start(out=C_p.rearrange("h k v -> k h v"), in_=Cf[:]), ['Cf'], ['C_p'], 'C_p')
                        P(lambda: nc.tensor.transpose(out=B[3][0:4, 384:512], in_=nf[:, :], identity=ident_f[:]), R=['nf', 'ident_f'], W=['B3'])
                        yield ('dve', ['B3'], ['npt'])
                        V(lambda: nc.vector.tensor_copy(out=npt[:], in_=B[3][0:4, 384:512]), R=['B3'], W=['npt'])
                        yield ('sp', ['npt'], ['n_p'])
                        tk.dma('sp', lambda: nc.sync.dma_start(out=n_p[:, :], in_=npt[:]), ['npt'], ['n_p'], 'n_p')
                    if ckpt('t%dj' % pos):
                        return
                    yield None
                    if spc:
                        for h in range(4):
                            P(lambda: nc.tensor.matmul(B[3][:, 400 + h * 16:416 + h * 16], lhsT=sel[:, h, :], rhs=decr[:, :], start=True, stop=True),
                              R=['sel', 'decr'], W=['B3'], sig=(h == 3))
                        yield ('dve', ['B3'], ['decS'])
                        V(lambda: nc.vector.tensor_copy(out=decS[:], in_=B[3][:, 400:464].rearrange("p (h j) -> p h j", h=4)), R=['B3'], W=['decS'])
                        sCv = sC.rearrange("j h k v -> k (j h) v")
                        Csv = C_s.rearrange("j h k v -> k (j h) v")
                        def ld_eighth(e_):
                            bf_ = C0f[e_ % 4]
                            bk_ = 'C0f%d' % (e_ % 4)
                            wk_ = [bk_] + (['usamp', 'Was', 'Wbs'] if (e_ % 4) >= 2 else [])
                            tk.dma('sp', lambda: nc.sync.dma_start(out=bf_[:], in_=sCv[:, 8 * e_:8 * e_ + 8, :]), [], wk_, 'ld' + bk_)
                        yield ('sp', [], ['C0f2', 'usamp', 'Was', 'Wbs'])
                        ld_eighth(2)
                        for ei in range(8):
                            cf = C0f[ei % 4]
                            cfk = 'C0f%d' % (ei % 4)
                            if ei + 3 < 8:
                                yield ('sp', [], ['C0f%d' % ((ei + 3) % 4)])
                                ld_eighth(ei + 3)
                            for jj in range(2):
                                j = 2 * ei + jj
                                kz = kwz[j % 2]
                                kzk = 'kwz%d' % (j % 2)
                                yield ('pool', ['ktm', 'wsm'], [kzk])
                                G(lambda: nc.gpsimd.tensor_tensor(out=kz[:], in0=ktm[:], in1=wsm[:, j, :].unsqueeze(2).to_broadcast([128, 4, 128]),
                                                                  op=ALU.mult), R=['ktm', 'wsm'], W=[kzk])
                                for h in range(4):
                                    P(lambda: nc.tensor.matmul(B[7][:, h * 128:(h + 1) * 128], lhsT=kz[:, h, :], rhs=vt[s2][:, h * 128:(h + 1) * 128],
                                                               start=True, stop=True), R=[kzk, 'vt%d' % s2], W=['B7'], sig=False)
                                    P(lambda: nc.tensor.matmul(B[3][:, 300 + h * 16 + j:301 + h * 16 + j], lhsT=kz[:, h, :], rhs=ones_bf[:, 0:1],
                                                               start=True, stop=True), R=[kzk, 'ones_bf'], W=['B3'], sig=(h == 3))
                                for h in range(4):
                                    yield ('dve', [cfk, 'decS', 'B7'], [cfk])
                                    V(lambda: nc.vector.scalar_tensor_tensor(out=cf[:, jj * 4 + h, :], in0=cf[:, jj * 4 + h, :],
                                                                             scalar=decS[:, h, j:j + 1], in1=B[7][:, h * 128:(h + 1) * 128],
                                                                             op0=ALU.mult, op1=ALU.add), R=[cfk, 'decS', 'B7'], W=[cfk])
                            yield ('sp', [cfk], ['C_s%d' % ei])
                            tk.dma('sp', lambda: nc.sync.dma_start(out=Csv[:, 8 * ei:8 * ei + 8, :], in_=cf[:]), [cfk], ['C_s%d' % ei], 'st' + cfk)
                        yield ('dve', ['n0f', 'decS'], ['n1s'])
                        V(lambda: nc.vector.tensor_tensor(out=n1s[:], in0=n0f[:].rearrange("p (j h) -> p h j", h=4), in1=decS[:], op=ALU.mult),
                          R=['n0f', 'decS'], W=['n1s'])
                        yield ('dve', ['n1s', 'B3'], ['n1s'])
                        V(lambda: nc.vector.tensor_tensor(out=n1s[:], in0=n1s[:], in1=B[3][:, 300:364].rearrange("p (h j) -> p h j", h=4), op=ALU.add),
                          R=['n1s', 'B3'], W=['n1s'])
                        P(lambda: nc.tensor.transpose(out=B[3][0:64, 384:512], in_=n1s[:].rearrange("p h j -> p (h j)"), identity=ident_f[:]),
                          R=['n1s', 'ident_f'], W=['B3'])
                        yield ('dve', ['B3'], ['n1t'])
                        V(lambda: nc.vector.tensor_copy(out=n1t[:], in_=B[3][0:64, 384:512]), R=['B3'], W=['n1t'])
                        n_s_v = n_s.rearrange("j h k -> h j k")
                        for h in range(4):
                            yield ('sp', ['n1t'], ['n_s%d' % h])
                            tk.dma('sp', lambda: nc.sync.dma_start(out=n_s_v[h], in_=n1t[16 * h:16 * h + 16, :]), ['n1t'], ['n_s%d' % h], 'n_s')

                    yield 'SPLIT'
                    xr_ = xr[pos % 2]
                    xrk = 'xr%d' % (pos % 2)
                    load_into(xr_, xrk, i, 'xr%d' % (pos % 2))
                    c_ = cl_
                    yield ('dve', [ck], [ck])
                    V(lambda: nc.vector.scalar_tensor_tensor(out=c_[:, 12:16], in0=c_[:, 12:16], scalar=-1.0, in1=c_[:, 12:16],
                                                             op0=ALU.mult, op1=ALU.max), R=[ck], W=[ck])
                    yield ('dve', [ck], [ck])
                    V(lambda: nc.vector.tensor_tensor(out=c_[:, 12:16], in0=c_[:, 12:16], in1=c_[:, 4:8], op=ALU.max), R=[ck], W=[ck])
                    yield ('dve', [ck], [ck])
                    V(lambda: nc.vector.reciprocal(out=c_[:, 12:16], in_=c_[:, 12:16]), R=[ck], W=[ck])
                    for h in range(4):
                        yield ('act', [nSk], ['hj', ck])
                        A(lambda: nc.scalar.activation(out=hj[:], in_=nS[:, h * 128:(h + 1) * 128], func=AF.Square,
                                                       accum_out=c_[:, 16 + h:17 + h]), R=[nSk], W=['hj', ck])
                    yield ('dve', [ck], [ck])
                    V(lambda: nc.vector.tensor_tensor(out=c_[:, 16:20], in0=c_[:, 16:20], in1=c_[:, 12:16], op=ALU.mult), R=[ck], W=[ck])
                    yield ('dve', [ck], [ck])
                    V(lambda: nc.vector.tensor_tensor(out=c_[:, 16:20], in0=c_[:, 16:20], in1=c_[:, 12:16], op=ALU.mult), R=[ck], W=[ck])
                    yield ('act', [ck], [ck])
                    A(lambda: nc.scalar.activation(out=c_[:, 20:24], in_=c_[:, 16:20], func=AF.Ln, scale=1.0 / 128, bias=EPS), R=[ck], W=[ck])
                    yield ('act', [ck], [ck])
                    A(lambda: nc.scalar.activation(out=c_[:, 20:24], in_=c_[:, 20:24], func=AF.Exp, scale=-0.5), R=[ck], W=[ck])
                    yield ('dve', [ck], [ck])
                    V(lambda: nc.vector.tensor_tensor(out=c_[:, 20:24], in0=c_[:, 20:24], in1=c_[:, 12:16], op=ALU.mult), R=[ck], W=[ck])
                    for h in range(4):
                        yield ('dve', [nSk, ck, ok_], ['mixh'])
                        V(lambda: nc.vector.scalar_tensor_tensor(out=mixh[:, h * 128:(h + 1) * 128], in0=nS[:, h * 128:(h + 1) * 128],
                                                                 scalar=c_[:, 20 + h:21 + h], in1=og[so][:, h * 128:(h + 1) * 128],
                                                                 op0=ALU.mult, op1=ALU.mult), R=[nSk, ck, ok_], W=['mixh'])
                    for h in range(4):
                        P(lambda: nc.tensor.transpose(out=B1b[:, h * 128:(h + 1) * 128], in_=mixh[:, h * 128:(h + 1) * 128], identity=ident_bf[:]),
                          R=['mixh', 'ident_bf'], W=['B1'], sig=(h == 3))
                    yield ('act', ['B1'], [mk])
                    A(lambda: nc.scalar.activation(out=mixT[s2][:, 4:8, :], in_=B1b[:, 0:512].rearrange("p (h t) -> p h t", h=4), func=AF.Copy),
                      R=['B1'], W=[mk])

                    if ckpt('t%dk' % pos):
                        return
                    yield 'SPLIT'
                    for g in range(4):
                        P(lambda: nc.tensor.matmul(B[2][:, g * 128:(g + 1) * 128], lhsT=w_pool_sb[:, g, :], rhs=Z[:, g, :], start=True, stop=True),
                          R=[zk, 'w_pool'], W=['B2'], sig=(g == 3))
                    yield ('dve', ['B2', 'pscol'], [mk])
                    V(lambda: nc.vector.tensor_tensor(out=mixT[s2][:, 0:4, :], in0=B2v, in1=pscol[:].unsqueeze(2).to_broadcast([128, 4, 128]),
                                                      op=ALU.mult), R=['B2', 'pscol'], W=[mk])

                    for half in range(2):
                        for k in range(8):
                            P(lambda: nc.tensor.matmul(B[4 + half][:, :], lhsT=mixT[s2][:, k, :], rhs=w_out_sb[:, k, half * 512:(half + 1) * 512],
                                                       start=(k == 0), stop=(k == 7)), R=[mk, 'w_out'], W=['B%d' % (4 + half)], sig=(k == 7))
                    for half in range(2):
                        yield ('dve', [xrk, 'B%d' % (4 + half)], [xrk])
                        V(lambda: nc.vector.tensor_tensor(out=xr_[:, half * 512:(half + 1) * 512], in0=xr_[:, half * 512:(half + 1) * 512],
                                                          in1=B[4 + half][:, :], op=ALU.add), R=[xrk, 'B%d' % (4 + half)], W=[xrk])
                    yield ('sp', [xrk], [x2key(i)])
                    tk.dma('sp', lambda: nc.sync.dma_start(out=x2s[128 * i:128 * i + 128, :], in_=xr_[:]), [xrk], [x2key(i)], 'x2st%d' % (pos % 2))

                gens = []
                pending = {}
                pos_next = 0
                load_x(1)
                tk.dma('sp', lambda: nc.sync.dma_start(out=spt[0][:, :], in_=spool.rearrange("j i c -> (j i) c")[0:128, :]),
                       [], ['Dt'], 'spt0')
                tk.dma('sp', lambda: nc.sync.dma_start(out=spt[1][0:112, :], in_=spool.rearrange("j i c -> (j i) c")[128:240, :]),
                       [], ['interB'], 'spt1')
                tk.dma('sp', lambda: nc.sync.dma_start(out=n0t[:, :], in_=sn[:, :]), [], ['n0t'], 'n0t')
                with nc.allow_non_contiguous_dma(reason="tiny"):
                    tk.dma('sp', lambda: nc.sync.dma_start(out=m0r[:, :], in_=sm_.rearrange("j h -> h j")), [], ['m0r'], 'm0r')
                tk.dma('pool', lambda: nc.gpsimd.dma_start(out=C0b[:], in_=sC.rearrange("j h k v -> k (j h) v")),
                       [], ['C0b', 'stg0', 'stg1', 'stg2', 'stg3'], 'C0b')
                tk.dma('sp', lambda: nc.sync.dma_start(out=C0f[0][:], in_=sC.rearrange("j h k v -> k (j h) v")[:, 0:8, :]), [], ['C0f0', 'stg1', 'stg2'], 'ldC0f0')
                tk.dma('sp', lambda: nc.sync.dma_start(out=C0f[1][:], in_=sC.rearrange("j h k v -> k (j h) v")[:, 8:16, :]), [], ['C0f1', 'stg1', 'stg2'], 'ldC0f1')
                tk.dma('sp', lambda: nc.sync.dma_start(out=pool_s[:, 0:11, :], in_=spool[:, 4:15, :]), [], ['pool_s_a'], 'pool_s_a')


                def advance(g_):
                    while True:
                        try:
                            v = next(g_)
                        except StopIteration:
                            v = 'END'
                        if v is None:
                            continue
                        pending[id(g_)] = v
                        return

                while gens or pos_next < NT:
                    if pos_next < NT:
                        if 0 < pos_next and pos_next + 1 < NT:
                            load_x(pos_next + 1)
                        g_new = mixer_tile(order[pos_next], pos_next)
                        gens.append(g_new)
                        pending[id(g_new)] = 'SPLIT'
                        pos_next += 1
                    if 7 <= pos_next <= 12:
                        w_up_v0 = w_up.rearrange("(k p) f -> p k f", p=128)
                        for blk in [pos_next - 7]:
                            dst = arena[:, blk * 2048:(blk + 1) * 2048].bitcast(BF16).rearrange("p (k c) -> p k c", k=8)
                            tk.dma('pool', lambda: nc.gpsimd.dma_start(out=dst, in_=w_up_v0[:, :, blk * 512:(blk + 1) * 512]),
                                   [], ['w_up%d' % blk, 'C0b', 'C0f0', 'C0f1', 'C0f2', 'C0f3', 'qz', 'usamp', 'Was', 'Wbs', 'kwz0'], 'w_up%d' % blk)
                    for g_ in gens:
                        advance(g_)
                    while True:
                        best = None
                        best_t = None
                        for gi_, g_ in enumerate(gens):
                            d_ = pending[id(g_)]
                            if isinstance(d_, tuple):
                                t_ = tk.estimate(d_[0], d_[1], d_[2]) + BETA * gi_
                                if best is None or t_ < best_t:
                                    best, best_t = g_, t_
                        if best is None:
                            break
                        if SCHED == 'rr':
                            for g_ in list(gens):
                                if isinstance(pending[id(g_)], tuple):
                                    advance(g_)
                                    junk_mm()
                        else:
                            advance(best)
                            junk_mm()
                    gens = [g_ for g_ in gens if pending[id(g_)] != 'END']
                    if stop[0]:
                        break

                P(lambda: nc.tensor.matmul(B[0][:, 0:128], lhsT=ident_bf[:], rhs=ident_bf[:], start=True, stop=True), R=['ident_bf'], W=['B0'], sig=True)
                tk.barrier()
                ckpt('mixer')
                if stop[0]:
                    tk.finish()
                    return nc

            with ExitStack() as fs:
                w_up_hi = sb(fs, "w_up_hi", [128, 2, 8, 512], BF16)

                def wup(blk):
                    if blk < 6:
                        return arena[:, blk * 2048:(blk + 1) * 2048].bitcast(BF16).rearrange("p (k c) -> p k c", k=8)
                    return w_up_hi[:, blk - 6, :, :]
                w_dn_sb = sb(fs, "w_dn_sb", [128, 32, D], BF16)
                w_up_v = w_up.rearrange("(k p) f -> p k f", p=128)
                w_dn_v = w_down.rearrange("(f p) d -> p f d", p=128)
                for blk in range(6, 8):
                    tk.dma('pool', lambda: nc.gpsimd.dma_start(out=w_up_hi[:, blk - 6, :, :], in_=w_up_v[:, :, blk * 512:(blk + 1) * 512]),
                           [], ['w_up%d' % blk], 'w_up%d' % blk)
                for blk in range(8):
                    tk.dma('pool', lambda: nc.gpsimd.dma_start(out=w_dn_sb[:, blk * 4:(blk + 1) * 4, :], in_=w_dn_v[:, blk * 4:(blk + 1) * 4, :]),
                           [], ['w_dn%d' % blk], 'w_dn%d' % blk)
                NXF = 6
                xf = [sb(fs, "xf%d" % i, [128, D]) for i in range(NXF)]
                junk2 = sb(fs, "junk2", [128, D], BF16)
                xn2 = [sb(fs, "xn2_%d" % i, [128, D], BF16) for i in range(2)]
                st2 = [sb(fs, "st2_%d" % i, [128, 8]) for i in range(2)]
                xn2T = [sb(fs, "xn2T%d" % i, [128, 8, 256], BF16) for i in range(2)]
                aT = [sb(fs, "aT%d" % i, [128, 32, 256], BF16) for i in range(2)]
                rr = [sb(fs, "rr%d" % i, [128, 2, 256]) for i in range(2)]

                groups = [[2 * g, 2 * g + 1] for g in range(8)] + [[SP]]
                tiles_in_order = [t for g in groups for t in g]

                slot_of = {}
                free_slots = list(range(NXF))
                load_queue = list(tiles_in_order)

                def prefetch():
                    while free_slots and load_queue:
                        i = load_queue.pop(0)
                        s = free_slots.pop(0)
                        slot_of[i] = s
                        tk.dma('sp', lambda: nc.sync.dma_start(out=xf[s][:], in_=x2s[128 * i:128 * i + 128, :]), [x2key(i)], ['xf%d' % s], 'xf%d' % s)

                prefetch()
                tcount = [0]
                evc = [0]

                xn_slot = {}

                def ffn_norm(gi, tl, only=None):
                    for ti, i in enumerate(tl):
                        if only is not None and ti != only:
                            continue
                        n = tcount[0]; tcount[0] += 1
                        s = slot_of[i]
                        s2 = n % 2
                        xn_slot[i] = s2
                        sk = 'st2_%d' % s2
                        xkey = 'xf%d' % s
                        A(lambda: nc.scalar.activation(out=junk2[:], in_=xf[s][:], func=AF.Square, accum_out=st2[s2][:, 0:1]),
                          R=[xkey], W=['junk2', sk])
                        A(lambda: nc.scalar.activation(out=st2[s2][:, 1:2], in_=st2[s2][:, 0:1], func=AF.Ln, scale=1.0 / D, bias=EPS), R=[sk], W=[sk])
                        A(lambda: nc.scalar.activation(out=st2[s2][:, 1:2], in_=st2[s2][:, 1:2], func=AF.Exp, scale=-0.5), R=[sk], W=[sk])
                        A(lambda: nc.scalar.activation(out=xn2[s2][:], in_=xf[s][:], func=AF.Copy, scale=st2[s2][:, 1:2]),
                          R=[xkey, sk], W=['xn2_%d' % s2])

                def ffn_tr(gi, tl, only=None):
                    gs = gi % 2
                    xTk = 'xn2T%d' % gs
                    for ti, i in enumerate(tl):
                        if only is not None and ti != only:
                            continue
                        s2 = xn_slot[i]
                        for k in range(8):
                            P(lambda: nc.tensor.transpose(out=B0b[:, k * 128:(k + 1) * 128], in_=xn2[s2][:, k * 128:(k + 1) * 128], identity=ident_bf[:]),
                              R=['xn2_%d' % s2, 'ident_bf'], W=['B0'], sig=(k == 7))
                        V(lambda: nc.vector.tensor_tensor(out=xn2T[gs][:, :, ti * 128:(ti + 1) * 128], in0=B0b.rearrange("p (k t) -> p k t", k=8),
                                                          in1=gcol2[:].unsqueeze(2).to_broadcast([128, 8, 128]), op=ALU.mult),
                          R=['B0', 'gcol2'], W=[xTk])

                def ffn_up(gi, tl, hook=None):
                    gs = gi % 2
                    ntok = 64 if tl == [SP] else 128 * len(tl)
                    xTk = 'xn2T%d' % gs
                    ak = 'aT%d' % gs
                    for fp in range(16):
                        if hook is not None and fp in (3, 9):
                            hook(0 if fp == 3 else 1)
                        e = evc[0]; evc[0] += 1
                        bank = 1 + e % 3
                        bkey = 'B%d' % bank
                        for hf in range(2):
                            f = 2 * fp + hf
                            for k in range(8):
                                P(lambda: nc.tensor.matmul(B[bank][:, hf * 256:hf * 256 + ntok], lhsT=wup(f // 4)[:, k, (f % 4) * 128:(f % 4 + 1) * 128],
                                                           rhs=xn2T[gs][:, k, 0:ntok], start=(k == 0), stop=(k == 7)),
                                  R=[xTk, 'w_up%d' % (f // 4)], W=[bkey], sig=(hf == 1 and k == 7))
                        r_ = rr[e % 2]
                        rk_ = 'rr%d' % (e % 2)
                        A(lambda: nc.scalar.activation(out=r_[:, :, 0:ntok], in_=B[bank][:].rearrange("p (a t) -> p a t", a=2)[:, :, 0:ntok],
                                                       func=AF.Relu), R=[bkey], W=[rk_])
                        V(lambda: nc.vector.tensor_tensor(out=aT[gs][:, 2 * fp:2 * fp + 2, 0:ntok], in0=r_[:, :, 0:ntok], in1=r_[:, :, 0:ntok],
                                                          op=ALU.mult), R=[rk_], W=[ak])

                def ffn_down(gi, tl, only=None):
                    gs = gi % 2
                    ak = 'aT%d' % gs
                    for ti, i in enumerate(tl):
                        if only is not None and ti != only:
                            continue
                        s = slot_of[i]
                        xkey = 'xf%d' % s
                        db = 4 + 2 * (ti % 2)
                        nr = 64 if i == SP else 128
                        for half in range(2):
                            for f in range(32):
                                P(lambda: nc.tensor.matmul(B[db + half][0:nr, :], lhsT=aT[gs][:, f, ti * 128:ti * 128 + nr],
                                                           rhs=w_dn_sb[:, f, half * 512:(half + 1) * 512], start=(f == 0), stop=(f == 31)),
                                  R=[ak, 'w_dn%d' % (f // 4)], W=['B%d' % (db + half)], sig=(f == 31))
                        for half in range(2):
                            V(lambda: nc.vector.tensor_tensor(out=xf[s][0:nr, half * 512:(half + 1) * 512], in0=xf[s][0:nr, half * 512:(half + 1) * 512],
                                                              in1=B[db + half][0:nr, :], op=ALU.add), R=[xkey, 'B%d' % (db + half)], W=[xkey])
                        s2 = ti % 2
                        sk = 'st2_%d' % s2
                        A(lambda: nc.scalar.activation(out=junk2[0:nr, :], in_=xf[s][0:nr, :], func=AF.Square, accum_out=st2[s2][0:nr, 2:3]), R=[xkey], W=['junk2', sk])
                        A(lambda: nc.scalar.activation(out=st2[s2][0:nr, 3:4], in_=st2[s2][0:nr, 2:3], func=AF.Ln, scale=1.0 / D, bias=EPS), R=[sk], W=[sk])
                        A(lambda: nc.scalar.activation(out=st2[s2][0:nr, 3:4], in_=st2[s2][0:nr, 3:4], func=AF.Exp, scale=-0.5), R=[sk], W=[sk])
                        V(lambda: nc.vector.scalar_tensor_tensor(out=xf[s][0:nr, :], in0=xf[s][0:nr, :], scalar=st2[s2][0:nr, 3:4], in1=gfB[0:nr, :],
                                                                 op0=ALU.mult, op1=ALU.mult), R=[xkey, sk, 'gfB'], W=[xkey])
                        if i == SP:
                            tk.dma('sp', lambda: nc.sync.dma_start(out=ys[:, :], in_=xf[s][0:64, :]), [xkey], ['ys'], 'yst%d' % s)
                        else:
                            tk.dma('sp', lambda: nc.sync.dma_start(out=yp[128 * i:128 * i + 128, :], in_=xf[s][:]), [xkey], ['yp%d' % i], 'yst%d' % s)
                        free_slots.append(s)
                        prefetch()

                ng = len(groups)

                def nhook(g_):
                    if g_ >= ng:
                        return None
                    return lambda t_: (ffn_norm(g_, groups[g_], only=t_) if t_ < len(groups[g_]) else None)

                def tr_down(g_next, g_cur):
                    for t_ in range(2):
                        if g_next < ng and t_ < len(groups[g_next]):
                            ffn_tr(g_next, groups[g_next], only=t_)
                        if t_ < len(groups[g_cur]):
                            ffn_down(g_cur, groups[g_cur], only=t_)

                ffn_norm(0, groups[0]); ffn_tr(0, groups[0])
                ffn_up(0, groups[0], nhook(1)); ffn_tr(1, groups[1])
                ffn_up(1, groups[1], nhook(2))
                tr_down(2, 0)
                ffn_down(1, groups[1])
                for gi in range(2, ng):
                    ffn_up(gi, groups[gi], nhook(gi + 1))
                    tr_down(gi + 1, gi)


                pass
        except _Stop:
            ms.close()
            tk.barrier()
        tk.finish()
    return nc


_NC_CACHE = {}


def kernel(x_prompt, x_sample, state_pool, state_C, state_n, state_m, meta_tokens, norm1, w_in,
           b_gate, w_pool, pool_scale, head_gain, w_out, norm2, w_up, w_down, norm_f):
    f = lambda a: np.ascontiguousarray(np.asarray(a, dtype=np.float32))
    if 'nc' not in _NC_CACHE:
        _NC_CACHE['nc'] = build_nc()
    nc = _NC_CACHE['nc']
    x_prompt = f(x_prompt); x_sample = f(x_sample)
    shared = {
        "meta": f(meta_tokens), "norm1": f(norm1).reshape(D), "w_in": f(w_in).reshape(D, INC),
        "b_gate": f(b_gate).reshape(8), "w_pool": f(w_pool).reshape(4, 128, 128),
        "pool_scale": f(pool_scale).reshape(512), "head_gain": f(head_gain).reshape(512),
        "w_out": f(w_out).reshape(D, D), "norm2": f(norm2).reshape(D), "w_up": f(w_up).reshape(D, DFF),
        "w_down": f(w_down).reshape(DFF, D), "norm_f": f(norm_f).reshape(D),
    }
    sp_ = f(state_pool)[0]; sc_ = f(state_C)[0]; sn_ = f(state_n)[0]; sm_ = f(state_m)[0]
    in_maps = []
    for c in range(NCORES):
        m = dict(shared)
        m["xp"] = x_prompt[c]
        m["xs"] = f(x_sample[16 * c:16 * c + 16].reshape(64, D))
        m["spool"] = f(sp_[16 * c:16 * c + 16])
        m["sC"] = f(sc_[16 * c:16 * c + 16])
        m["sn"] = f(sn_[16 * c:16 * c + 16].reshape(64, 128))
        m["sm"] = f(sm_[16 * c:16 * c + 16])
        in_maps.append(m)
    res = run_bass_kernel_spmd(nc, in_maps, core_ids=list(range(NCORES)))
    rs = res.results
    cat = lambda k: np.stack([np.asarray(r[k], dtype=np.float32) for r in rs], axis=0)
    y_prompt = cat("yp")
    y_sample = np.concatenate([np.asarray(r["ys"], dtype=np.float32).reshape(16, 4, D) for r in rs], axis=0)
    pool_prompt = cat("pool_p")[None]
    C_prompt = cat("C_p")[None]
    n_prompt = cat("n_p")[None]
    m_prompt = cat("m_p").reshape(NCORES, 4)[None]
    pool_sample = np.concatenate([np.asarray(r["pool_s"], dtype=np.float32) for r in rs], axis=0)[None]
    C_sample = np.concatenate([np.asarray(r["C_s"], dtype=np.float32) for r in rs], axis=0)[None]
    n_sample = np.concatenate([np.asarray(r["n_s"], dtype=np.float32) for r in rs], axis=0)[None]
    m_sample = np.concatenate([np.asarray(r["m_s"], dtype=np.float32) for r in rs], axis=0)[None]
    return (y_prompt, y_sample, pool_prompt, C_prompt, n_prompt, m_prompt,
            pool_sample, C_sample, n_sample, m_sample)
```

```python
import numpy as np
from contextlib import ExitStack
import concourse.bass as bass
import concourse.mybir as mybir
from concourse.bass_utils import run_bass_kernel_spmd

F32 = mybir.dt.float32
BF16 = mybir.dt.bfloat16
ALU = mybir.AluOpType
AF = mybir.ActivationFunctionType

NCORES = 8
D = 1024
SEQ = 2048
NT = 17
SP = 16
DFF = 4096
INC = 2568
EPS = 1e-6
WIN = (2, 4, 8, 16)
SCHED = 'list'
JUNK_EVERY = 0
OPLOG = None
PE_RATE = 1400.0
BETA = 0.01


class Tk:
    HOP = 0.0

    def __init__(self, nc, es):
        self.nc = nc
        self.es = es
        self.eng = {'pe': nc.tensor, 'act': nc.scalar, 'dve': nc.vector, 'pool': nc.gpsimd, 'sp': nc.sync}
        self.sems = {}
        self.cnt = {}
        for e in self.eng:
            self.sems['E' + e] = es.enter_context(nc.semaphore('s_' + e))
            self.cnt['E' + e] = 0
        self.lastw = {}
        self.readers = {}
        self.waited = {}
        self.nwaits = 0
        self.attach_waits = True
        self.oplog = None
        self.efree = {e: 0.0 for e in self.eng}
        self.wready = {}
        self.rready = {}

    def estimate(self, eng, R, W):
        t = self.efree.get(eng, 0.0)
        h = self.HOP
        for k in R:
            t = max(t, self.wready.get(k, 0.0) + h)
        for k in W:
            t = max(t, self.wready.get(k, 0.0) + h, self.rready.get(k, 0.0) + h)
        return t

    @staticmethod
    def _fsize(ins):
        try:
            ap = ins.ins.outs[0].ap
            n = 1
            for st_, cnt in ap[1:]:
                n *= cnt
            return n
        except Exception:
            return 128

    def _model(self, eng, R, W, ins, is_dma):
        start = self.estimate(eng, R, W)
        n = self._fsize(ins)
        if is_dma:
            self.efree[eng] = start + 0.1
            end = start + 2.5
        else:
            if eng == 'pe':
                cost = max(0.04, n / PE_RATE)
            elif eng == 'act':
                cost = 0.22 + n / 1200.0
            elif eng == 'dve':
                cost = 0.12 + n / 960.0
            else:
                cost = 0.15 + n / 480.0
            end = start + cost
            self.efree[eng] = end
        for k in R:
            if end > self.rready.get(k, 0.0):
                self.rready[k] = end
        for k in W:
            self.wready[k] = end
        if self.oplog is not None:
            self.oplog.append((start, end, eng, is_dma, tuple(R), tuple(W)))

    def chan(self, name):
        k = 'C' + name
        if k not in self.sems:
            self.sems[k] = self.es.enter_context(self.nc.semaphore('c_' + name))
            self.cnt[k] = 0
        return k

    def _collect(self, R, W):
        deps = []
        for k in R:
            t = self.lastw.get(k)
            if t is not None:
                deps.append(('raw', t))
            if len(k) == 2 and k[0] == 'B' and k[1].isdigit():
                for t in self.readers.get(k, ()):
                    deps.append(('war', t))
        for k in W:
            t = self.lastw.get(k)
            if t is not None:
                deps.append(('waw', t))
            for t in self.readers.get(k, ()):
                deps.append(('war', t))
        return deps

    def _wait(self, eng, deps, is_dma):
        best = {}
        for kind, (sk, val, prod) in deps:
            if not is_dma and prod == eng:
                if eng == 'pe':
                    continue
                if kind == 'war':
                    continue
            if val > best.get(sk, 0):
                best[sk] = val
        need = []
        for sk, val in best.items():
            if self.waited.get((eng, sk), 0) >= val:
                continue
            need.append((sk, val))
            self.waited[(eng, sk)] = val
            self.nwaits += 1
        attach = need.pop() if (need and self.attach_waits) else None
        for sk, val in need:
            self.eng[eng].wait_ge(self.sems[sk], val)
        return attach

    def _record(self, tok, R, W):
        for k in R:
            self.readers.setdefault(k, []).append(tok)
        for k in W:
            self.lastw[k] = tok
            self.readers[k] = []

    def op(self, eng, fn, R=(), W=(), signal=True):
        att = self._wait(eng, self._collect(R, W), False)
        ins = fn()
        if att is not None:
            ins._wait_ge(self.sems[att[0]], att[1])
        sk = 'E' + eng
        if signal:
            self.cnt[sk] += 1
            ins.then_inc(self.sems[sk], 1)
            tok = (sk, self.cnt[sk], eng)
        else:
            tok = (sk, self.cnt[sk] + 1, eng)
        self._record(tok, R, W)
        self._model(eng, R, W, ins, False)
        return ins

    def dma(self, queue, fn, R, W, chan):
        ck = self.chan(chan)
        att = self._wait(queue, self._collect(R, W), True)
        if att is not None:
            self.eng[queue].wait_ge(self.sems[att[0]], att[1])
        ins = fn()
        self.cnt[ck] += 16
        ins.then_inc(self.sems[ck], 16)
        tok = (ck, self.cnt[ck], 'dma')
        self._record(tok, R, W)
        self._model(queue, R, W, ins, True)
        return ins

    def seal(self, chan, keys):
        ck = self.chan(chan)
        for k in keys:
            self.lastw[k] = (ck, self.cnt[ck], 'dma')

    def barrier(self):
        for e in self.eng:
            for sk, c in self.cnt.items():
                if c > 0 and self.waited.get((e, sk), 0) < c:
                    self.eng[e].wait_ge(self.sems[sk], c)
                    self.waited[(e, sk)] = c
        self.lastw.clear()
        self.readers.clear()

    def finish(self):
        for sk, c in self.cnt.items():
            if c > 0 and self.waited.get(('sp', sk), 0) < c:
                self.nc.sync.wait_ge(self.sems[sk], c)
                self.waited[('sp', sk)] = c


class _Stop(Exception):
    pass


def build_nc(upto=None):
    nc = bass.Bass("TRN2", target_bir_lowering=False)

    def din(name, shape):
        return nc.dram_tensor(name, list(shape), F32, kind="ExternalInput").ap()

    def dout(name, shape):
        return nc.dram_tensor(name, list(shape), F32, kind="ExternalOutput").ap()

    xp = din("xp", [SEQ, D]); xs = din("xs", [64, D]); meta = din("meta", [16, D])
    spool = din("spool", [16, 15, 512]); sC = din("sC", [16, 4, 128, 128])
    sn = din("sn", [64, 128]); sm_ = din("sm", [16, 4])
    norm1 = din("norm1", [D]); w_in = din("w_in", [D, INC]); b_gate = din("b_gate", [8])
    w_pool = din("w_pool", [4, 128, 128]); pool_scale = din("pool_scale", [512])
    head_gain = din("head_gain", [512]); w_out = din("w_out", [D, D]); norm2 = din("norm2", [D])
    w_up = din("w_up", [D, DFF]); w_down = din("w_down", [DFF, D]); norm_f = din("norm_f", [D])

    yp = dout("yp", [SEQ, D]); ys = dout("ys", [64, D])
    pool_p = dout("pool_p", [15, 512]); C_p = dout("C_p", [4, 128, 128]); n_p = dout("n_p", [4, 128])
    m_p = dout("m_p", [1, 4])
    pool_s = dout("pool_s", [16, 15, 512]); C_s = dout("C_s", [16, 4, 128, 128])
    n_s = dout("n_s", [16, 4, 128]); m_s = dout("m_s", [16, 4])
    x2s = nc.dram_tensor("x2s", [NT * 128, D], F32, kind="Internal").ap()
    dbg_out = {}

    with ExitStack() as es:
        E = es.enter_context
        tk = Tk(nc, es)
        if OPLOG is not None:
            tk.oplog = OPLOG

        def sb(es_, name, shape, dt=F32):
            return es_.enter_context(nc.sbuf_tensor(name, list(shape), dt))

        def A(fn, R=(), W=()): return tk.op('act', fn, R, W)
        def V(fn, R=(), W=()): return tk.op('dve', fn, R, W)
        def G(fn, R=(), W=()): return tk.op('pool', fn, R, W)
        def P(fn, R=(), W=(), sig=True): return tk.op('pe', fn, R, W, signal=sig)

        B = [E(nc.psum_tensor("B%d" % i, [128, 512], F32)) for i in range(8)]
        B0b = B[0][:].bitcast(BF16)
        B1b = B[1][:].bitcast(BF16)

        ident_bf = sb(es, "ident_bf", [128, 128], BF16)
        gcol2 = sb(es, "gcol2", [128, 8])
        gfB = sb(es, "gfB", [128, D])
        ARENA_W = 12288
        arena = sb(es, "arena", [128, ARENA_W])
        ms = ExitStack()
        ident_f = sb(ms, "ident_f", [128, 128])
        sel = sb(ms, "sel", [4, 4, 128])
        maskC = sb(ms, "maskC", [128, 128])
        maskS = sb(ms, "maskS", [128, 128])
        maskcols = sb(ms, "maskcols", [128, 17])
        selmask = sb(ms, "selmask", [128, 16, 64], BF16)
        E1 = sb(ms, "E1", [16, 128])
        E2 = sb(ms, "E2", [1, 128])
        zeros4 = sb(ms, "zeros4", [4, 128])
        ones_bf = sb(ms, "ones_bf", [128, 1], BF16)
        gcol1 = sb(ms, "gcol1", [128, 8])
        hgB = sb(ms, "hgB", [128, 512])
        pscol = sb(ms, "pscol", [128, 4])
        bi_col = sb(ms, "bi_col", [4, 1]); bf_col = sb(ms, "bf_col", [4, 1])
        rc_meta = sb(ms, "rc_meta", [128, 4, 16])

        NX = 3
        xt = [sb(ms, "xt%d" % i, [128, D]) for i in range(NX)]
        tk.op('pool', lambda: nc.gpsimd.memset(xt[0][:], 0.0), (), ['xt0'])
        tk.dma('sp', lambda: nc.sync.dma_start(out=xt[0][0:64, :], in_=xs[:, :]), [], ['xt0'], 'x0a')
        tk.dma('sp', lambda: nc.sync.dma_start(out=xt[0][64:80, :], in_=meta[:, :]), ['xt0'], ['xt0'], 'x0')
        def mk_ident(t, key):
            G(lambda: nc.gpsimd.memset(t[:], 1.0), W=[key])
            G(lambda: nc.gpsimd.affine_select(out=t[:], in_=t[:], pattern=[[-1, 128]], compare_op=ALU.is_equal,
                                              fill=0.0, base=0, channel_multiplier=1), R=[key], W=[key])
        mk_ident(ident_bf, 'ident_bf'); mk_ident(ident_f, 'ident_f')
        G(lambda: nc.gpsimd.memset(sel[:], 1.0), W=['sel'])
        G(lambda: nc.gpsimd.affine_select(out=sel[:], in_=sel[:], pattern=[[-1, 4], [0, 128]], compare_op=ALU.is_equal,
                                          fill=0.0, base=0, channel_multiplier=1), R=['sel'], W=['sel'])
        G(lambda: nc.gpsimd.memset(maskC[:], 1.0), W=['maskC'])
        G(lambda: nc.gpsimd.affine_select(out=maskC[:], in_=maskC[:], pattern=[[1, 128]], compare_op=ALU.is_ge,
                                          fill=0.0, base=0, channel_multiplier=-1), R=['maskC'], W=['maskC'])
        G(lambda: nc.gpsimd.memset(selmask[:], 1.0), W=['selmask'])
        G(lambda: nc.gpsimd.affine_select(out=selmask[:], in_=selmask[:], pattern=[[-4, 16], [1, 64]], compare_op=ALU.is_ge,
                                          fill=0.0, base=0, channel_multiplier=0), R=['selmask'], W=['selmask'])
        G(lambda: nc.gpsimd.affine_select(out=selmask[:], in_=selmask[:], pattern=[[4, 16], [-1, 64]], compare_op=ALU.is_ge,
                                          fill=0.0, base=3, channel_multiplier=0), R=['selmask'], W=['selmask'])
        G(lambda: nc.gpsimd.memset(E1[:], 1.0), W=['E1'])
        G(lambda: nc.gpsimd.affine_select(out=E1[:], in_=E1[:], pattern=[[1, 128]], compare_op=ALU.is_ge,
                                          fill=0.0, base=0, channel_multiplier=-4), R=['E1'], W=['E1'])
        G(lambda: nc.gpsimd.affine_select(out=E1[:], in_=E1[:], pattern=[[-1, 128]], compare_op=ALU.is_ge,
                                          fill=0.0, base=3, channel_multiplier=4), R=['E1'], W=['E1'])
        G(lambda: nc.gpsimd.memset(E2[:], 0.0), W=['E2'])
        G(lambda: nc.gpsimd.memset(E2[:, 64:80], 1.0), R=['E2'], W=['E2'])
        G(lambda: nc.gpsimd.memset(zeros4[:], 0.0), W=['zeros4'])
        G(lambda: nc.gpsimd.memset(ones_bf[:], 1.0), W=['ones_bf'])
        for g, w in enumerate(WIN):
            G(lambda: nc.gpsimd.memset(rc_meta[:, g, :], 1.0 / w), R=['rc_meta'], W=['rc_meta'])
            for pos in range(w - 1):
                G(lambda: nc.gpsimd.memset(rc_meta[:, g, pos:pos + 1], 1.0 / (pos + 1)), R=['rc_meta'], W=['rc_meta'])
        P(lambda: nc.tensor.matmul(B[3][:, 0:128], lhsT=E1[:, :], rhs=E1[:, :], start=True, stop=False),
          R=['E1'], W=['B3'], sig=False)
        P(lambda: nc.tensor.matmul(B[3][:, 0:128], lhsT=E2[:, :], rhs=E2[:, :], start=False, stop=True),
          R=['E2'], W=['B3'])
        V(lambda: nc.vector.tensor_tensor(out=maskS[:], in0=B[3][:, 0:128], in1=maskC[:], op=ALU.mult),
          R=['B3', 'maskC'], W=['maskS'])
        P(lambda: nc.tensor.transpose(out=B[3][:, 128:144], in_=E1[:, :], identity=ident_f[0:16, 0:16]),
          R=['E1', 'ident_f'], W=['B3'])
        P(lambda: nc.tensor.transpose(out=B[3][:, 144:145], in_=E2[:, :], identity=ident_f[0:1, 0:1]),
          R=['E2', 'ident_f'], W=['B3'])
        V(lambda: nc.vector.tensor_copy(out=maskcols[:], in_=B[3][:, 128:145]), R=['B3'], W=['maskcols'])

        w_in_sb = sb(ms, "w_in_sb", [128, 8, INC], BF16)
        w_in_v = w_in.rearrange("(k p) c -> p k c", p=128)
        stg = [arena[:, j * INC:(j + 1) * INC] for j in range(4)]
        for k in range(8):
            j = k % 4
            eng_q = 'sp' if k % 2 == 0 else 'act'
            qobj = nc.sync if k % 2 == 0 else nc.scalar
            tk.dma(eng_q, lambda: qobj.dma_start(out=stg[j], in_=w_in_v[:, k, :]), [], ['stg%d' % j], 'stg%d' % j)
            V(lambda: nc.vector.tensor_copy(out=w_in_sb[:, k, :], in_=stg[j]), R=['stg%d' % j], W=['w_in'])
        cl = []
        def cload(t, src, key):
            tk.dma('sp', lambda: nc.sync.dma_start(out=t, in_=src), [], [key], 'const')
            cl.append(key)
        with nc.allow_non_contiguous_dma(reason="small constant loads"):
            cload(gcol1[:], norm1.rearrange("(k p) -> p k", p=128), 'gcol1')
            cload(gcol2[:], norm2.rearrange("(k p) -> p k", p=128), 'gcol2')
            cload(pscol[:], pool_scale.rearrange("(g p) -> p g", p=128), 'pscol')
            cload(bi_col[:], b_gate[0:4].rearrange("(h o) -> h o", o=1), 'bi_col')
            cload(bf_col[:], b_gate[4:8].rearrange("(h o) -> h o", o=1), 'bf_col')
        cload(gfB[:], norm_f.partition_broadcast(128), 'gfB')
        cload(hgB[:], head_gain.partition_broadcast(128), 'hgB')
        tk.seal('const', cl)

        stop = [False]

        def ckpt(name):
            if upto == name:
                stop[0] = True
            return stop[0]

        def x2key(i): return 'x2s%d' % i

        try:
            ckpt('setup')
            with ms:
                w_out_sb = sb(ms, "w_out_sb", [128, 8, D], BF16)
                w_pool_sb = sb(ms, "w_pool_sb", [128, 4, 128], BF16)
                tk.dma('pool', lambda: nc.gpsimd.dma_start(out=w_pool_sb[:], in_=w_pool.rearrange("g c d -> c g d")),
                       [], ['w_pool'], 'w_pool')
                w_out_v = w_out.rearrange("(k p) c -> p k c", p=128)
                for k2 in range(2):
                    tk.dma('pool', lambda: nc.gpsimd.dma_start(out=w_out_sb[:, 4 * k2:4 * k2 + 4, :], in_=w_out_v[:, 4 * k2:4 * k2 + 4, :]),
                           [], ['w_out'], 'w_out')
                tk.seal('w_out', ['w_out'])

                xn = [sb(ms, "xn%d" % i, [128, D], BF16) for i in range(1)]
                xr = [sb(ms, "xr%d" % i, [128, D]) for i in range(2)]
                numS = [sb(ms, "numS%d" % i, [128, 512]) for i in range(2)]
                st = [sb(ms, "st%d" % i, [128, 8]) for i in range(2)]
                xnT = [sb(ms, "xnT%d" % i, [128, 8, 128], BF16) for i in range(2)]
                qT = [sb(ms, "qT%d" % i, [128, 4, 128], BF16) for i in range(2)]
                kT = [sb(ms, "kT%d" % i, [128, 4, 128], BF16) for i in range(2)]
                vt = [sb(ms, "vt%d" % i, [128, 512], BF16) for i in range(2)]
                og = [sb(ms, "og%d" % i, [128, 512]) for i in range(3)]
                utile = [sb(ms, "utile%d" % i, [128, 4, 143]) for i in range(2)]
                Wa = sb(ms, "Wa", [128, 4, 143]); Wb = sb(ms, "Wb", [128, 4, 143])
                zt = [sb(ms, "zt%d" % i, [128, 4, 128], BF16) for i in range(4)]
                mixT = [sb(ms, "mixT%d" % i, [128, 8, 128], BF16) for i in range(2)]
                NR = 9
                rows = [sb(ms, "rows%d" % i, [4, NR, 128]) for i in range(2)]
                cols = [sb(ms, "cols%d" % i, [128, 24]) for i in range(2)]
                Dt = sb(ms, "Dt", [128, 4, 128]); Dm = sb(ms, "Dm", [128, 4, 128])
                interB = sb(ms, "interB", [128, 4, 128])
                ut = Dm[:].rearrange("p h t -> p (h t)")
                spt = [Dt[:].rearrange("p h t -> p (h t)"), interB[:].rearrange("p h t -> p (h t)")]
                Pt = sb(ms, "Pt", [128, 4, 128], BF16)
                qs = sb(ms, "qs", [128, 4, 128], BF16)
                kw = sb(ms, "kw", [128, 4, 128], BF16)
                mixh = sb(ms, "mixh", [128, 512], BF16)
                hj = sb(ms, "hj", [128, 128], BF16)
                Cf = sb(ms, "Cf", [128, 4, 128]); nf = sb(ms, "nf", [128, 4])
                Cb = [sb(ms, "Cb%d" % i, [128, 4, 128], BF16) for i in range(2)]
                nb = [sb(ms, "nb%d" % i, [128, 4], BF16) for i in range(2)]
                usamp = arena[:, 8192:9408].rearrange("p (a b) -> p a b", b=19)
                Was = arena[:, 9408:10624].rearrange("p (a b) -> p a b", b=19)
                Wbs = arena[:, 10624:11840].rearrange("p (a b) -> p a b", b=19)
                umeta = sb(ms, "umeta", [128, 4, 31])
                C0b = arena[:, 0:4096].bitcast(BF16).rearrange("p (a b) -> p a b", b=128)
                C0f = [arena[:, 4096:5120].rearrange("p (a b) -> p a b", b=128), arena[:, 5120:6144].rearrange("p (a b) -> p a b", b=128),
                       arena[:, 8192:9216].rearrange("p (a b) -> p a b", b=128), arena[:, 9216:10240].rearrange("p (a b) -> p a b", b=128)]
                qz = arena[:, 6144:8192].bitcast(BF16).rearrange("p (a b c) -> p a b c", b=4, c=64)
                ktm = sb(ms, "ktm", [128, 4, 128], BF16)
                kwz = [arena[:, 11840:12096].bitcast(BF16).rearrange("p (a b) -> p a b", b=128), sb(ms, "kwz1", [128, 4, 128], BF16)]
                wsm = sb(ms, "wsm", [128, 17, 4])
                m0r = sb(ms, "m0r", [4, 16]); decr = sb(ms, "decr", [4, 16]); msr = sb(ms, "msr", [4, 16])
                decS = sb(ms, "decS", [128, 4, 16])
                n0t = sb(ms, "n0t", [64, 128]); n0f = sb(ms, "n0f", [128, 64]); n0b = sb(ms, "n0b", [128, 64], BF16)
                n1s = sb(ms, "n1s", [128, 4, 16]); n1t = sb(ms, "n1t", [64, 128])
                npt = sb(ms, "npt", [4, 128]); mpr = sb(ms, "mpr", [4, 1])

                G(lambda: nc.gpsimd.memset(Cf[:], 0.0), W=['Cf'])
                G(lambda: nc.gpsimd.memset(nf[:], 0.0), W=['nf'])

                order = [SP] + list(range(16))

                def load_into(buf, key, i, chan):
                    if i == SP:
                        G(lambda: nc.gpsimd.memset(buf[:], 0.0), W=[key])
                        tk.dma('sp', lambda: nc.sync.dma_start(out=buf[0:64, :], in_=xs[:, :]), [], [key], chan + 'a')
                        tk.dma('sp', lambda: nc.sync.dma_start(out=buf[64:80, :], in_=meta[:, :]), [key], [key], chan)
                    else:
                        tk.dma('sp', lambda: nc.sync.dma_start(out=buf[:], in_=xp[128 * i:128 * i + 128, :]), [], [key], chan)

                def load_x(pos_):
                    s = pos_ % NX
                    load_into(xt[s], 'xt%d' % s, order[pos_], 'x%d' % s)

                def rstd_chain(stt, ss_col, out_col, n, keyR, keyW):
                    yield ('act', keyR, keyW)
                    A(lambda: nc.scalar.activation(out=stt[:, out_col:out_col + 1], in_=stt[:, ss_col:ss_col + 1], func=AF.Ln,
                                                   scale=1.0 / n, bias=EPS), R=keyR, W=keyW)
                    yield ('act', keyW, keyW)
                    A(lambda: nc.scalar.activation(out=stt[:, out_col:out_col + 1], in_=stt[:, out_col:out_col + 1], func=AF.Exp,
                                                   scale=-0.5), R=keyW, W=keyW)

                b0_live = [False]
                njunk = [0]

                def junk_mm():
                    if JUNK_EVERY <= 0 or b0_live[0]:
                        return
                    njunk[0] += 1
                    if njunk[0] % JUNK_EVERY:
                        return
                    P(lambda: nc.tensor.matmul(B[0][:, :], lhsT=ident_bf[:], rhs=w_in_sb[:, 0, 0:512], start=True, stop=True),
                      R=['ident_bf', 'w_in'], W=['B0'], sig=False)

                def norm_T(xtile, xkey, slot, gcol, gkey):
                    sk = 'st%d' % slot
                    yield ('act', [xkey], ['xn0', sk])
                    A(lambda: nc.scalar.activation(out=xn[0][:], in_=xtile[:], func=AF.Square, accum_out=st[slot][:, 0:1]),
                      R=[xkey], W=['xn0', sk])
                    yield from rstd_chain(st[slot], 0, 1, D, [sk], [sk])
                    yield ('act', [xkey, sk], ['xn0'])
                    A(lambda: nc.scalar.activation(out=xn[0][:], in_=xtile[:], func=AF.Copy, scale=st[slot][:, 1:2]),
                      R=[xkey, sk], W=['xn0'])
                    b0_live[0] = True
                    for k in range(8):
                        P(lambda: nc.tensor.transpose(out=B0b[:, k * 128:(k + 1) * 128], in_=xn[0][:, k * 128:(k + 1) * 128],
                                                      identity=ident_bf[:]),
                          R=['xn0', 'ident_bf'], W=['B0'], sig=(k == 7))
                    yield ('dve', ['B0', gkey], ['xnT%d' % slot])
                    V(lambda: nc.vector.tensor_tensor(out=xnT[slot][:], in0=B0b.rearrange("p (k t) -> p k t", k=8),
                                                      in1=gcol[:].unsqueeze(2).to_broadcast([128, 8, 128]), op=ALU.mult),
                      R=['B0', gkey], W=['xnT%d' % slot])
                    b0_live[0] = False

                def pool_sums(U, Wa_, Wb_, rpg, L, keyU, keyWa, keyWb):
                    yield ('pool', [keyU], [keyWa])
                    G(lambda: nc.gpsimd.tensor_tensor(out=Wa_[:, :, 1:L], in0=U[:, :, 1:L], in1=U[:, :, 0:L - 1], op=ALU.add),
                      R=[keyU], W=[keyWa])
                    yield ('pool', [keyWa], [keyWb])
                    G(lambda: nc.gpsimd.tensor_tensor(out=Wb_[:, rpg:4 * rpg, 3:L], in0=Wa_[:, rpg:4 * rpg, 3:L],
                                                      in1=Wa_[:, rpg:4 * rpg, 1:L - 2], op=ALU.add), R=[keyWa], W=[keyWb])
                    yield ('pool', [keyWb], [keyWa])
                    G(lambda: nc.gpsimd.tensor_tensor(out=Wa_[:, 2 * rpg:4 * rpg, 7:L], in0=Wb_[:, 2 * rpg:4 * rpg, 7:L],
                                                      in1=Wb_[:, 2 * rpg:4 * rpg, 3:L - 4], op=ALU.add), R=[keyWb], W=[keyWa])
                    yield ('pool', [keyWa], [keyWb])
                    G(lambda: nc.gpsimd.tensor_tensor(out=Wb_[:, 3 * rpg:4 * rpg, 15:L], in0=Wa_[:, 3 * rpg:4 * rpg, 15:L],
                                                      in1=Wa_[:, 3 * rpg:4 * rpg, 7:L - 8], op=ALU.add), R=[keyWa], W=[keyWb])
                    return [Wa_, Wb_, Wa_, Wb_]

                def mixer_tile(i, pos):
                    spc = (i == SP)
                    s2 = pos % 2
                    xs_ = pos % NX
                    xkey = 'xt%d' % xs_
                    rk = 'rows%d' % s2
                    ck = 'cols%d' % s2
                    rw = rows[s2]
                    prev_rw = rows[1 - s2]
                    R_XF, R_T, R_LF, R_F, R_A, R_M, R_AREL, R_MREL, R_WS = range(9)
                    R_MNEG, R_EMN, R_MEND = R_XF, R_T, R_LF

                    yield from norm_T(xt[xs_], xkey, s2, gcol1, 'gcol1')
                    yield 'SPLIT'
                    xk = 'xnT%d' % s2
                    X = xnT[s2]

                    if ckpt('t%da' % pos):
                        return
                    yield None
                    for part, c0 in ((0, 2560), (1, 2564)):
                        for k in range(8):
                            P(lambda: nc.tensor.matmul(B[3][0:4, part * 128:(part + 1) * 128], lhsT=w_in_sb[:, k, c0:c0 + 4],
                                                       rhs=X[:, k, :], start=(k == 0), stop=(k == 7)),
                              R=[xk, 'w_in'], W=['B3'], sig=(k == 7))

                    if ckpt('t%db' % pos):
                        return
                    yield None
                    if spc:
                        yield ('pool', (), [rk])
                        G(lambda: nc.gpsimd.memset(rw[:], 0.0), W=[rk])
                    yield ('dve', ['B3', 'bf_col'], [rk])
                    V(lambda: nc.vector.tensor_scalar(out=rw[:, R_XF, :], in0=B[3][0:4, 128:256], scalar1=bf_col[:, 0:1],
                                                      scalar2=None, op0=ALU.add), R=['B3', 'bf_col'], W=[rk])
                    yield ('dve', [rk], [rk])
                    V(lambda: nc.vector.scalar_tensor_tensor(out=rw[:, R_T, :], in0=rw[:, R_XF, :], scalar=-1.0, in1=rw[:, R_XF, :],
                                                             op0=ALU.mult, op1=ALU.max), R=[rk], W=[rk])
                    yield ('act', [rk], [rk])
                    A(lambda: nc.scalar.activation(out=rw[:, R_T, :], in_=rw[:, R_T, :], func=AF.Exp, scale=-1.0), R=[rk], W=[rk])
                    yield ('act', [rk], [rk])
                    A(lambda: nc.scalar.activation(out=rw[:, R_T, :], in_=rw[:, R_T, :], func=AF.Ln, bias=1.0), R=[rk], W=[rk])
                    yield ('dve', [rk], [rk])
                    V(lambda: nc.vector.scalar_tensor_tensor(out=rw[:, R_LF, :], in0=rw[:, R_XF, :], scalar=0.0, in1=rw[:, R_T, :],
                                                             op0=ALU.min, op1=ALU.subtract), R=[rk], W=[rk])
                    if not spc:
                        if i == 0:
                            Fc = prev_rw[:, R_F, 79:80]; Mc = prev_rw[:, R_M, 79:80]
                        else:
                            Fc = prev_rw[:, R_F, 127:128]; Mc = prev_rw[:, R_M, 127:128]
                        pk = 'rows%d' % (1 - s2)
                        yield ('dve', [rk, pk, 'zeros4'], [rk])
                        V(lambda: nc.vector.tensor_tensor_scan(out=rw[:, R_F, :], data0=rw[:, R_LF, :], data1=zeros4[:, :],
                                                               initial=Fc, op0=ALU.add, op1=ALU.add), R=[rk, pk, 'zeros4'], W=[rk])
                        yield ('dve', ['B3', 'bi_col', rk], [rk])
                        V(lambda: nc.vector.scalar_tensor_tensor(out=rw[:, R_A, :], in0=B[3][0:4, 0:128], scalar=bi_col[:, 0:1],
                                                                 in1=rw[:, R_F, :], op0=ALU.add, op1=ALU.subtract),
                          R=['B3', 'bi_col', rk], W=[rk])
                        yield ('dve', [rk, pk], [rk])
                        V(lambda: nc.vector.tensor_tensor_scan(out=rw[:, R_M, :], data0=rw[:, R_A, :], data1=rw[:, R_A, :],
                                                               initial=Mc, op0=ALU.max, op1=ALU.max), R=[rk, pk], W=[rk])
                        yield ('dve', [rk, pk], [rk])
                        V(lambda: nc.vector.tensor_scalar(out=rw[:, R_AREL, :], in0=rw[:, R_A, :], scalar1=Mc, scalar2=None,
                                                          op0=ALU.subtract), R=[rk, pk], W=[rk])
                        yield ('dve', [rk, pk], [rk])
                        V(lambda: nc.vector.tensor_scalar(out=rw[:, R_MREL, :], in0=rw[:, R_M, :], scalar1=Mc, scalar2=None,
                                                          op0=ALU.subtract), R=[rk, pk], W=[rk])
                        yield ('dve', [rk], [rk])
                        V(lambda: nc.vector.tensor_scalar(out=rw[:, R_WS, :], in0=rw[:, R_AREL, :], scalar1=rw[:, R_MREL, 127:128],
                                                          scalar2=None, op0=ALU.subtract), R=[rk], W=[rk])
                    else:
                        yield ('dve', [rk, 'zeros4'], [rk])
                        V(lambda: nc.vector.tensor_tensor_scan(out=rw[:, R_F, 64:80], data0=rw[:, R_LF, 64:80], data1=zeros4[:, 0:16],
                                                               initial=0.0, op0=ALU.add, op1=ALU.add), R=[rk, 'zeros4'], W=[rk])
                        lf3 = rw[:, R_LF, 0:64].rearrange("h (j t) -> h j t", t=4)
                        F3 = rw[:, R_F, 0:64].rearrange("h (j t) -> h j t", t=4)
                        yield ('dve', [rk], [rk])
                        V(lambda: nc.vector.tensor_copy(out=F3[:, :, 0], in_=lf3[:, :, 0]), R=[rk], W=[rk])
                        for t in range(1, 4):
                            yield ('dve', [rk], [rk])
                            V(lambda: nc.vector.tensor_tensor(out=F3[:, :, t], in0=F3[:, :, t - 1], in1=lf3[:, :, t], op=ALU.add),
                              R=[rk], W=[rk])
                        yield ('dve', ['B3', 'bi_col', rk], [rk])
                        V(lambda: nc.vector.scalar_tensor_tensor(out=rw[:, R_A, 0:80], in0=B[3][0:4, 0:80], scalar=bi_col[:, 0:1],
                                                                 in1=rw[:, R_F, 0:80], op0=ALU.add, op1=ALU.subtract),
                          R=['B3', 'bi_col', rk], W=[rk])
                        yield ('dve', [rk], [rk])
                        V(lambda: nc.vector.tensor_tensor_scan(out=rw[:, R_M, 64:80], data0=rw[:, R_A, 64:80], data1=rw[:, R_A, 64:80],
                                                               initial=0.0, op0=ALU.max, op1=ALU.max), R=[rk], W=[rk])
                        A3 = rw[:, R_A, 0:64].rearrange("h (j t) -> h j t", t=4)
                        M3 = rw[:, R_M, 0:64].rearrange("h (j t) -> h j t", t=4)
                        yield ('dve', [rk, 'm0r'], [rk])
                        V(lambda: nc.vector.tensor_tensor(out=M3[:, :, 0], in0=m0r[:, :], in1=A3[:, :, 0], op=ALU.max),
                          R=[rk, 'm0r'], W=[rk])
                        for t in range(1, 4):
                            yield ('dve', [rk], [rk])
                            V(lambda: nc.vector.tensor_tensor(out=M3[:, :, t], in0=M3[:, :, t - 1], in1=A3[:, :, t], op=ALU.max),
                              R=[rk], W=[rk])
                        m0b = m0r[:, :].unsqueeze(2).to_broadcast([4, 16, 4])
                        AR3 = rw[:, R_AREL, 0:64].rearrange("h (j t) -> h j t", t=4)
                        MR3 = rw[:, R_MREL, 0:64].rearrange("h (j t) -> h j t", t=4)
                        ME3 = rw[:, R_MEND, 0:64].rearrange("h (j t) -> h j t", t=4)
                        yield ('dve', [rk, 'm0r'], [rk])
                        V(lambda: nc.vector.tensor_tensor(out=AR3, in0=A3, in1=m0b, op=ALU.subtract), R=[rk, 'm0r'], W=[rk])
                        yield ('dve', [rk, 'm0r'], [rk])
                        V(lambda: nc.vector.tensor_tensor(out=MR3, in0=M3, in1=m0b, op=ALU.subtract), R=[rk, 'm0r'], W=[rk])
                        yield ('dve', [rk], [rk])
                        V(lambda: nc.vector.tensor_copy(out=rw[:, R_AREL, 64:80], in_=rw[:, R_A, 64:80]), R=[rk], W=[rk])
                        yield ('dve', [rk], [rk])
                        V(lambda: nc.vector.tensor_copy(out=rw[:, R_MREL, 64:80], in_=rw[:, R_M, 64:80]), R=[rk], W=[rk])
                        yield ('dve', [rk], [rk])
                        V(lambda: nc.vector.tensor_copy(out=ME3, in_=MR3[:, :, 3:4].to_broadcast([4, 16, 4])), R=[rk], W=[rk])
                        yield ('dve', [rk], [rk])
                        V(lambda: nc.vector.tensor_copy(out=rw[:, R_MEND, 64:80], in_=rw[:, R_MREL, 79:80].to_broadcast([4, 16])),
                          R=[rk], W=[rk])
                        yield ('dve', [rk], [rk])
                        V(lambda: nc.vector.tensor_tensor(out=rw[:, R_WS, :], in0=rw[:, R_AREL, :], in1=rw[:, R_MEND, :],
                                                          op=ALU.subtract), R=[rk], W=[rk])
                        yield ('act', [rk], ['decr'])
                        A(lambda: nc.scalar.activation(out=decr[:, :], in_=MR3[:, :, 3], func=AF.Exp, scale=-1.0),
                          R=[rk], W=['decr'])
                    yield ('dve', [rk], [rk])
                    V(lambda: nc.vector.scalar_tensor_tensor(out=rw[:, R_MNEG, :], in0=rw[:, R_F, :], scalar=-1.0, in1=rw[:, R_M, :],
                                                             op0=ALU.mult, op1=ALU.subtract), R=[rk], W=[rk])
                    yield ('act', [rk], [rk])
                    A(lambda: nc.scalar.activation(out=rw[:, R_EMN, :], in_=rw[:, R_MNEG, :], func=AF.Exp), R=[rk], W=[rk])
                    yield ('act', [rk], [rk])
                    A(lambda: nc.scalar.activation(out=rw[:, R_WS, :], in_=rw[:, R_WS, :], func=AF.Exp), R=[rk], W=[rk])
                    if spc:
                        MN3 = rw[:, R_MNEG, 0:64].rearrange("h (j t) -> h j t", t=4)
                        yield ('dve', [rk], ['msr'])
                        V(lambda: nc.vector.tensor_scalar(out=msr[:, :], in0=MN3[:, :, 3], scalar1=-1.0, scalar2=None, op0=ALU.mult),
                          R=[rk], W=['msr'])
                        with nc.allow_non_contiguous_dma(reason="tiny"):
                            yield ('sp', ['msr'], ['m_s'])
                            tk.dma('sp', lambda: nc.sync.dma_start(out=m_s.rearrange("j h -> h j"), in_=msr[:, :]), ['msr'], ['m_s'], 'm_s')
                    if i == 15:
                        yield ('dve', [rk], ['mpr'])
                        V(lambda: nc.vector.tensor_scalar(out=mpr[:, :], in0=rw[:, R_MNEG, 127:128], scalar1=-1.0, scalar2=None,
                                                          op0=ALU.mult), R=[rk], W=['mpr'])
                        with nc.allow_non_contiguous_dma(reason="tiny"):
                            yield ('sp', ['mpr'], ['m_p'])
                            tk.dma('sp', lambda: nc.sync.dma_start(out=m_p.rearrange("o h -> h o"), in_=mpr[:, :]), ['mpr'], ['m_p'], 'm_p')
                    if ckpt('t%dc' % pos):
                        return
                    yield None
                    def fm_round(c0):
                        for c in range(4):
                            for k in range(8):
                                P(lambda: nc.tensor.matmul(B[2][:, c * 128:(c + 1) * 128], lhsT=w_in_sb[:, k, c0 + c * 128:c0 + (c + 1) * 128],
                                                           rhs=X[:, k, :], start=(k == 0), stop=(k == 7)),
                                  R=[xk, 'w_in'], W=['B2'], sig=(c == 3 and k == 7))
                        yield None
                    B2v = B[2][:].rearrange("p (g t) -> p g t", g=4)
                    uk = 'utile%d' % s2
                    so = pos % 3
                    ok_ = 'og%d' % so

                    def tm_round(c0, bank):
                        for k in range(8):
                            P(lambda: nc.tensor.matmul(B[bank][:, :], lhsT=X[:, k, :], rhs=w_in_sb[:, k, c0:c0 + 512],
                                                       start=(k == 0), stop=(k == 7)), R=[xk, 'w_in'], W=['B%d' % bank], sig=(k == 7))

                    def evac_u():
                        if not spc:
                            yield ('act', ['B2'], [uk])
                            A(lambda: nc.scalar.activation(out=utile[s2][:, :, 15:143], in_=B2v, func=AF.Copy), R=['B2'], W=[uk])
                        else:
                            for g in range(4):
                                yield ('act', ['B2'], ['usamp'])
                                A(lambda: nc.scalar.activation(out=usamp[:, 16 * g:16 * g + 16, 15:19],
                                                               in_=B[2][:, g * 128:g * 128 + 64].rearrange("p (j t) -> p j t", t=4),
                                                               func=AF.Copy), R=['B2'], W=['usamp'])
                            yield ('pool', (), ['umeta'])
                            G(lambda: nc.gpsimd.memset(umeta[:, :, 0:15], 0.0), W=['umeta'])
                            yield ('act', ['B2'], ['umeta'])
                            A(lambda: nc.scalar.activation(out=umeta[:, :, 15:31], in_=B2v[:, :, 64:80], func=AF.Copy), R=['B2'], W=['umeta'])

                    yield from fm_round(0)
                    tm_round(1536, 4)
                    yield from evac_u()
                    yield from fm_round(512)
                    yield ('act', ['B4'], ['vt%d' % s2])
                    A(lambda: nc.scalar.activation(out=vt[s2][:], in_=B[4][:, :], func=AF.Copy), R=['B4'], W=['vt%d' % s2])
                    tm_round(2048, 5)
                    yield ('act', ['B2'], ['qT%d' % s2])
                    A(lambda: nc.scalar.activation(out=qT[s2][:], in_=B2v, func=AF.Copy, scale=float(128 ** -0.5)),
                      R=['B2'], W=['qT%d' % s2])
                    yield from fm_round(1024)
                    yield ('act', ['B5'], [ok_])
                    A(lambda: nc.scalar.activation(out=og[so][:], in_=B[5][:, :], func=AF.Exp, scale=-1.0), R=['B5'], W=[ok_])
                    yield ('act', [ok_], [ok_])
                    A(lambda: nc.scalar.activation(out=og[so][:], in_=og[so][:], func=AF.Ln, bias=1.0), R=[ok_], W=[ok_])
                    yield ('dve', ['B2'], ['kT%d' % s2])
                    V(lambda: nc.vector.tensor_copy(out=kT[s2][:], in_=B2v), R=['B2'], W=['kT%d' % s2])
                    yield ('act', [ok_], [ok_])
                    A(lambda: nc.scalar.activation(out=og[so][:], in_=og[so][:], func=AF.Exp, scale=-1.0), R=[ok_], W=[ok_])
                    yield ('pool', [ok_, 'hgB'], [ok_])
                    G(lambda: nc.gpsimd.tensor_tensor(out=og[so][:], in0=og[so][:], in1=hgB[:], op=ALU.mult), R=[ok_, 'hgB'], W=[ok_])
                    if i == 15 or spc:
                        for k in range(8):
                            P(lambda: nc.tensor.matmul(B[4][:, :], lhsT=X[:, k, :], rhs=w_in_sb[:, k, 0:512],
                                                       start=(k == 0), stop=(k == 7)), R=[xk, 'w_in'], W=['B4'], sig=(k == 7))
                        yield ('act', ['B4'], ['Dm'])
                        A(lambda: nc.scalar.activation(out=ut, in_=B[4][:, :], func=AF.Copy), R=['B4'], W=['Dm'])
                        if spc:
                            yield ('sp', ['Dm'], ['pool_s_b'])
                            tk.dma('sp', lambda: nc.sync.dma_start(out=pool_s[:, 11:15, :], in_=ut[0:64, :]), ['Dm'], ['pool_s_b'], 'pool_s_b')
                        else:
                            yield ('sp', ['Dm'], ['pool_p'])
                            tk.dma('sp', lambda: nc.sync.dma_start(out=pool_p[:, :], in_=ut[113:128, :]), ['Dm'], ['pool_p'], 'pool_p')

                    if ckpt('t%de' % pos):
                        return
                    yield None
                    zk = 'zt%d' % (pos % 4)
                    Z = zt[pos % 4]
                    if not spc:
                        U = utile[s2]
                        if i == 0:
                            yield ('pool', ['umeta'], [uk])
                            G(lambda: nc.gpsimd.tensor_copy(out=U[:, :, 0:15], in_=umeta[:, :, 16:31]), R=['umeta'], W=[uk])
                        else:
                            yield ('pool', ['utile%d' % (1 - s2)], [uk])
                            G(lambda: nc.gpsimd.tensor_copy(out=U[:, :, 0:15], in_=utile[1 - s2][:, :, 128:143]),
                              R=['utile%d' % (1 - s2)], W=[uk])
                        Ws = yield from pool_sums(U, Wa, Wb, 1, 143, uk, 'Wa', 'Wb')
                        for g in range(4):
                            yield ('dve', ['Wa', 'Wb', uk], [zk])
                            V(lambda: nc.vector.scalar_tensor_tensor(out=Z[:, g, :], in0=Ws[g][:, g, 15:143], scalar=1.0 / WIN[g],
                                                                     in1=U[:, g, 15:143], op0=ALU.mult, op1=ALU.subtract),
                              R=['Wa', 'Wb', uk], W=[zk])
                    else:
                        for g in range(4):
                            P(lambda: nc.tensor.transpose(out=B[4 + g // 2][:, (g % 2) * 240:(g % 2) * 240 + 128],
                                                          in_=spt[0][:, g * 128:(g + 1) * 128], identity=ident_f[:]),
                              R=['Dt', 'ident_f'], W=['B%d' % (4 + g // 2)], sig=False)
                            P(lambda: nc.tensor.transpose(out=B[4 + g // 2][:, (g % 2) * 240 + 128:(g % 2) * 240 + 240],
                                                          in_=spt[1][0:112, g * 128:(g + 1) * 128], identity=ident_f[0:112, 0:112]),
                              R=['interB', 'ident_f'], W=['B%d' % (4 + g // 2)], sig=True)
                        for g in range(4):
                            yield ('dve', ['B%d' % (4 + g // 2)], ['usamp'])
                            V(lambda: nc.vector.tensor_copy(out=usamp[:, 16 * g:16 * g + 16, 0:15],
                                                            in_=B[4 + g // 2][:, (g % 2) * 240:(g % 2) * 240 + 240].rearrange("p (j i) -> p j i", i=15)),
                              R=['B%d' % (4 + g // 2)], W=['usamp'])
                        yield ('pool', (), [zk])
                        G(lambda: nc.gpsimd.memset(Z[:], 0.0), W=[zk])
                        Ws = yield from pool_sums(usamp, Was, Wbs, 16, 19, 'usamp', 'Was', 'Wbs')
                        for g in range(4):
                            yield ('pool', ['Was', 'Wbs'], ['Was', 'Wbs'])
                            G(lambda: nc.gpsimd.tensor_scalar(out=Ws[g][:, 16 * g:16 * g + 16, 15:19], in0=Ws[g][:, 16 * g:16 * g + 16, 15:19],
                                                              scalar1=1.0 / WIN[g], scalar2=None, op0=ALU.mult), R=['Was', 'Wbs'], W=['Was', 'Wbs'])
                            yield ('pool', ['Was', 'Wbs', 'usamp'], [zk])
                            G(lambda: nc.gpsimd.tensor_tensor(out=Z[:, g, 0:64].rearrange("p (j t) -> p j t", t=4),
                                                              in0=Ws[g][:, 16 * g:16 * g + 16, 15:19],
                                                              in1=usamp[:, 16 * g:16 * g + 16, 15:19], op=ALU.subtract),
                              R=['Was', 'Wbs', 'usamp'], W=[zk])
                        Wm = yield from pool_sums(umeta, Wa[:, :, 0:31], Wb[:, :, 0:31], 1, 31, 'umeta', 'Wa', 'Wb')
                        for g in range(4):
                            yield ('pool', ['Wa', 'Wb', 'rc_meta'], ['Wa', 'Wb'])
                            G(lambda: nc.gpsimd.tensor_tensor(out=Wm[g][:, g, 15:31], in0=Wm[g][:, g, 15:31], in1=rc_meta[:, g, :], op=ALU.mult),
                              R=['Wa', 'Wb', 'rc_meta'], W=['Wa', 'Wb'])
                            yield ('pool', ['Wa', 'Wb', 'umeta'], [zk])
                            G(lambda: nc.gpsimd.tensor_tensor(out=Z[:, g, 64:80], in0=Wm[g][:, g, 15:31], in1=umeta[:, g, 15:31], op=ALU.subtract),
                              R=['Wa', 'Wb', 'umeta'], W=[zk])
                    mk = 'mixT%d' % s2
                    if ckpt('t%df' % pos):
                        return
                    yield 'SPLIT'
                    for n_, r_ in enumerate((R_AREL, R_EMN, R_WS)):
                        P(lambda: nc.tensor.transpose(out=B[3][:, 256 + 4 * n_:260 + 4 * n_], in_=rw[:, r_, :], identity=ident_f[0:4, 0:4]),
                          R=[rk, 'ident_f'], W=['B3'], sig=(n_ == 2))
                    cl_ = cols[s2]
                    yield ('dve', ['B3'], [ck])
                    V(lambda: nc.vector.tensor_copy(out=cl_[:, 0:12], in_=B[3][:, 256:268]), R=['B3'], W=[ck])
                    for h in range(4):
                        P(lambda: nc.tensor.matmul(B[6][:, h * 128:(h + 1) * 128], lhsT=sel[:, h, :], rhs=rw[:, R_MREL, :],
                                                   start=True, stop=True), R=[rk, 'sel'], W=['B6'], sig=(h == 3))

                    B6v = B[6][:].rearrange("p (h t) -> p h t", h=4)
                    B7v = B[7][:].rearrange("p (h t) -> p h t", h=4)
                    for h in range(4):
                        P(lambda: nc.tensor.matmul(B[7][:, h * 128:(h + 1) * 128], lhsT=kT[s2][:, h, :], rhs=qT[s2][:, h, :], start=True, stop=True),
                          R=['kT%d' % s2, 'qT%d' % s2], W=['B7'], sig=(h == 3))
                    for h in range(4):
                        yield ('act', ['B6', ck], ['Dt'])
                        A(lambda: nc.scalar.activation(out=Dt[:, h, :], in_=B[6][:, h * 128:(h + 1) * 128], func=AF.Exp, scale=-1.0,
                                                       bias=cl_[:, h:h + 1]), R=['B6', ck], W=['Dt'])
                    yield ('act', ['B6'], ['interB'])
                    A(lambda: nc.scalar.activation(out=interB[:], in_=B6v, func=AF.Exp, scale=-1.0), R=['B6'], W=['interB'])
                    msk = maskS if spc else maskC
                    yield ('pool', ['Dt', 'maskS', 'maskC'], ['Dm'])
                    G(lambda: nc.gpsimd.tensor_tensor(out=Dm[:], in0=Dt[:], in1=msk[:].unsqueeze(1).to_broadcast([128, 4, 128]), op=ALU.mult),
                      R=['Dt', 'maskS', 'maskC'], W=['Dm'])
                    yield ('dve', ['B7', 'Dm'], ['Pt'])
                    V(lambda: nc.vector.tensor_tensor(out=Pt[:], in0=B7v, in1=Dm[:], op=ALU.mult), R=['B7', 'Dm'], W=['Pt'])
                    yield ('dve', ['qT%d' % s2, 'interB'], ['qs'])
                    V(lambda: nc.vector.tensor_tensor(out=qs[:], in0=qT[s2][:], in1=interB[:], op=ALU.mult), R=['qT%d' % s2, 'interB'], W=['qs'])
                    cb = Cb[s2]; nbb = nb[s2]
                    cbk = 'Cb%d' % s2
                    if spc:
                        for j in range(16):
                            yield ('pool', ['qs', 'selmask'], ['qz'])
                            G(lambda: nc.gpsimd.tensor_tensor(out=qz[:, j, :, :], in0=qs[:, :, 0:64],
                                                              in1=selmask[:, j, :].unsqueeze(1).to_broadcast([128, 4, 64]), op=ALU.mult),
                              R=['qs', 'selmask'], W=['qz'])
                        P(lambda: nc.tensor.transpose(out=B[3][:, 300:364], in_=n0t[:, :], identity=ident_f[0:64, 0:64]),
                          R=['n0t', 'ident_f'], W=['B3'])
                        yield ('dve', ['B3'], ['n0f'])
                        V(lambda: nc.vector.tensor_copy(out=n0f[:], in_=B[3][:, 300:364]), R=['B3'], W=['n0f'])
                        yield ('act', ['B3'], ['n0b'])
                        A(lambda: nc.scalar.activation(out=n0b[:], in_=B[3][:, 300:364], func=AF.Copy), R=['B3'], W=['n0b'])
                    if ckpt('t%dg' % pos):
                        return
                    yield None
                    for h in range(4):
                        hc = slice(h * 128, (h + 1) * 128)
                        if not spc:
                            P(lambda: nc.tensor.matmul(B[6][:, hc], lhsT=Pt[:, h, :], rhs=vt[s2][:, hc], start=True, stop=False),
                              R=['Pt', 'vt%d' % s2], W=['B6'], sig=False)
                            P(lambda: nc.tensor.matmul(B[6][:, hc], lhsT=qs[:, h, :], rhs=cb[:, h, :], start=False, stop=True),
                              R=['qs', cbk], W=['B6'], sig=True)
                            P(lambda: nc.tensor.matmul(B[3][:, 272 + h:273 + h], lhsT=Pt[:, h, :], rhs=ones_bf[:, 0:1], start=True, stop=False),
                              R=['Pt', 'ones_bf'], W=['B3'], sig=False)
                            P(lambda: nc.tensor.matmul(B[3][:, 272 + h:273 + h], lhsT=qs[:, h, :], rhs=nbb[:, h:h + 1], start=False, stop=True),
                              R=['qs', cbk], W=['B3'], sig=True)
                        else:
                            P(lambda: nc.tensor.matmul(B[6][64:128, hc], lhsT=Pt[:, h, 64:128], rhs=vt[s2][:, hc], start=True, stop=True),
                              R=['Pt', 'vt%d' % s2], W=['B6'], sig=False)
                            P(lambda: nc.tensor.matmul(B[3][64:128, 272 + h:273 + h], lhsT=Pt[:, h, 64:128], rhs=ones_bf[:, 0:1], start=True, stop=True),
                              R=['Pt', 'ones_bf'], W=['B3'], sig=False)
                            for j in range(16):
                                P(lambda: nc.tensor.matmul(B[6][0:64, hc], lhsT=qz[:, j, h, :], rhs=C0b[:, j * 4 + h, :],
                                                           start=(j == 0), stop=False), R=['qz', 'C0b'], W=['B6'], sig=False)
                            P(lambda: nc.tensor.matmul(B[6][0:64, hc], lhsT=Pt[:, h, 0:64], rhs=vt[s2][:, hc], start=False, stop=True),
                              R=['Pt', 'vt%d' % s2], W=['B6'], sig=True)
                            for j in range(16):
                                P(lambda: nc.tensor.matmul(B[3][0:64, 272 + h:273 + h], lhsT=qz[:, j, h, :], rhs=n0b[:, j * 4 + h:j * 4 + h + 1],
                                                           start=(j == 0), stop=False), R=['qz', 'n0b'], W=['B3'], sig=False)
                            P(lambda: nc.tensor.matmul(B[3][0:64, 272 + h:273 + h], lhsT=Pt[:, h, 0:64], rhs=ones_bf[:, 0:1], start=False, stop=True),
                              R=['Pt', 'ones_bf'], W=['B3'], sig=True)
                    if ckpt('t%dh' % pos):
                        return
                    yield None
                    nS = numS[s2]
                    nSk = 'numS%d' % s2
                    yield ('act', ['B6'], [nSk])
                    A(lambda: nc.scalar.activation(out=nS[:], in_=B[6][:, :], func=AF.Copy), R=['B6'], W=[nSk])
                    c_ = cl_
                    yield ('dve', ['B3'], [ck])
                    V(lambda: nc.vector.tensor_copy(out=c_[:, 12:16], in_=B[3][:, 272:276]), R=['B3'], W=[ck])
                    if ckpt('t%di' % pos):
                        return
                    yield None
                    for h in range(4):
                        P(lambda: nc.tensor.transpose(out=B1b[:, 512 + h * 128:512 + (h + 1) * 128], in_=kT[s2][:, h, :], identity=ident_bf[:]),
                          R=['kT%d' % s2, 'ident_bf'], W=['B1'], sig=(h == 3))
                    if ckpt('t%di1' % pos):
                        return
                    yield None
                    B1k = B1b[:, 512:1024].rearrange("p (h t) -> p h t", h=4)
                    if spc:
                        for h in range(4):
                            yield ('dve', ['maskcols', ck], ['wsm'])
                            V(lambda: nc.vector.tensor_scalar(out=wsm[:, :, h], in0=maskcols[:, :], scalar1=c_[:, 8 + h:9 + h], scalar2=None,
                                                              op0=ALU.mult), R=['maskcols', ck], W=['wsm'])
                        yield ('act', ['B1'], ['ktm'])
                        A(lambda: nc.scalar.activation(out=ktm[:], in_=B1k, func=AF.Copy), R=['B1'], W=['ktm'])
                        wsx = wsm[:, 16, :]
                        wk = ['wsm']
                    else:
                        wsx = c_[:, 8:12]
                        wk = [ck]
                    if ckpt('t%di2' % pos):
                        return
                    yield None
                    yield ('dve', ['B1'] + wk, ['kw'])
                    V(lambda: nc.vector.tensor_tensor(out=kw[:], in0=B1k, in1=wsx.unsqueeze(2).to_broadcast([128, 4, 128]), op=ALU.mult),
                      R=['B1'] + wk, W=['kw'])
                    if ckpt('t%di3' % pos):
                        return
                    yield None
                    for h in range(4):
                        P(lambda: nc.tensor.matmul(B[7][:, h * 128:(h + 1) * 128], lhsT=kw[:, h, :], rhs=vt[s2][:, h * 128:(h + 1) * 128],
                                                   start=True, stop=True), R=['kw', 'vt%d' % s2], W=['B7'], sig=False)
                        P(lambda: nc.tensor.matmul(B[3][:, 280 + h:281 + h], lhsT=kw[:, h, :], rhs=ones_bf[:, 0:1], start=True, stop=True),
                          R=['kw', 'ones_bf'], W=['B3'], sig=(h == 3))
                    if ckpt('t%di4' % pos):
                        return
                    yield None
                    for h in range(4):
                        yield ('dve', ['Cf', 'interB', 'B7'], ['Cf'])
                        V(lambda: nc.vector.scalar_tensor_tensor(out=Cf[:, h, :], in0=Cf[:, h, :], scalar=interB[:, h, 127:128],
                                                                 in1=B[7][:, h * 128:(h + 1) * 128], op0=ALU.mult, op1=ALU.add),
                          R=['Cf', 'interB', 'B7'], W=['Cf'])
                    if ckpt('t%di5' % pos):
                        return
                    yield None
                    yield ('dve', ['nf', 'interB'], ['nf'])
                    V(lambda: nc.vector.tensor_tensor(out=nf[:], in0=nf[:], in1=interB[:, :, 127], op=ALU.mult), R=['nf', 'interB'], W=['nf'])
                    yield ('dve', ['nf', 'B3'], ['nf'])
                    V(lambda: nc.vector.tensor_tensor(out=nf[:], in0=nf[:], in1=B[3][:, 280:284], op=ALU.add), R=['nf', 'B3'], W=['nf'])
                    if ckpt('t%di6' % pos):
                        return
                    yield None
                    ncb = Cb[1 - s2]; nnb = nb[1 - s2]
                    yield ('act', ['Cf'], ['Cb%d' % (1 - s2)])
                    A(lambda: nc.scalar.activation(out=ncb[:], in_=Cf[:], func=AF.Copy), R=['Cf'], W=['Cb%d' % (1 - s2)])
                    yield ('act', ['nf'], ['Cb%d' % (1 - s2)])
                    A(lambda: nc.scalar.activation(out=nnb[:], in_=nf[:], func=AF.Copy), R=['nf'], W=['Cb%d' % (1 - s2)])
                    if i == 15:
                        yield ('sp', ['Cf'], ['C_p'])
                        tk.dma('sp', lambda: nc.sync.dma_start(out=C_p.rearrange("h k v -> k h v"), in_=Cf[:]), ['Cf'], ['C_p'], 'C_p')
                        P(lambda: nc.tensor.transpose(out=B[3][0:4, 384:512], in_=nf[:, :], identity=ident_f[:]), R=['nf', 'ident_f'], W=['B3'])
                        yield ('dve', ['B3'], ['npt'])
                        V(lambda: nc.vector.tensor_copy(out=npt[:], in_=B[3][0:4, 384:512]), R=['B3'], W=['npt'])
                        yield ('sp', ['npt'], ['n_p'])
                        tk.dma('sp', lambda: nc.sync.dma_start(out=n_p[:, :], in_=npt[:]), ['npt'], ['n_p'], 'n_p')
                    if ckpt('t%dj' % pos):
                        return
                    yield None
                    if spc:
                        for h in range(4):
                            P(lambda: nc.tensor.matmul(B[3][:, 400 + h * 16:416 + h * 16], lhsT=sel[:, h, :], rhs=decr[:, :], start=True, stop=True),
                              R=['sel', 'decr'], W=['B3'], sig=(h == 3))
                        yield ('dve', ['B3'], ['decS'])
                        V(lambda: nc.vector.tensor_copy(out=decS[:], in_=B[3][:, 400:464].rearrange("p (h j) -> p h j", h=4)), R=['B3'], W=['decS'])
                        sCv = sC.rearrange("j h k v -> k (j h) v")
                        Csv = C_s.rearrange("j h k v -> k (j h) v")
                        def ld_eighth(e_):
                            bf_ = C0f[e_ % 4]
                            bk_ = 'C0f%d' % (e_ % 4)
                            wk_ = [bk_] + (['usamp', 'Was', 'Wbs'] if (e_ % 4) >= 2 else [])
                            tk.dma('sp', lambda: nc.sync.dma_start(out=bf_[:], in_=sCv[:, 8 * e_:8 * e_ + 8, :]), [], wk_, 'ld' + bk_)
                        yield ('sp', [], ['C0f2', 'usamp', 'Was', 'Wbs'])
                        ld_eighth(2)
                        for ei in range(8):
                            cf = C0f[ei % 4]
                            cfk = 'C0f%d' % (ei % 4)
                            if ei + 3 < 8:
                                yield ('sp', [], ['C0f%d' % ((ei + 3) % 4)])
                                ld_eighth(ei + 3)
                            for jj in range(2):
                                j = 2 * ei + jj
                                kz = kwz[j % 2]
                                kzk = 'kwz%d' % (j % 2)
                                yield ('pool', ['ktm', 'wsm'], [kzk])
                                G(lambda: nc.gpsimd.tensor_tensor(out=kz[:], in0=ktm[:], in1=wsm[:, j, :].unsqueeze(2).to_broadcast([128, 4, 128]),
                                                                  op=ALU.mult), R=['ktm', 'wsm'], W=[kzk])
                                for h in range(4):
                                    P(lambda: nc.tensor.matmul(B[7][:, h * 128:(h + 1) * 128], lhsT=kz[:, h, :], rhs=vt[s2][:, h * 128:(h + 1) * 128],
                                                               start=True, stop=True), R=[kzk, 'vt%d' % s2], W=['B7'], sig=False)
                                    P(lambda: nc.tensor.matmul(B[3][:, 300 + h * 16 + j:301 + h * 16 + j], lhsT=kz[:, h, :], rhs=ones_bf[:, 0:1],
                                                               start=True, stop=True), R=[kzk, 'ones_bf'], W=['B3'], sig=(h == 3))
                                for h in range(4):
                                    yield ('dve', [cfk, 'decS', 'B7'], [cfk])
                                    V(lambda: nc.vector.scalar_tensor_tensor(out=cf[:, jj * 4 + h, :], in0=cf[:, jj * 4 + h, :],
                                                                             scalar=decS[:, h, j:j + 1], in1=B[7][:, h * 128:(h + 1) * 128],
                                                                             op0=ALU.mult, op1=ALU.add), R=[cfk, 'decS', 'B7'], W=[cfk])
                            yield ('sp', [cfk], ['C_s%d' % ei])
                            tk.dma('sp', lambda: nc.sync.dma_start(out=Csv[:, 8 * ei:8 * ei + 8, :], in_=cf[:]), [cfk], ['C_s%d' % ei], 'st' + cfk)
                        yield ('dve', ['n0f', 'decS'], ['n1s'])
                        V(lambda: nc.vector.tensor_tensor(out=n1s[:], in0=n0f[:].rearrange("p (j h) -> p h j", h=4), in1=decS[:], op=ALU.mult),
                          R=['n0f', 'decS'], W=['n1s'])
                        yield ('dve', ['n1s', 'B3'], ['n1s'])
                        V(lambda: nc.vector.tensor_tensor(out=n1s[:], in0=n1s[:], in1=B[3][:, 300:364].rearrange("p (h j) -> p h j", h=4), op=ALU.add),
                          R=['n1s', 'B3'], W=['n1s'])
                        P(lambda: nc.tensor.transpose(out=B[3][0:64, 384:512], in_=n1s[:].rearrange("p h j -> p (h j)"), identity=ident_f[:]),
                          R=['n1s', 'ident_f'], W=['B3'])
                        yield ('dve', ['B3'], ['n1t'])
                        V(lambda: nc.vector.tensor_copy(out=n1t[:], in_=B[3][0:64, 384:512]), R=['B3'], W=['n1t'])
                        n_s_v = n_s.rearrange("j h k -> h j k")
                        for h in range(4):
                            yield ('sp', ['n1t'], ['n_s%d' % h])
                            tk.dma('sp', lambda: nc.sync.dma_start(out=n_s_v[h], in_=n1t[16 * h:16 * h + 16, :]), ['n1t'], ['n_s%d' % h], 'n_s')

                    yield 'SPLIT'
                    xr_ = xr[pos % 2]
                    xrk = 'xr%d' % (pos % 2)
                    load_into(xr_, xrk, i, 'xr%d' % (pos % 2))
                    c_ = cl_
                    yield ('dve', [ck], [ck])
                    V(lambda: nc.vector.scalar_tensor_tensor(out=c_[:, 12:16], in0=c_[:, 12:16], scalar=-1.0, in1=c_[:, 12:16],
                                                             op0=ALU.mult, op1=ALU.max), R=[ck], W=[ck])
                    yield ('dve', [ck], [ck])
                    V(lambda: nc.vector.tensor_tensor(out=c_[:, 12:16], in0=c_[:, 12:16], in1=c_[:, 4:8], op=ALU.max), R=[ck], W=[ck])
                    yield ('dve', [ck], [ck])
                    V(lambda: nc.vector.reciprocal(out=c_[:, 12:16], in_=c_[:, 12:16]), R=[ck], W=[ck])
                    for h in range(4):
                        yield ('act', [nSk], ['hj', ck])
                        A(lambda: nc.scalar.activation(out=hj[:], in_=nS[:, h * 128:(h + 1) * 128], func=AF.Square,
                                                       accum_out=c_[:, 16 + h:17 + h]), R=[nSk], W=['hj', ck])
                    yield ('dve', [ck], [ck])
                    V(lambda: nc.vector.tensor_tensor(out=c_[:, 16:20], in0=c_[:, 16:20], in1=c_[:, 12:16], op=ALU.mult), R=[ck], W=[ck])
                    yield ('dve', [ck], [ck])
                    V(lambda: nc.vector.tensor_tensor(out=c_[:, 16:20], in0=c_[:, 16:20], in1=c_[:, 12:16], op=ALU.mult), R=[ck], W=[ck])
                    yield ('act', [ck], [ck])
                    A(lambda: nc.scalar.activation(out=c_[:, 20:24], in_=c_[:, 16:20], func=AF.Ln, scale=1.0 / 128, bias=EPS), R=[ck], W=[ck])
                    yield ('act', [ck], [ck])
                    A(lambda: nc.scalar.activation(out=c_[:, 20:24], in_=c_[:, 20:24], func=AF.Exp, scale=-0.5), R=[ck], W=[ck])
                    yield ('dve', [ck], [ck])
                    V(lambda: nc.vector.tensor_tensor(out=c_[:, 20:24], in0=c_[:, 20:24], in1=c_[:, 12:16], op=ALU.mult), R=[ck], W=[ck])
                    for h in range(4):
                        yield ('dve', [nSk, ck, ok_], ['mixh'])
                        V(lambda: nc.vector.scalar_tensor_tensor(out=mixh[:, h * 128:(h + 1) * 128], in0=nS[:, h * 128:(h + 1) * 128],
                                                                 scalar=c_[:, 20 + h:21 + h], in1=og[so][:, h * 128:(h + 1) * 128],
                                                                 op0=ALU.mult, op1=ALU.mult), R=[nSk, ck, ok_], W=['mixh'])
                    for h in range(4):
                        P(lambda: nc.tensor.transpose(out=B1b[:, h * 128:(h + 1) * 128], in_=mixh[:, h * 128:(h + 1) * 128], identity=ident_bf[:]),
                          R=['mixh', 'ident_bf'], W=['B1'], sig=(h == 3))
                    yield ('act', ['B1'], [mk])
                    A(lambda: nc.scalar.activation(out=mixT[s2][:, 4:8, :], in_=B1b[:, 0:512].rearrange("p (h t) -> p h t", h=4), func=AF.Copy),
                      R=['B1'], W=[mk])

                    if ckpt('t%dk' % pos):
                        return
                    yield 'SPLIT'
                    for g in range(4):
                        P(lambda: nc.tensor.matmul(B[2][:, g * 128:(g + 1) * 128], lhsT=w_pool_sb[:, g, :], rhs=Z[:, g, :], start=True, stop=True),
                          R=[zk, 'w_pool'], W=['B2'], sig=(g == 3))
                    yield ('dve', ['B2', 'pscol'], [mk])
                    V(lambda: nc.vector.tensor_tensor(out=mixT[s2][:, 0:4, :], in0=B2v, in1=pscol[:].unsqueeze(2).to_broadcast([128, 4, 128]),
                                                      op=ALU.mult), R=['B2', 'pscol'], W=[mk])

                    for half in range(2):
                        for k in range(8):
                            P(lambda: nc.tensor.matmul(B[4 + half][:, :], lhsT=mixT[s2][:, k, :], rhs=w_out_sb[:, k, half * 512:(half + 1) * 512],
                                                       start=(k == 0), stop=(k == 7)), R=[mk, 'w_out'], W=['B%d' % (4 + half)], sig=(k == 7))
                    for half in range(2):
                        yield ('dve', [xrk, 'B%d' % (4 + half)], [xrk])
                        V(lambda: nc.vector.tensor_tensor(out=xr_[:, half * 512:(half + 1) * 512], in0=xr_[:, half * 512:(half + 1) * 512],
                                                          in1=B[4 + half][:, :], op=ALU.add), R=[xrk, 'B%d' % (4 + half)], W=[xrk])
                    yield ('sp', [xrk], [x2key(i)])
                    tk.dma('sp', lambda: nc.sync.dma_start(out=x2s[128 * i:128 * i + 128, :], in_=xr_[:]), [xrk], [x2key(i)], 'x2st%d' % (pos % 2))

                gens = []
                pending = {}
                pos_next = 0
                load_x(1)
                tk.dma('sp', lambda: nc.sync.dma_start(out=spt[0][:, :], in_=spool.rearrange("j i c -> (j i) c")[0:128, :]),
                       [], ['Dt'], 'spt0')
                tk.dma('sp', lambda: nc.sync.dma_start(out=spt[1][0:112, :], in_=spool.rearrange("j i c -> (j i) c")[128:240, :]),
                       [], ['interB'], 'spt1')
                tk.dma('sp', lambda: nc.sync.dma_start(out=n0t[:, :], in_=sn[:, :]), [], ['n0t'], 'n0t')
                with nc.allow_non_contiguous_dma(reason="tiny"):
                    tk.dma('sp', lambda: nc.sync.dma_start(out=m0r[:, :], in_=sm_.rearrange("j h -> h j")), [], ['m0r'], 'm0r')
                tk.dma('pool', lambda: nc.gpsimd.dma_start(out=C0b[:], in_=sC.rearrange("j h k v -> k (j h) v")),
                       [], ['C0b', 'stg0', 'stg1', 'stg2', 'stg3'], 'C0b')
                tk.dma('sp', lambda: nc.sync.dma_start(out=C0f[0][:], in_=sC.rearrange("j h k v -> k (j h) v")[:, 0:8, :]), [], ['C0f0', 'stg1', 'stg2'], 'ldC0f0')
                tk.dma('sp', lambda: nc.sync.dma_start(out=C0f[1][:], in_=sC.rearrange("j h k v -> k (j h) v")[:, 8:16, :]), [], ['C0f1', 'stg1', 'stg2'], 'ldC0f1')
                tk.dma('sp', lambda: nc.sync.dma_start(out=pool_s[:, 0:11, :], in_=spool[:, 4:15, :]), [], ['pool_s_a'], 'pool_s_a')


                def advance(g_):
                    while True:
                        try:
                            v = next(g_)
                        except StopIteration:
                            v = 'END'
                        if v is None:
                            continue
                        pending[id(g_)] = v
                        return

                while gens or pos_next < NT:
                    if pos_next < NT:
                        if 0 < pos_next and pos_next + 1 < NT:
                            load_x(pos_next + 1)
                        g_new = mixer_tile(order[pos_next], pos_next)
                        gens.append(g_new)
                        pending[id(g_new)] = 'SPLIT'
                        pos_next += 1
                    if 7 <= pos_next <= 12:
                        w_up_v0 = w_up.rearrange("(k p) f -> p k f", p=128)
                        for blk in [pos_next - 7]:
                            dst = arena[:, blk * 2048:(blk + 1) * 2048].bitcast(BF16).rearrange("p (k c) -> p k c", k=8)
                            tk.dma('pool', lambda: nc.gpsimd.dma_start(out=dst, in_=w_up_v0[:, :, blk * 512:(blk + 1) * 512]),
                                   [], ['w_up%d' % blk, 'C0b', 'C0f0', 'C0f1', 'C0f2', 'C0f3', 'qz', 'usamp', 'Was', 'Wbs', 'kwz0'], 'w_up%d' % blk)
                    for g_ in gens:
                        advance(g_)
                    while True:
                        best = None
                        best_t = None
                        for gi_, g_ in enumerate(gens):
                            d_ = pending[id(g_)]
                            if isinstance(d_, tuple):
                                t_ = tk.estimate(d_[0], d_[1], d_[2]) + BETA * gi_
                                if best is None or t_ < best_t:
                                    best, best_t = g_, t_
                        if best is None:
                            break
                        if SCHED == 'rr':
                            for g_ in list(gens):
                                if isinstance(pending[id(g_)], tuple):
                                    advance(g_)
                                    junk_mm()
                        else:
                            advance(best)
                            junk_mm()
                    gens = [g_ for g_ in gens if pending[id(g_)] != 'END']
                    if stop[0]:
                        break

                P(lambda: nc.tensor.matmul(B[0][:, 0:128], lhsT=ident_bf[:], rhs=ident_bf[:], start=True, stop=True), R=['ident_bf'], W=['B0'], sig=True)
                tk.barrier()
                ckpt('mixer')
                if stop[0]:
                    tk.finish()
                    return nc

            with ExitStack() as fs:
                w_up_hi = sb(fs, "w_up_hi", [128, 2, 8, 512], BF16)

                def wup(blk):
                    if blk < 6:
                        return arena[:, blk * 2048:(blk + 1) * 2048].bitcast(BF16).rearrange("p (k c) -> p k c", k=8)
                    return w_up_hi[:, blk - 6, :, :]
                w_dn_sb = sb(fs, "w_dn_sb", [128, 32, D], BF16)
                w_up_v = w_up.rearrange("(k p) f -> p k f", p=128)
                w_dn_v = w_down.rearrange("(f p) d -> p f d", p=128)
                for blk in range(6, 8):
                    tk.dma('pool', lambda: nc.gpsimd.dma_start(out=w_up_hi[:, blk - 6, :, :], in_=w_up_v[:, :, blk * 512:(blk + 1) * 512]),
                           [], ['w_up%d' % blk], 'w_up%d' % blk)
                for blk in range(8):
                    tk.dma('pool', lambda: nc.gpsimd.dma_start(out=w_dn_sb[:, blk * 4:(blk + 1) * 4, :], in_=w_dn_v[:, blk * 4:(blk + 1) * 4, :]),
                           [], ['w_dn%d' % blk], 'w_dn%d' % blk)
                NXF = 6
                xf = [sb(fs, "xf%d" % i, [128, D]) for i in range(NXF)]
                junk2 = sb(fs, "junk2", [128, D], BF16)
                xn2 = [sb(fs, "xn2_%d" % i, [128, D], BF16) for i in range(2)]
                st2 = [sb(fs, "st2_%d" % i, [128, 8]) for i in range(2)]
                xn2T = [sb(fs, "xn2T%d" % i, [128, 8, 256], BF16) for i in range(2)]
                aT = [sb(fs, "aT%d" % i, [128, 32, 256], BF16) for i in range(2)]
                rr = [sb(fs, "rr%d" % i, [128, 2, 256]) for i in range(2)]

                groups = [[2 * g, 2 * g + 1] for g in range(8)] + [[SP]]
                tiles_in_order = [t for g in groups for t in g]

                slot_of = {}
                free_slots = list(range(NXF))
                load_queue = list(tiles_in_order)

                def prefetch():
                    while free_slots and load_queue:
                        i = load_queue.pop(0)
                        s = free_slots.pop(0)
                        slot_of[i] = s
                        tk.dma('sp', lambda: nc.sync.dma_start(out=xf[s][:], in_=x2s[128 * i:128 * i + 128, :]), [x2key(i)], ['xf%d' % s], 'xf%d' % s)

                prefetch()
                tcount = [0]
                evc = [0]

                xn_slot = {}

                def ffn_norm(gi, tl, only=None):
                    for ti, i in enumerate(tl):
                        if only is not None and ti != only:
                            continue
                        n = tcount[0]; tcount[0] += 1
                        s = slot_of[i]
                        s2 = n % 2
                        xn_slot[i] = s2
                        sk = 'st2_%d' % s2
                        xkey = 'xf%d' % s
                        A(lambda: nc.scalar.activation(out=junk2[:], in_=xf[s][:], func=AF.Square, accum_out=st2[s2][:, 0:1]),
                          R=[xkey], W=['junk2', sk])
                        A(lambda: nc.scalar.activation(out=st2[s2][:, 1:2], in_=st2[s2][:, 0:1], func=AF.Ln, scale=1.0 / D, bias=EPS), R=[sk], W=[sk])
                        A(lambda: nc.scalar.activation(out=st2[s2][:, 1:2], in_=st2[s2][:, 1:2], func=AF.Exp, scale=-0.5), R=[sk], W=[sk])
                        A(lambda: nc.scalar.activation(out=xn2[s2][:], in_=xf[s][:], func=AF.Copy, scale=st2[s2][:, 1:2]),
                          R=[xkey, sk], W=['xn2_%d' % s2])

                def ffn_tr(gi, tl, only=None):
                    gs = gi % 2
                    xTk = 'xn2T%d' % gs
                    for ti, i in enumerate(tl):
                        if only is not None and ti != only:
                            continue
                        s2 = xn_slot[i]
                        for k in range(8):
                            P(lambda: nc.tensor.transpose(out=B0b[:, k * 128:(k + 1) * 128], in_=xn2[s2][:, k * 128:(k + 1) * 128], identity=ident_bf[:]),
                              R=['xn2_%d' % s2, 'ident_bf'], W=['B0'], sig=(k == 7))
                        V(lambda: nc.vector.tensor_tensor(out=xn2T[gs][:, :, ti * 128:(ti + 1) * 128], in0=B0b.rearrange("p (k t) -> p k t", k=8),
                                                          in1=gcol2[:].unsqueeze(2).to_broadcast([128, 8, 128]), op=ALU.mult),
                          R=['B0', 'gcol2'], W=[xTk])

                def ffn_up(gi, tl, hook=None):
                    gs = gi % 2
                    ntok = 64 if tl == [SP] else 128 * len(tl)
                    xTk = 'xn2T%d' % gs
                    ak = 'aT%d' % gs
                    for fp in range(16):
                        if hook is not None and fp in (3, 9):
                            hook(0 if fp == 3 else 1)
                        e = evc[0]; evc[0] += 1
                        bank = 1 + e % 3
                        bkey = 'B%d' % bank
                        for hf in range(2):
                            f = 2 * fp + hf
                            for k in range(8):
                                P(lambda: nc.tensor.matmul(B[bank][:, hf * 256:hf * 256 + ntok], lhsT=wup(f // 4)[:, k, (f % 4) * 128:(f % 4 + 1) * 128],
                                                           rhs=xn2T[gs][:, k, 0:ntok], start=(k == 0), stop=(k == 7)),
                                  R=[xTk, 'w_up%d' % (f // 4)], W=[bkey], sig=(hf == 1 and k == 7))
                        r_ = rr[e % 2]
                        rk_ = 'rr%d' % (e % 2)
                        A(lambda: nc.scalar.activation(out=r_[:, :, 0:ntok], in_=B[bank][:].rearrange("p (a t) -> p a t", a=2)[:, :, 0:ntok],
                                                       func=AF.Relu), R=[bkey], W=[rk_])
                        V(lambda: nc.vector.tensor_tensor(out=aT[gs][:, 2 * fp:2 * fp + 2, 0:ntok], in0=r_[:, :, 0:ntok], in1=r_[:, :, 0:ntok],
                                                          op=ALU.mult), R=[rk_], W=[ak])

                def ffn_down(gi, tl, only=None):
                    gs = gi % 2
                    ak = 'aT%d' % gs
                    for ti, i in enumerate(tl):
                        if only is not None and ti != only:
                            continue
                        s = slot_of[i]
                        xkey = 'xf%d' % s
                        db = 4 + 2 * (ti % 2)
                        nr = 64 if i == SP else 128
                        for half in range(2):
                            for f in range(32):
                                P(lambda: nc.tensor.matmul(B[db + half][0:nr, :], lhsT=aT[gs][:, f, ti * 128:ti * 128 + nr],
                                                           rhs=w_dn_sb[:, f, half * 512:(half + 1) * 512], start=(f == 0), stop=(f == 31)),
                                  R=[ak, 'w_dn%d' % (f // 4)], W=['B%d' % (db + half)], sig=(f == 31))
                        for half in range(2):
                            V(lambda: nc.vector.tensor_tensor(out=xf[s][0:nr, half * 512:(half + 1) * 512], in0=xf[s][0:nr, half * 512:(half + 1) * 512],
                                                              in1=B[db + half][0:nr, :], op=ALU.add), R=[xkey, 'B%d' % (db + half)], W=[xkey])
                        s2 = ti % 2
                        sk = 'st2_%d' % s2
                        A(lambda: nc.scalar.activation(out=junk2[0:nr, :], in_=xf[s][0:nr, :], func=AF.Square, accum_out=st2[s2][0:nr, 2:3]), R=[xkey], W=['junk2', sk])
                        A(lambda: nc.scalar.activation(out=st2[s2][0:nr, 3:4], in_=st2[s2][0:nr, 2:3], func=AF.Ln, scale=1.0 / D, bias=EPS), R=[sk], W=[sk])
                        A(lambda: nc.scalar.activation(out=st2[s2][0:nr, 3:4], in_=st2[s2][0:nr, 3:4], func=AF.Exp, scale=-0.5), R=[sk], W=[sk])
                        V(lambda: nc.vector.scalar_tensor_tensor(out=xf[s][0:nr, :], in0=xf[s][0:nr, :], scalar=st2[s2][0:nr, 3:4], in1=gfB[0:nr, :],
                                                                 op0=ALU.mult, op1=ALU.mult), R=[xkey, sk, 'gfB'], W=[xkey])
                        if i == SP:
                            tk.dma('sp', lambda: nc.sync.dma_start(out=ys[:, :], in_=xf[s][0:64, :]), [xkey], ['ys'], 'yst%d' % s)
                        else:
                            tk.dma('sp', lambda: nc.sync.dma_start(out=yp[128 * i:128 * i + 128, :], in_=xf[s][:]), [xkey], ['yp%d' % i], 'yst%d' % s)
                        free_slots.append(s)
                        prefetch()

                ng = len(groups)

                def nhook(g_):
                    if g_ >= ng:
                        return None
                    return lambda t_: (ffn_norm(g_, groups[g_], only=t_) if t_ < len(groups[g_]) else None)

                def tr_down(g_next, g_cur):
                    for t_ in range(2):
                        if g_next < ng and t_ < len(groups[g_next]):
                            ffn_tr(g_next, groups[g_next], only=t_)
                        if t_ < len(groups[g_cur]):
                            ffn_down(g_cur, groups[g_cur], only=t_)

                ffn_norm(0, groups[0]); ffn_tr(0, groups[0])
                ffn_up(0, groups[0], nhook(1)); ffn_tr(1, groups[1])
                ffn_up(1, groups[1], nhook(2))
                tr_down(2, 0)
                ffn_down(1, groups[1])
                for gi in range(2, ng):
                    ffn_up(gi, groups[gi], nhook(gi + 1))
                    tr_down(gi + 1, gi)


                pass
        except _Stop:
            ms.close()
            tk.barrier()
        tk.finish()
    return nc


_NC_CACHE = {}


def kernel(x_prompt, x_sample, state_pool, state_C, state_n, state_m, meta_tokens, norm1, w_in,
           b_gate, w_pool, pool_scale, head_gain, w_out, norm2, w_up, w_down, norm_f):
    f = lambda a: np.ascontiguousarray(np.asarray(a, dtype=np.float32))
    if 'nc' not in _NC_CACHE:
        _NC_CACHE['nc'] = build_nc()
    nc = _NC_CACHE['nc']
    x_prompt = f(x_prompt); x_sample = f(x_sample)
    shared = {
        "meta": f(meta_tokens), "norm1": f(norm1).reshape(D), "w_in": f(w_in).reshape(D, INC),
        "b_gate": f(b_gate).reshape(8), "w_pool": f(w_pool).reshape(4, 128, 128),
        "pool_scale": f(pool_scale).reshape(512), "head_gain": f(head_gain).reshape(512),
        "w_out": f(w_out).reshape(D, D), "norm2": f(norm2).reshape(D), "w_up": f(w_up).reshape(D, DFF),
        "w_down": f(w_down).reshape(DFF, D), "norm_f": f(norm_f).reshape(D),
    }
    sp_ = f(state_pool)[0]; sc_ = f(state_C)[0]; sn_ = f(state_n)[0]; sm_ = f(state_m)[0]
    in_maps = []
    for c in range(NCORES):
        m = dict(shared)
        m["xp"] = x_prompt[c]
        m["xs"] = f(x_sample[16 * c:16 * c + 16].reshape(64, D))
        m["spool"] = f(sp_[16 * c:16 * c + 16])
        m["sC"] = f(sc_[16 * c:16 * c + 16])
        m["sn"] = f(sn_[16 * c:16 * c + 16].reshape(64, 128))
        m["sm"] = f(sm_[16 * c:16 * c + 16])
        in_maps.append(m)
    res = run_bass_kernel_spmd(nc, in_maps, core_ids=list(range(NCORES)))
    rs = res.results
    cat = lambda k: np.stack([np.asarray(r[k], dtype=np.float32) for r in rs], axis=0)
    y_prompt = cat("yp")
    y_sample = np.concatenate([np.asarray(r["ys"], dtype=np.float32).reshape(16, 4, D) for r in rs], axis=0)
    pool_prompt = cat("pool_p")[None]
    C_prompt = cat("C_p")[None]
    n_prompt = cat("n_p")[None]
    m_prompt = cat("m_p").reshape(NCORES, 4)[None]
    pool_sample = np.concatenate([np.asarray(r["pool_s"], dtype=np.float32) for r in rs], axis=0)[None]
    C_sample = np.concatenate([np.asarray(r["C_s"], dtype=np.float32) for r in rs], axis=0)[None]
    n_sample = np.concatenate([np.asarray(r["n_s"], dtype=np.float32) for r in rs], axis=0)[None]
    m_sample = np.concatenate([np.asarray(r["m_s"], dtype=np.float32) for r in rs], axis=0)[None]
    return (y_prompt, y_sample, pool_prompt, C_prompt, n_prompt, m_prompt,
            pool_sample, C_sample, n_sample, m_sample)
```

```python
import numpy as np
from contextlib import ExitStack
import concourse.bass as bass
import concourse.mybir as mybir
from concourse.bass_utils import run_bass_kernel_spmd

F32 = mybir.dt.float32
BF16 = mybir.dt.bfloat16
ALU = mybir.AluOpType
AF = mybir.ActivationFunctionType

NCORES = 8
D = 1024
SEQ = 2048
NT = 17
SP = 16
DFF = 4096
INC = 2568
EPS = 1e-6
WIN = (2, 4, 8, 16)
SCHED = 'list'
JUNK_EVERY = 0
OPLOG = None
PE_RATE = 1400.0
BETA = 0.0


class Tk:
    HOP = 0.0

    def __init__(self, nc, es):
        self.nc = nc
        self.es = es
        self.eng = {'pe': nc.tensor, 'act': nc.scalar, 'dve': nc.vector, 'pool': nc.gpsimd, 'sp': nc.sync}
        self.sems = {}
        self.cnt = {}
        for e in self.eng:
            self.sems['E' + e] = es.enter_context(nc.semaphore('s_' + e))
            self.cnt['E' + e] = 0
        self.lastw = {}
        self.readers = {}
        self.waited = {}
        self.nwaits = 0
        self.attach_waits = True
        self.oplog = None
        self.efree = {e: 0.0 for e in self.eng}
        self.wready = {}
        self.rready = {}

    def estimate(self, eng, R, W):
        t = self.efree.get(eng, 0.0)
        h = self.HOP
        for k in R:
            t = max(t, self.wready.get(k, 0.0) + h)
        for k in W:
            t = max(t, self.wready.get(k, 0.0) + h, self.rready.get(k, 0.0) + h)
        return t

    @staticmethod
    def _fsize(ins):
        try:
            ap = ins.ins.outs[0].ap
            n = 1
            for st_, cnt in ap[1:]:
                n *= cnt
            return n
        except Exception:
            return 128

    def _model(self, eng, R, W, ins, is_dma):
        start = self.estimate(eng, R, W)
        n = self._fsize(ins)
        if is_dma:
            self.efree[eng] = start + 0.1
            end = start + 2.5
        else:
            if eng == 'pe':
                cost = max(0.04, n / PE_RATE)
            elif eng == 'act':
                cost = 0.22 + n / 1200.0
            elif eng == 'dve':
                cost = 0.12 + n / 960.0
            else:
                cost = 0.15 + n / 480.0
            end = start + cost
            self.efree[eng] = end
        for k in R:
            if end > self.rready.get(k, 0.0):
                self.rready[k] = end
        for k in W:
            self.wready[k] = end
        if self.oplog is not None:
            self.oplog.append((start, end, eng, is_dma, tuple(R), tuple(W)))

    def chan(self, name):
        k = 'C' + name
        if k not in self.sems:
            self.sems[k] = self.es.enter_context(self.nc.semaphore('c_' + name))
            self.cnt[k] = 0
        return k

    def _collect(self, R, W):
        deps = []
        for k in R:
            t = self.lastw.get(k)
            if t is not None:
                deps.append(('raw', t))
            if len(k) == 2 and k[0] == 'B' and k[1].isdigit():
                for t in self.readers.get(k, ()):
                    deps.append(('war', t))
        for k in W:
            t = self.lastw.get(k)
            if t is not None:
                deps.append(('waw', t))
            for t in self.readers.get(k, ()):
                deps.append(('war', t))
        return deps

    def _wait(self, eng, deps, is_dma):
        best = {}
        for kind, (sk, val, prod) in deps:
            if not is_dma and prod == eng:
                if eng == 'pe':
                    continue
                if kind == 'war':
                    continue
            if val > best.get(sk, 0):
                best[sk] = val
        need = []
        for sk, val in best.items():
            if self.waited.get((eng, sk), 0) >= val:
                continue
            need.append((sk, val))
            self.waited[(eng, sk)] = val
            self.nwaits += 1
        attach = need.pop() if (need and self.attach_waits) else None
        for sk, val in need:
            self.eng[eng].wait_ge(self.sems[sk], val)
        return attach

    def _record(self, tok, R, W):
        for k in R:
            self.readers.setdefault(k, []).append(tok)
        for k in W:
            self.lastw[k] = tok
            self.readers[k] = []

    def op(self, eng, fn, R=(), W=(), signal=True):
        att = self._wait(eng, self._collect(R, W), False)
        ins = fn()
        if att is not None:
            ins._wait_ge(self.sems[att[0]], att[1])
        sk = 'E' + eng
        if signal:
            self.cnt[sk] += 1
            ins.then_inc(self.sems[sk], 1)
            tok = (sk, self.cnt[sk], eng)
        else:
            tok = (sk, self.cnt[sk] + 1, eng)
        self._record(tok, R, W)
        self._model(eng, R, W, ins, False)
        return ins

    def dma(self, queue, fn, R, W, chan):
        ck = self.chan(chan)
        att = self._wait(queue, self._collect(R, W), True)
        if att is not None:
            self.eng[queue].wait_ge(self.sems[att[0]], att[1])
        ins = fn()
        self.cnt[ck] += 16
        ins.then_inc(self.sems[ck], 16)
        tok = (ck, self.cnt[ck], 'dma')
        self._record(tok, R, W)
        self._model(queue, R, W, ins, True)
        return ins

    def seal(self, chan, keys):
        ck = self.chan(chan)
        for k in keys:
            self.lastw[k] = (ck, self.cnt[ck], 'dma')

    def barrier(self):
        sp = self.eng['sp']
        for sk, c in self.cnt.items():
            if c > 0 and self.waited.get(('sp', sk), 0) < c:
                sp.wait_ge(self.sems[sk], c)
                self.waited[('sp', sk)] = c
        self.cnt['Esp'] += 1
        sp.sem_inc(self.sems['Esp'], 1)
        rel = self.cnt['Esp']
        for e in self.eng:
            if e != 'sp':
                self.eng[e].wait_ge(self.sems['Esp'], rel)
            for sk, c in self.cnt.items():
                self.waited[(e, sk)] = max(self.waited.get((e, sk), 0), c if sk != 'Esp' or e != 'sp' else rel)
        self.lastw.clear()
        self.readers.clear()

    def finish(self):
        for sk, c in self.cnt.items():
            if c > 0 and self.waited.get(('sp', sk), 0) < c:
                self.nc.sync.wait_ge(self.sems[sk], c)
                self.waited[('sp', sk)] = c


class _Stop(Exception):
    pass


def build_nc(upto=None):
    nc = bass.Bass("TRN2", target_bir_lowering=False)

    def din(name, shape):
        return nc.dram_tensor(name, list(shape), F32, kind="ExternalInput").ap()

    def dout(name, shape):
        return nc.dram_tensor(name, list(shape), F32, kind="ExternalOutput").ap()

    xp = din("xp", [SEQ, D]); xs = din("xs", [64, D]); meta = din("meta", [16, D])
    spool = din("spool", [16, 15, 512]); sC = din("sC", [16, 4, 128, 128])
    sn = din("sn", [64, 128]); sm_ = din("sm", [16, 4])
    norm1 = din("norm1", [D]); w_in = din("w_in", [D, INC]); b_gate = din("b_gate", [8])
    w_pool = din("w_pool", [4, 128, 128]); pool_scale = din("pool_scale", [512])
    head_gain = din("head_gain", [512]); w_out = din("w_out", [D, D]); norm2 = din("norm2", [D])
    w_up = din("w_up", [D, DFF]); w_down = din("w_down", [DFF, D]); norm_f = din("norm_f", [D])

    yp = dout("yp", [SEQ, D]); ys = dout("ys", [64, D])
    pool_p = dout("pool_p", [15, 512]); C_p = dout("C_p", [4, 128, 128]); n_p = dout("n_p", [4, 128])
    m_p = dout("m_p", [1, 4])
    pool_s = dout("pool_s", [16, 15, 512]); C_s = dout("C_s", [16, 4, 128, 128])
    n_s = dout("n_s", [16, 4, 128]); m_s = dout("m_s", [16, 4])
    x2s = nc.dram_tensor("x2s", [NT * 128, D], F32, kind="Internal").ap()
    dbg_out = {}

    with ExitStack() as es:
        E = es.enter_context
        tk = Tk(nc, es)
        if OPLOG is not None:
            tk.oplog = OPLOG

        def sb(es_, name, shape, dt=F32):
            return es_.enter_context(nc.sbuf_tensor(name, list(shape), dt))

        def A(fn, R=(), W=()): return tk.op('act', fn, R, W)
        def V(fn, R=(), W=()): return tk.op('dve', fn, R, W)
        def G(fn, R=(), W=()): return tk.op('pool', fn, R, W)
        def P(fn, R=(), W=(), sig=True): return tk.op('pe', fn, R, W, signal=sig)

        B = [E(nc.psum_tensor("B%d" % i, [128, 512], F32)) for i in range(8)]
        B0b = B[0][:].bitcast(BF16)
        B1b = B[1][:].bitcast(BF16)

        ident_bf = sb(es, "ident_bf", [128, 128], BF16)
        gcol2 = sb(es, "gcol2", [128, 8])
        gfB = sb(es, "gfB", [128, D])
        ARENA_W = 12288
        arena = sb(es, "arena", [128, ARENA_W])
        ms = ExitStack()
        ident_f = sb(ms, "ident_f", [128, 128])
        sel = sb(ms, "sel", [4, 4, 128])
        maskC = sb(ms, "maskC", [128, 128])
        maskS = sb(ms, "maskS", [128, 128])
        maskcols = sb(ms, "maskcols", [128, 17])
        selmask = sb(ms, "selmask", [128, 16, 64], BF16)
        E1 = sb(ms, "E1", [16, 128])
        E2 = sb(ms, "E2", [1, 128])
        zeros4 = sb(ms, "zeros4", [4, 128])
        ones_bf = sb(ms, "ones_bf", [128, 1], BF16)
        gcol1 = sb(ms, "gcol1", [128, 8])
        hgB = sb(ms, "hgB", [128, 512])
        pscol = sb(ms, "pscol", [128, 4])
        bi_col = sb(ms, "bi_col", [4, 1]); bf_col = sb(ms, "bf_col", [4, 1])
        rc_meta = sb(ms, "rc_meta", [128, 4, 16])

        NX = 3
        xt = [sb(ms, "xt%d" % i, [128, D]) for i in range(NX)]
        tk.op('pool', lambda: nc.gpsimd.memset(xt[0][:], 0.0), (), ['xt0'])
        tk.dma('sp', lambda: nc.sync.dma_start(out=xt[0][0:64, :], in_=xs[:, :]), [], ['xt0'], 'x0a')
        tk.dma('sp', lambda: nc.sync.dma_start(out=xt[0][64:80, :], in_=meta[:, :]), ['xt0'], ['xt0'], 'x0')
        def mk_ident(t, key):
            G(lambda: nc.gpsimd.memset(t[:], 1.0), W=[key])
            G(lambda: nc.gpsimd.affine_select(out=t[:], in_=t[:], pattern=[[-1, 128]], compare_op=ALU.is_equal,
                                              fill=0.0, base=0, channel_multiplier=1), R=[key], W=[key])
        mk_ident(ident_bf, 'ident_bf'); mk_ident(ident_f, 'ident_f')
        G(lambda: nc.gpsimd.memset(sel[:], 1.0), W=['sel'])
        G(lambda: nc.gpsimd.affine_select(out=sel[:], in_=sel[:], pattern=[[-1, 4], [0, 128]], compare_op=ALU.is_equal,
                                          fill=0.0, base=0, channel_multiplier=1), R=['sel'], W=['sel'])
        G(lambda: nc.gpsimd.memset(maskC[:], 1.0), W=['maskC'])
        G(lambda: nc.gpsimd.affine_select(out=maskC[:], in_=maskC[:], pattern=[[1, 128]], compare_op=ALU.is_ge,
                                          fill=0.0, base=0, channel_multiplier=-1), R=['maskC'], W=['maskC'])
        G(lambda: nc.gpsimd.memset(selmask[:], 1.0), W=['selmask'])
        G(lambda: nc.gpsimd.affine_select(out=selmask[:], in_=selmask[:], pattern=[[-4, 16], [1, 64]], compare_op=ALU.is_ge,
                                          fill=0.0, base=0, channel_multiplier=0), R=['selmask'], W=['selmask'])
        G(lambda: nc.gpsimd.affine_select(out=selmask[:], in_=selmask[:], pattern=[[4, 16], [-1, 64]], compare_op=ALU.is_ge,
                                          fill=0.0, base=3, channel_multiplier=0), R=['selmask'], W=['selmask'])
        G(lambda: nc.gpsimd.memset(E1[:], 1.0), W=['E1'])
        G(lambda: nc.gpsimd.affine_select(out=E1[:], in_=E1[:], pattern=[[1, 128]], compare_op=ALU.is_ge,
                                          fill=0.0, base=0, channel_multiplier=-4), R=['E1'], W=['E1'])
        G(lambda: nc.gpsimd.affine_select(out=E1[:], in_=E1[:], pattern=[[-1, 128]], compare_op=ALU.is_ge,
                                          fill=0.0, base=3, channel_multiplier=4), R=['E1'], W=['E1'])
        G(lambda: nc.gpsimd.memset(E2[:], 0.0), W=['E2'])
        G(lambda: nc.gpsimd.memset(E2[:, 64:80], 1.0), R=['E2'], W=['E2'])
        G(lambda: nc.gpsimd.memset(zeros4[:], 0.0), W=['zeros4'])
        G(lambda: nc.gpsimd.memset(ones_bf[:], 1.0), W=['ones_bf'])
        for g, w in enumerate(WIN):
            G(lambda: nc.gpsimd.memset(rc_meta[:, g, :], 1.0 / w), R=['rc_meta'], W=['rc_meta'])
            for pos in range(w - 1):
                G(lambda: nc.gpsimd.memset(rc_meta[:, g, pos:pos + 1], 1.0 / (pos + 1)), R=['rc_meta'], W=['rc_meta'])
        P(lambda: nc.tensor.matmul(B[3][:, 0:128], lhsT=E1[:, :], rhs=E1[:, :], start=True, stop=False),
          R=['E1'], W=['B3'], sig=False)
        P(lambda: nc.tensor.matmul(B[3][:, 0:128], lhsT=E2[:, :], rhs=E2[:, :], start=False, stop=True),
          R=['E2'], W=['B3'])
        V(lambda: nc.vector.tensor_tensor(out=maskS[:], in0=B[3][:, 0:128], in1=maskC[:], op=ALU.mult),
          R=['B3', 'maskC'], W=['maskS'])
        P(lambda: nc.tensor.transpose(out=B[3][:, 128:144], in_=E1[:, :], identity=ident_f[0:16, 0:16]),
          R=['E1', 'ident_f'], W=['B3'])
        P(lambda: nc.tensor.transpose(out=B[3][:, 144:145], in_=E2[:, :], identity=ident_f[0:1, 0:1]),
          R=['E2', 'ident_f'], W=['B3'])
        V(lambda: nc.vector.tensor_copy(out=maskcols[:], in_=B[3][:, 128:145]), R=['B3'], W=['maskcols'])

        w_in_sb = sb(ms, "w_in_sb", [128, 8, INC], BF16)
        w_in_v = w_in.rearrange("(k p) c -> p k c", p=128)
        stg = [arena[:, j * INC:(j + 1) * INC] for j in range(4)]
        for k in range(8):
            j = k % 4
            eng_q = 'sp' if k % 2 == 0 else 'act'
            qobj = nc.sync if k % 2 == 0 else nc.scalar
            tk.dma(eng_q, lambda: qobj.dma_start(out=stg[j], in_=w_in_v[:, k, :]), [], ['stg%d' % j], 'stg%d' % j)
            V(lambda: nc.vector.tensor_copy(out=w_in_sb[:, k, :], in_=stg[j]), R=['stg%d' % j], W=['w_in'])
        cl = []
        def cload(t, src, key):
            tk.dma('sp', lambda: nc.sync.dma_start(out=t, in_=src), [], [key], 'const')
            cl.append(key)
        with nc.allow_non_contiguous_dma(reason="small constant loads"):
            cload(gcol1[:], norm1.rearrange("(k p) -> p k", p=128), 'gcol1')
            cload(gcol2[:], norm2.rearrange("(k p) -> p k", p=128), 'gcol2')
            cload(pscol[:], pool_scale.rearrange("(g p) -> p g", p=128), 'pscol')
            cload(bi_col[:], b_gate[0:4].rearrange("(h o) -> h o", o=1), 'bi_col')
            cload(bf_col[:], b_gate[4:8].rearrange("(h o) -> h o", o=1), 'bf_col')
        cload(gfB[:], norm_f.partition_broadcast(128), 'gfB')
        cload(hgB[:], head_gain.partition_broadcast(128), 'hgB')
        tk.seal('const', cl)

        stop = [False]

        def ckpt(name):
            if upto == name:
                stop[0] = True
            return stop[0]

        def x2key(i): return 'x2s%d' % i

        try:
            ckpt('setup')
            with ms:
                w_out_sb = sb(ms, "w_out_sb", [128, 8, D], BF16)
                w_pool_sb = sb(ms, "w_pool_sb", [128, 4, 128], BF16)
                tk.dma('pool', lambda: nc.gpsimd.dma_start(out=w_pool_sb[:], in_=w_pool.rearrange("g c d -> c g d")),
                       [], ['w_pool'], 'w_pool')
                w_out_v = w_out.rearrange("(k p) c -> p k c", p=128)
                for k2 in range(2):
                    tk.dma('pool', lambda: nc.gpsimd.dma_start(out=w_out_sb[:, 4 * k2:4 * k2 + 4, :], in_=w_out_v[:, 4 * k2:4 * k2 + 4, :]),
                           [], ['w_out'], 'w_out')
                tk.seal('w_out', ['w_out'])

                xn = [sb(ms, "xn%d" % i, [128, D], BF16) for i in range(1)]
                xr = [sb(ms, "xr%d" % i, [128, D]) for i in range(2)]
                numS = [sb(ms, "numS%d" % i, [128, 512]) for i in range(2)]
                st = [sb(ms, "st%d" % i, [128, 8]) for i in range(2)]
                xnT = [sb(ms, "xnT%d" % i, [128, 8, 128], BF16) for i in range(2)]
                qT = [sb(ms, "qT%d" % i, [128, 4, 128], BF16) for i in range(2)]
                kT = [sb(ms, "kT%d" % i, [128, 4, 128], BF16) for i in range(2)]
                vt = [sb(ms, "vt%d" % i, [128, 512], BF16) for i in range(2)]
                og = [sb(ms, "og%d" % i, [128, 512]) for i in range(3)]
                utile = [sb(ms, "utile%d" % i, [128, 4, 143]) for i in range(2)]
                Wa = sb(ms, "Wa", [128, 4, 143]); Wb = sb(ms, "Wb", [128, 4, 143])
                zt = [sb(ms, "zt%d" % i, [128, 4, 128], BF16) for i in range(4)]
                mixT = [sb(ms, "mixT%d" % i, [128, 8, 128], BF16) for i in range(2)]
                NR = 9
                rows = [sb(ms, "rows%d" % i, [4, NR, 128]) for i in range(2)]
                cols = [sb(ms, "cols%d" % i, [128, 24]) for i in range(2)]
                Dt = sb(ms, "Dt", [128, 4, 128]); Dm = sb(ms, "Dm", [128, 4, 128])
                interB = sb(ms, "interB", [128, 4, 128])
                ut = Dm[:].rearrange("p h t -> p (h t)")
                spt = [Dt[:].rearrange("p h t -> p (h t)"), interB[:].rearrange("p h t -> p (h t)")]
                Pt = sb(ms, "Pt", [128, 4, 128], BF16)
                qs = sb(ms, "qs", [128, 4, 128], BF16)
                kw = sb(ms, "kw", [128, 4, 128], BF16)
                mixh = sb(ms, "mixh", [128, 512], BF16)
                hj = sb(ms, "hj", [128, 128], BF16)
                Cf = sb(ms, "Cf", [128, 4, 128]); nf = sb(ms, "nf", [128, 4])
                Cb = [sb(ms, "Cb%d" % i, [128, 4, 128], BF16) for i in range(2)]
                nb = [sb(ms, "nb%d" % i, [128, 4], BF16) for i in range(2)]
                usamp = arena[:, 8192:9408].rearrange("p (a b) -> p a b", b=19)
                Was = arena[:, 9408:10624].rearrange("p (a b) -> p a b", b=19)
                Wbs = arena[:, 10624:11840].rearrange("p (a b) -> p a b", b=19)
                umeta = sb(ms, "umeta", [128, 4, 31])
                C0b = arena[:, 0:4096].bitcast(BF16).rearrange("p (a b) -> p a b", b=128)
                C0f = [arena[:, 4096:5120].rearrange("p (a b) -> p a b", b=128), arena[:, 5120:6144].rearrange("p (a b) -> p a b", b=128),
                       arena[:, 8192:9216].rearrange("p (a b) -> p a b", b=128), arena[:, 9216:10240].rearrange("p (a b) -> p a b", b=128)]
                qz = arena[:, 6144:8192].bitcast(BF16).rearrange("p (a b c) -> p a b c", b=4, c=64)
                ktm = sb(ms, "ktm", [128, 4, 128], BF16)
                kwz = [arena[:, 11840:12096].bitcast(BF16).rearrange("p (a b) -> p a b", b=128), sb(ms, "kwz1", [128, 4, 128], BF16)]
                wsm = sb(ms, "wsm", [128, 17, 4])
                m0r = sb(ms, "m0r", [4, 16]); decr = sb(ms, "decr", [4, 16]); msr = sb(ms, "msr", [4, 16])
                decS = sb(ms, "decS", [128, 4, 16])
                n0t = sb(ms, "n0t", [64, 128]); n0f = sb(ms, "n0f", [128, 64]); n0b = sb(ms, "n0b", [128, 64], BF16)
                n1s = sb(ms, "n1s", [128, 4, 16]); n1t = sb(ms, "n1t", [64, 128])
                npt = sb(ms, "npt", [4, 128]); mpr = sb(ms, "mpr", [4, 1])

                G(lambda: nc.gpsimd.memset(Cf[:], 0.0), W=['Cf'])
                G(lambda: nc.gpsimd.memset(nf[:], 0.0), W=['nf'])

                order = [SP] + list(range(16))

                def load_into(buf, key, i, chan):
                    if i == SP:
                        G(lambda: nc.gpsimd.memset(buf[:], 0.0), W=[key])
                        tk.dma('sp', lambda: nc.sync.dma_start(out=buf[0:64, :], in_=xs[:, :]), [], [key], chan + 'a')
                        tk.dma('sp', lambda: nc.sync.dma_start(out=buf[64:80, :], in_=meta[:, :]), [key], [key], chan)
                    else:
                        tk.dma('sp', lambda: nc.sync.dma_start(out=buf[:], in_=xp[128 * i:128 * i + 128, :]), [], [key], chan)

                def load_x(pos_):
                    s = pos_ % NX
                    load_into(xt[s], 'xt%d' % s, order[pos_], 'x%d' % s)

                def rstd_chain(stt, ss_col, out_col, n, keyR, keyW):
                    yield ('act', keyR, keyW)
                    A(lambda: nc.scalar.activation(out=stt[:, out_col:out_col + 1], in_=stt[:, ss_col:ss_col + 1], func=AF.Ln,
                                                   scale=1.0 / n, bias=EPS), R=keyR, W=keyW)
                    yield ('act', keyW, keyW)
                    A(lambda: nc.scalar.activation(out=stt[:, out_col:out_col + 1], in_=stt[:, out_col:out_col + 1], func=AF.Exp,
                                                   scale=-0.5), R=keyW, W=keyW)

                b0_live = [False]
                njunk = [0]

                def junk_mm():
                    if JUNK_EVERY <= 0 or b0_live[0]:
                        return
                    njunk[0] += 1
                    if njunk[0] % JUNK_EVERY:
                        return
                    P(lambda: nc.tensor.matmul(B[0][:, :], lhsT=ident_bf[:], rhs=w_in_sb[:, 0, 0:512], start=True, stop=True),
                      R=['ident_bf', 'w_in'], W=['B0'], sig=False)

                def norm_T(xtile, xkey, slot, gcol, gkey):
                    sk = 'st%d' % slot
                    yield ('act', [xkey], ['xn0', sk])
                    A(lambda: nc.scalar.activation(out=xn[0][:], in_=xtile[:], func=AF.Square, accum_out=st[slot][:, 0:1]),
                      R=[xkey], W=['xn0', sk])
                    yield from rstd_chain(st[slot], 0, 1, D, [sk], [sk])
                    yield ('act', [xkey, sk], ['xn0'])
                    A(lambda: nc.scalar.activation(out=xn[0][:], in_=xtile[:], func=AF.Copy, scale=st[slot][:, 1:2]),
                      R=[xkey, sk], W=['xn0'])
                    b0_live[0] = True
                    for k in range(8):
                        P(lambda: nc.tensor.transpose(out=B0b[:, k * 128:(k + 1) * 128], in_=xn[0][:, k * 128:(k + 1) * 128],
                                                      identity=ident_bf[:]),
                          R=['xn0', 'ident_bf'], W=['B0'], sig=(k == 7))
                    yield ('dve', ['B0', gkey], ['xnT%d' % slot])
                    V(lambda: nc.vector.tensor_tensor(out=xnT[slot][:], in0=B0b.rearrange("p (k t) -> p k t", k=8),
                                                      in1=gcol[:].unsqueeze(2).to_broadcast([128, 8, 128]), op=ALU.mult),
                      R=['B0', gkey], W=['xnT%d' % slot])
                    b0_live[0] = False

                def pool_sums(U, Wa_, Wb_, rpg, L, keyU, keyWa, keyWb):
                    yield ('pool', [keyU], [keyWa])
                    G(lambda: nc.gpsimd.tensor_tensor(out=Wa_[:, :, 1:L], in0=U[:, :, 1:L], in1=U[:, :, 0:L - 1], op=ALU.add),
                      R=[keyU], W=[keyWa])
                    yield ('pool', [keyWa], [keyWb])
                    G(lambda: nc.gpsimd.tensor_tensor(out=Wb_[:, rpg:4 * rpg, 3:L], in0=Wa_[:, rpg:4 * rpg, 3:L],
                                                      in1=Wa_[:, rpg:4 * rpg, 1:L - 2], op=ALU.add), R=[keyWa], W=[keyWb])
                    yield ('pool', [keyWb], [keyWa])
                    G(lambda: nc.gpsimd.tensor_tensor(out=Wa_[:, 2 * rpg:4 * rpg, 7:L], in0=Wb_[:, 2 * rpg:4 * rpg, 7:L],
                                                      in1=Wb_[:, 2 * rpg:4 * rpg, 3:L - 4], op=ALU.add), R=[keyWb], W=[keyWa])
                    yield ('pool', [keyWa], [keyWb])
                    G(lambda: nc.gpsimd.tensor_tensor(out=Wb_[:, 3 * rpg:4 * rpg, 15:L], in0=Wa_[:, 3 * rpg:4 * rpg, 15:L],
                                                      in1=Wa_[:, 3 * rpg:4 * rpg, 7:L - 8], op=ALU.add), R=[keyWa], W=[keyWb])
                    return [Wa_, Wb_, Wa_, Wb_]

                def mixer_tile(i, pos):
                    spc = (i == SP)
                    s2 = pos % 2
                    xs_ = pos % NX
                    xkey = 'xt%d' % xs_
                    rk = 'rows%d' % s2
                    ck = 'cols%d' % s2
                    rw = rows[s2]
                    prev_rw = rows[1 - s2]
                    R_XF, R_T, R_LF, R_F, R_A, R_M, R_AREL, R_MREL, R_WS = range(9)
                    R_MNEG, R_EMN, R_MEND = R_XF, R_T, R_LF

                    yield from norm_T(xt[xs_], xkey, s2, gcol1, 'gcol1')
                    yield 'SPLIT'
                    xk = 'xnT%d' % s2
                    X = xnT[s2]

                    if ckpt('t%da' % pos):
                        return
                    yield None
                    for part, c0 in ((0, 2560), (1, 2564)):
                        for k in range(8):
                            P(lambda: nc.tensor.matmul(B[3][0:4, part * 128:(part + 1) * 128], lhsT=w_in_sb[:, k, c0:c0 + 4],
                                                       rhs=X[:, k, :], start=(k == 0), stop=(k == 7)),
                              R=[xk, 'w_in'], W=['B3'], sig=(k == 7))

                    if ckpt('t%db' % pos):
                        return
                    yield None
                    if spc:
                        yield ('pool', (), [rk])
                        G(lambda: nc.gpsimd.memset(rw[:], 0.0), W=[rk])
                    yield ('dve', ['B3', 'bf_col'], [rk])
                    V(lambda: nc.vector.tensor_scalar(out=rw[:, R_XF, :], in0=B[3][0:4, 128:256], scalar1=bf_col[:, 0:1],
                                                      scalar2=None, op0=ALU.add), R=['B3', 'bf_col'], W=[rk])
                    yield ('dve', [rk], [rk])
                    V(lambda: nc.vector.scalar_tensor_tensor(out=rw[:, R_T, :], in0=rw[:, R_XF, :], scalar=-1.0, in1=rw[:, R_XF, :],
                                                             op0=ALU.mult, op1=ALU.max), R=[rk], W=[rk])
                    yield ('act', [rk], [rk])
                    A(lambda: nc.scalar.activation(out=rw[:, R_T, :], in_=rw[:, R_T, :], func=AF.Exp, scale=-1.0), R=[rk], W=[rk])
                    yield ('act', [rk], [rk])
                    A(lambda: nc.scalar.activation(out=rw[:, R_T, :], in_=rw[:, R_T, :], func=AF.Ln, bias=1.0), R=[rk], W=[rk])
                    yield ('dve', [rk], [rk])
                    V(lambda: nc.vector.scalar_tensor_tensor(out=rw[:, R_LF, :], in0=rw[:, R_XF, :], scalar=0.0, in1=rw[:, R_T, :],
                                                             op0=ALU.min, op1=ALU.subtract), R=[rk], W=[rk])
                    if not spc:
                        if i == 0:
                            Fc = prev_rw[:, R_F, 79:80]; Mc = prev_rw[:, R_M, 79:80]
                        else:
                            Fc = prev_rw[:, R_F, 127:128]; Mc = prev_rw[:, R_M, 127:128]
                        pk = 'rows%d' % (1 - s2)
                        yield ('dve', [rk, pk, 'zeros4'], [rk])
                        V(lambda: nc.vector.tensor_tensor_scan(out=rw[:, R_F, :], data0=rw[:, R_LF, :], data1=zeros4[:, :],
                                                               initial=Fc, op0=ALU.add, op1=ALU.add), R=[rk, pk, 'zeros4'], W=[rk])
                        yield ('dve', ['B3', 'bi_col', rk], [rk])
                        V(lambda: nc.vector.scalar_tensor_tensor(out=rw[:, R_A, :], in0=B[3][0:4, 0:128], scalar=bi_col[:, 0:1],
                                                                 in1=rw[:, R_F, :], op0=ALU.add, op1=ALU.subtract),
                          R=['B3', 'bi_col', rk], W=[rk])
                        yield ('dve', [rk, pk], [rk])
                        V(lambda: nc.vector.tensor_tensor_scan(out=rw[:, R_M, :], data0=rw[:, R_A, :], data1=rw[:, R_A, :],
                                                               initial=Mc, op0=ALU.max, op1=ALU.max), R=[rk, pk], W=[rk])
                        yield ('dve', [rk, pk], [rk])
                        V(lambda: nc.vector.tensor_scalar(out=rw[:, R_AREL, :], in0=rw[:, R_A, :], scalar1=Mc, scalar2=None,
                                                          op0=ALU.subtract), R=[rk, pk], W=[rk])
                        yield ('dve', [rk, pk], [rk])
                        V(lambda: nc.vector.tensor_scalar(out=rw[:, R_MREL, :], in0=rw[:, R_M, :], scalar1=Mc, scalar2=None,
                                                          op0=ALU.subtract), R=[rk, pk], W=[rk])
                        yield ('dve', [rk], [rk])
                        V(lambda: nc.vector.tensor_scalar(out=rw[:, R_WS, :], in0=rw[:, R_AREL, :], scalar1=rw[:, R_MREL, 127:128],
                                                          scalar2=None, op0=ALU.subtract), R=[rk], W=[rk])
                    else:
                        yield ('dve', [rk, 'zeros4'], [rk])
                        V(lambda: nc.vector.tensor_tensor_scan(out=rw[:, R_F, 64:80], data0=rw[:, R_LF, 64:80], data1=zeros4[:, 0:16],
                                                               initial=0.0, op0=ALU.add, op1=ALU.add), R=[rk, 'zeros4'], W=[rk])
                        lf3 = rw[:, R_LF, 0:64].rearrange("h (j t) -> h j t", t=4)
                        F3 = rw[:, R_F, 0:64].rearrange("h (j t) -> h j t", t=4)
                        yield ('dve', [rk], [rk])
                        V(lambda: nc.vector.tensor_copy(out=F3[:, :, 0], in_=lf3[:, :, 0]), R=[rk], W=[rk])
                        for t in range(1, 4):
                            yield ('dve', [rk], [rk])
                            V(lambda: nc.vector.tensor_tensor(out=F3[:, :, t], in0=F3[:, :, t - 1], in1=lf3[:, :, t], op=ALU.add),
                              R=[rk], W=[rk])
                        yield ('dve', ['B3', 'bi_col', rk], [rk])
                        V(lambda: nc.vector.scalar_tensor_tensor(out=rw[:, R_A, 0:80], in0=B[3][0:4, 0:80], scalar=bi_col[:, 0:1],
                                                                 in1=rw[:, R_F, 0:80], op0=ALU.add, op1=ALU.subtract),
                          R=['B3', 'bi_col', rk], W=[rk])
                        yield ('dve', [rk], [rk])
                        V(lambda: nc.vector.tensor_tensor_scan(out=rw[:, R_M, 64:80], data0=rw[:, R_A, 64:80], data1=rw[:, R_A, 64:80],
                                                               initial=0.0, op0=ALU.max, op1=ALU.max), R=[rk], W=[rk])
                        A3 = rw[:, R_A, 0:64].rearrange("h (j t) -> h j t", t=4)
                        M3 = rw[:, R_M, 0:64].rearrange("h (j t) -> h j t", t=4)
                        yield ('dve', [rk, 'm0r'], [rk])
                        V(lambda: nc.vector.tensor_tensor(out=M3[:, :, 0], in0=m0r[:, :], in1=A3[:, :, 0], op=ALU.max),
                          R=[rk, 'm0r'], W=[rk])
                        for t in range(1, 4):
                            yield ('dve', [rk], [rk])
                            V(lambda: nc.vector.tensor_tensor(out=M3[:, :, t], in0=M3[:, :, t - 1], in1=A3[:, :, t], op=ALU.max),
                              R=[rk], W=[rk])
                        m0b = m0r[:, :].unsqueeze(2).to_broadcast([4, 16, 4])
                        AR3 = rw[:, R_AREL, 0:64].rearrange("h (j t) -> h j t", t=4)
                        MR3 = rw[:, R_MREL, 0:64].rearrange("h (j t) -> h j t", t=4)
                        ME3 = rw[:, R_MEND, 0:64].rearrange("h (j t) -> h j t", t=4)
                        yield ('dve', [rk, 'm0r'], [rk])
                        V(lambda: nc.vector.tensor_tensor(out=AR3, in0=A3, in1=m0b, op=ALU.subtract), R=[rk, 'm0r'], W=[rk])
                        yield ('dve', [rk, 'm0r'], [rk])
                        V(lambda: nc.vector.tensor_tensor(out=MR3, in0=M3, in1=m0b, op=ALU.subtract), R=[rk, 'm0r'], W=[rk])
                        yield ('dve', [rk], [rk])
                        V(lambda: nc.vector.tensor_copy(out=rw[:, R_AREL, 64:80], in_=rw[:, R_A, 64:80]), R=[rk], W=[rk])
                        yield ('dve', [rk], [rk])
                        V(lambda: nc.vector.tensor_copy(out=rw[:, R_MREL, 64:80], in_=rw[:, R_M, 64:80]), R=[rk], W=[rk])
                        yield ('dve', [rk], [rk])
                        V(lambda: nc.vector.tensor_copy(out=ME3, in_=MR3[:, :, 3:4].to_broadcast([4, 16, 4])), R=[rk], W=[rk])
                        yield ('dve', [rk], [rk])
                        V(lambda: nc.vector.tensor_copy(out=rw[:, R_MEND, 64:80], in_=rw[:, R_MREL, 79:80].to_broadcast([4, 16])),
                          R=[rk], W=[rk])
                        yield ('dve', [rk], [rk])
                        V(lambda: nc.vector.tensor_tensor(out=rw[:, R_WS, :], in0=rw[:, R_AREL, :], in1=rw[:, R_MEND, :],
                                                          op=ALU.subtract), R=[rk], W=[rk])
                        yield ('act', [rk], ['decr'])
                        A(lambda: nc.scalar.activation(out=decr[:, :], in_=MR3[:, :, 3], func=AF.Exp, scale=-1.0),
                          R=[rk], W=['decr'])
                    yield ('dve', [rk], [rk])
                    V(lambda: nc.vector.scalar_tensor_tensor(out=rw[:, R_MNEG, :], in0=rw[:, R_F, :], scalar=-1.0, in1=rw[:, R_M, :],
                                                             op0=ALU.mult, op1=ALU.subtract), R=[rk], W=[rk])
                    yield ('act', [rk], [rk])
                    A(lambda: nc.scalar.activation(out=rw[:, R_EMN, :], in_=rw[:, R_MNEG, :], func=AF.Exp), R=[rk], W=[rk])
                    yield ('act', [rk], [rk])
                    A(lambda: nc.scalar.activation(out=rw[:, R_WS, :], in_=rw[:, R_WS, :], func=AF.Exp), R=[rk], W=[rk])
                    if spc:
                        MN3 = rw[:, R_MNEG, 0:64].rearrange("h (j t) -> h j t", t=4)
                        yield ('dve', [rk], ['msr'])
                        V(lambda: nc.vector.tensor_scalar(out=msr[:, :], in0=MN3[:, :, 3], scalar1=-1.0, scalar2=None, op0=ALU.mult),
                          R=[rk], W=['msr'])
                        with nc.allow_non_contiguous_dma(reason="tiny"):
                            yield ('sp', ['msr'], ['m_s'])
                            tk.dma('sp', lambda: nc.sync.dma_start(out=m_s.rearrange("j h -> h j"), in_=msr[:, :]), ['msr'], ['m_s'], 'm_s')
                    if i == 15:
                        yield ('dve', [rk], ['mpr'])
                        V(lambda: nc.vector.tensor_scalar(out=mpr[:, :], in0=rw[:, R_MNEG, 127:128], scalar1=-1.0, scalar2=None,
                                                          op0=ALU.mult), R=[rk], W=['mpr'])
                        with nc.allow_non_contiguous_dma(reason="tiny"):
                            yield ('sp', ['mpr'], ['m_p'])
                            tk.dma('sp', lambda: nc.sync.dma_start(out=m_p.rearrange("o h -> h o"), in_=mpr[:, :]), ['mpr'], ['m_p'], 'm_p')
                    if ckpt('t%dc' % pos):
                        return
                    yield None
                    def fm_round(c0):
                        for c in range(4):
                            for k in range(8):
                                P(lambda: nc.tensor.matmul(B[2][:, c * 128:(c + 1) * 128], lhsT=w_in_sb[:, k, c0 + c * 128:c0 + (c + 1) * 128],
                                                           rhs=X[:, k, :], start=(k == 0), stop=(k == 7)),
                                  R=[xk, 'w_in'], W=['B2'], sig=(c == 3 and k == 7))
                        yield None
                    B2v = B[2][:].rearrange("p (g t) -> p g t", g=4)
                    uk = 'utile%d' % s2
                    so = pos % 3
                    ok_ = 'og%d' % so

                    def tm_round(c0, bank):
                        for k in range(8):
                            P(lambda: nc.tensor.matmul(B[bank][:, :], lhsT=X[:, k, :], rhs=w_in_sb[:, k, c0:c0 + 512],
                                                       start=(k == 0), stop=(k == 7)), R=[xk, 'w_in'], W=['B%d' % bank], sig=(k == 7))

                    def evac_u():
                        if not spc:
                            yield ('act', ['B2'], [uk])
                            A(lambda: nc.scalar.activation(out=utile[s2][:, :, 15:143], in_=B2v, func=AF.Copy), R=['B2'], W=[uk])
                        else:
                            for g in range(4):
                                yield ('act', ['B2'], ['usamp'])
                                A(lambda: nc.scalar.activation(out=usamp[:, 16 * g:16 * g + 16, 15:19],
                                                               in_=B[2][:, g * 128:g * 128 + 64].rearrange("p (j t) -> p j t", t=4),
                                                               func=AF.Copy), R=['B2'], W=['usamp'])
                            yield ('pool', (), ['umeta'])
                            G(lambda: nc.gpsimd.memset(umeta[:, :, 0:15], 0.0), W=['umeta'])
                            yield ('act', ['B2'], ['umeta'])
                            A(lambda: nc.scalar.activation(out=umeta[:, :, 15:31], in_=B2v[:, :, 64:80], func=AF.Copy), R=['B2'], W=['umeta'])

                    yield from fm_round(0)
                    tm_round(1536, 4)
                    yield from evac_u()
                    yield from fm_round(512)
                    yield ('act', ['B4'], ['vt%d' % s2])
                    A(lambda: nc.scalar.activation(out=vt[s2][:], in_=B[4][:, :], func=AF.Copy), R=['B4'], W=['vt%d' % s2])
                    tm_round(2048, 5)
                    yield ('act', ['B2'], ['qT%d' % s2])
                    A(lambda: nc.scalar.activation(out=qT[s2][:], in_=B2v, func=AF.Copy, scale=float(128 ** -0.5)),
                      R=['B2'], W=['qT%d' % s2])
                    yield from fm_round(1024)
                    yield ('act', ['B5'], [ok_])
                    A(lambda: nc.scalar.activation(out=og[so][:], in_=B[5][:, :], func=AF.Exp, scale=-1.0), R=['B5'], W=[ok_])
                    yield ('act', [ok_], [ok_])
                    A(lambda: nc.scalar.activation(out=og[so][:], in_=og[so][:], func=AF.Ln, bias=1.0), R=[ok_], W=[ok_])
                    yield ('dve', ['B2'], ['kT%d' % s2])
                    V(lambda: nc.vector.tensor_copy(out=kT[s2][:], in_=B2v), R=['B2'], W=['kT%d' % s2])
                    yield ('act', [ok_], [ok_])
                    A(lambda: nc.scalar.activation(out=og[so][:], in_=og[so][:], func=AF.Exp, scale=-1.0), R=[ok_], W=[ok_])
                    yield ('pool', [ok_, 'hgB'], [ok_])
                    G(lambda: nc.gpsimd.tensor_tensor(out=og[so][:], in0=og[so][:], in1=hgB[:], op=ALU.mult), R=[ok_, 'hgB'], W=[ok_])
                    if i == 15 or spc:
                        for k in range(8):
                            P(lambda: nc.tensor.matmul(B[4][:, :], lhsT=X[:, k, :], rhs=w_in_sb[:, k, 0:512],
                                                       start=(k == 0), stop=(k == 7)), R=[xk, 'w_in'], W=['B4'], sig=(k == 7))
                        yield ('act', ['B4'], ['Dm'])
                        A(lambda: nc.scalar.activation(out=ut, in_=B[4][:, :], func=AF.Copy), R=['B4'], W=['Dm'])
                        if spc:
                            yield ('sp', ['Dm'], ['pool_s_b'])
                            tk.dma('sp', lambda: nc.sync.dma_start(out=pool_s[:, 11:15, :], in_=ut[0:64, :]), ['Dm'], ['pool_s_b'], 'pool_s_b')
                        else:
                            yield ('sp', ['Dm'], ['pool_p'])
                            tk.dma('sp', lambda: nc.sync.dma_start(out=pool_p[:, :], in_=ut[113:128, :]), ['Dm'], ['pool_p'], 'pool_p')

                    if ckpt('t%de' % pos):
                        return
                    yield None
                    zk = 'zt%d' % (pos % 4)
                    Z = zt[pos % 4]
                    if not spc:
                        U = utile[s2]
                        if i == 0:
                            yield ('pool', ['umeta'], [uk])
                            G(lambda: nc.gpsimd.tensor_copy(out=U[:, :, 0:15], in_=umeta[:, :, 16:31]), R=['umeta'], W=[uk])
                        else:
                            yield ('pool', ['utile%d' % (1 - s2)], [uk])
                            G(lambda: nc.gpsimd.tensor_copy(out=U[:, :, 0:15], in_=utile[1 - s2][:, :, 128:143]),
                              R=['utile%d' % (1 - s2)], W=[uk])
                        Ws = yield from pool_sums(U, Wa, Wb, 1, 143, uk, 'Wa', 'Wb')
                        for g in range(4):
                            yield ('dve', ['Wa', 'Wb', uk], [zk])
                            V(lambda: nc.vector.scalar_tensor_tensor(out=Z[:, g, :], in0=Ws[g][:, g, 15:143], scalar=1.0 / WIN[g],
                                                                     in1=U[:, g, 15:143], op0=ALU.mult, op1=ALU.subtract),
                              R=['Wa', 'Wb', uk], W=[zk])
                    else:
                        for g in range(4):
                            P(lambda: nc.tensor.transpose(out=B[4 + g // 2][:, (g % 2) * 240:(g % 2) * 240 + 128],
                                                          in_=spt[0][:, g * 128:(g + 1) * 128], identity=ident_f[:]),
                              R=['Dt', 'ident_f'], W=['B%d' % (4 + g // 2)], sig=False)
                            P(lambda: nc.tensor.transpose(out=B[4 + g // 2][:, (g % 2) * 240 + 128:(g % 2) * 240 + 240],
                                                          in_=spt[1][0:112, g * 128:(g + 1) * 128], identity=ident_f[0:112, 0:112]),
                              R=['interB', 'ident_f'], W=['B%d' % (4 + g // 2)], sig=True)
                        for g in range(4):
                            yield ('dve', ['B%d' % (4 + g // 2)], ['usamp'])
                            V(lambda: nc.vector.tensor_copy(out=usamp[:, 16 * g:16 * g + 16, 0:15],
                                                            in_=B[4 + g // 2][:, (g % 2) * 240:(g % 2) * 240 + 240].rearrange("p (j i) -> p j i", i=15)),
                              R=['B%d' % (4 + g // 2)], W=['usamp'])
                        yield ('pool', (), [zk])
                        G(lambda: nc.gpsimd.memset(Z[:], 0.0), W=[zk])
                        Ws = yield from pool_sums(usamp, Was, Wbs, 16, 19, 'usamp', 'Was', 'Wbs')
                        for g in range(4):
                            yield ('pool', ['Was', 'Wbs'], ['Was', 'Wbs'])
                            G(lambda: nc.gpsimd.tensor_scalar(out=Ws[g][:, 16 * g:16 * g + 16, 15:19], in0=Ws[g][:, 16 * g:16 * g + 16, 15:19],
                                                              scalar1=1.0 / WIN[g], scalar2=None, op0=ALU.mult), R=['Was', 'Wbs'], W=['Was', 'Wbs'])
                            yield ('pool', ['Was', 'Wbs', 'usamp'], [zk])
                            G(lambda: nc.gpsimd.tensor_tensor(out=Z[:, g, 0:64].rearrange("p (j t) -> p j t", t=4),
                                                              in0=Ws[g][:, 16 * g:16 * g + 16, 15:19],
                                                              in1=usamp[:, 16 * g:16 * g + 16, 15:19], op=ALU.subtract),
                              R=['Was', 'Wbs', 'usamp'], W=[zk])
                        Wm = yield from pool_sums(umeta, Wa[:, :, 0:31], Wb[:, :, 0:31], 1, 31, 'umeta', 'Wa', 'Wb')
                        for g in range(4):
                            yield ('pool', ['Wa', 'Wb', 'rc_meta'], ['Wa', 'Wb'])
                            G(lambda: nc.gpsimd.tensor_tensor(out=Wm[g][:, g, 15:31], in0=Wm[g][:, g, 15:31], in1=rc_meta[:, g, :], op=ALU.mult),
                              R=['Wa', 'Wb', 'rc_meta'], W=['Wa', 'Wb'])
                            yield ('pool', ['Wa', 'Wb', 'umeta'], [zk])
                            G(lambda: nc.gpsimd.tensor_tensor(out=Z[:, g, 64:80], in0=Wm[g][:, g, 15:31], in1=umeta[:, g, 15:31], op=ALU.subtract),
                              R=['Wa', 'Wb', 'umeta'], W=[zk])
                    mk = 'mixT%d' % s2
                    if ckpt('t%df' % pos):
                        return
                    yield 'SPLIT'
                    for n_, r_ in enumerate((R_AREL, R_EMN, R_WS)):
                        P(lambda: nc.tensor.transpose(out=B[3][:, 256 + 4 * n_:260 + 4 * n_], in_=rw[:, r_, :], identity=ident_f[0:4, 0:4]),
                          R=[rk, 'ident_f'], W=['B3'], sig=(n_ == 2))
                    cl_ = cols[s2]
                    yield ('dve', ['B3'], [ck])
                    V(lambda: nc.vector.tensor_copy(out=cl_[:, 0:12], in_=B[3][:, 256:268]), R=['B3'], W=[ck])
                    for h in range(4):
                        P(lambda: nc.tensor.matmul(B[6][:, h * 128:(h + 1) * 128], lhsT=sel[:, h, :], rhs=rw[:, R_MREL, :],
                                                   start=True, stop=True), R=[rk, 'sel'], W=['B6'], sig=(h == 3))

                    B6v = B[6][:].rearrange("p (h t) -> p h t", h=4)
                    B7v = B[7][:].rearrange("p (h t) -> p h t", h=4)
                    for h in range(4):
                        P(lambda: nc.tensor.matmul(B[7][:, h * 128:(h + 1) * 128], lhsT=kT[s2][:, h, :], rhs=qT[s2][:, h, :], start=True, stop=True),
                          R=['kT%d' % s2, 'qT%d' % s2], W=['B7'], sig=(h == 3))
                    for h in range(4):
                        yield ('act', ['B6', ck], ['Dt'])
                        A(lambda: nc.scalar.activation(out=Dt[:, h, :], in_=B[6][:, h * 128:(h + 1) * 128], func=AF.Exp, scale=-1.0,
                                                       bias=cl_[:, h:h + 1]), R=['B6', ck], W=['Dt'])
                    yield ('act', ['B6'], ['interB'])
                    A(lambda: nc.scalar.activation(out=interB[:], in_=B6v, func=AF.Exp, scale=-1.0), R=['B6'], W=['interB'])
                    msk = maskS if spc else maskC
                    yield ('pool', ['Dt', 'maskS', 'maskC'], ['Dm'])
                    G(lambda: nc.gpsimd.tensor_tensor(out=Dm[:], in0=Dt[:], in1=msk[:].unsqueeze(1).to_broadcast([128, 4, 128]), op=ALU.mult),
                      R=['Dt', 'maskS', 'maskC'], W=['Dm'])
                    yield ('dve', ['B7', 'Dm'], ['Pt'])
                    V(lambda: nc.vector.tensor_tensor(out=Pt[:], in0=B7v, in1=Dm[:], op=ALU.mult), R=['B7', 'Dm'], W=['Pt'])
                    yield ('dve', ['qT%d' % s2, 'interB'], ['qs'])
                    V(lambda: nc.vector.tensor_tensor(out=qs[:], in0=qT[s2][:], in1=interB[:], op=ALU.mult), R=['qT%d' % s2, 'interB'], W=['qs'])
                    cb = Cb[s2]; nbb = nb[s2]
                    cbk = 'Cb%d' % s2
                    if spc:
                        for j in range(16):
                            yield ('pool', ['qs', 'selmask'], ['qz'])
                            G(lambda: nc.gpsimd.tensor_tensor(out=qz[:, j, :, :], in0=qs[:, :, 0:64],
                                                              in1=selmask[:, j, :].unsqueeze(1).to_broadcast([128, 4, 64]), op=ALU.mult),
                              R=['qs', 'selmask'], W=['qz'])
                        P(lambda: nc.tensor.transpose(out=B[3][:, 300:364], in_=n0t[:, :], identity=ident_f[0:64, 0:64]),
                          R=['n0t', 'ident_f'], W=['B3'])
                        yield ('dve', ['B3'], ['n0f'])
                        V(lambda: nc.vector.tensor_copy(out=n0f[:], in_=B[3][:, 300:364]), R=['B3'], W=['n0f'])
                        yield ('act', ['B3'], ['n0b'])
                        A(lambda: nc.scalar.activation(out=n0b[:], in_=B[3][:, 300:364], func=AF.Copy), R=['B3'], W=['n0b'])
                    if ckpt('t%dg' % pos):
                        return
                    yield None
                    for h in range(4):
                        hc = slice(h * 128, (h + 1) * 128)
                        if not spc:
                            P(lambda: nc.tensor.matmul(B[6][:, hc], lhsT=Pt[:, h, :], rhs=vt[s2][:, hc], start=True, stop=False),
                              R=['Pt', 'vt%d' % s2], W=['B6'], sig=False)
                            P(lambda: nc.tensor.matmul(B[6][:, hc], lhsT=qs[:, h, :], rhs=cb[:, h, :], start=False, stop=True),
                              R=['qs', cbk], W=['B6'], sig=True)
                            P(lambda: nc.tensor.matmul(B[3][:, 272 + h:273 + h], lhsT=Pt[:, h, :], rhs=ones_bf[:, 0:1], start=True, stop=False),
                              R=['Pt', 'ones_bf'], W=['B3'], sig=False)
                            P(lambda: nc.tensor.matmul(B[3][:, 272 + h:273 + h], lhsT=qs[:, h, :], rhs=nbb[:, h:h + 1], start=False, stop=True),
                              R=['qs', cbk], W=['B3'], sig=True)
                        else:
                            P(lambda: nc.tensor.matmul(B[6][64:128, hc], lhsT=Pt[:, h, 64:128], rhs=vt[s2][:, hc], start=True, stop=True),
                              R=['Pt', 'vt%d' % s2], W=['B6'], sig=False)
                            P(lambda: nc.tensor.matmul(B[3][64:128, 272 + h:273 + h], lhsT=Pt[:, h, 64:128], rhs=ones_bf[:, 0:1], start=True, stop=True),
                              R=['Pt', 'ones_bf'], W=['B3'], sig=False)
                            for j in range(16):
                                P(lambda: nc.tensor.matmul(B[6][0:64, hc], lhsT=qz[:, j, h, :], rhs=C0b[:, j * 4 + h, :],
                                                           start=(j == 0), stop=False), R=['qz', 'C0b'], W=['B6'], sig=False)
                            P(lambda: nc.tensor.matmul(B[6][0:64, hc], lhsT=Pt[:, h, 0:64], rhs=vt[s2][:, hc], start=False, stop=True),
                              R=['Pt', 'vt%d' % s2], W=['B6'], sig=True)
                            for j in range(16):
                                P(lambda: nc.tensor.matmul(B[3][0:64, 272 + h:273 + h], lhsT=qz[:, j, h, :], rhs=n0b[:, j * 4 + h:j * 4 + h + 1],
                                                           start=(j == 0), stop=False), R=['qz', 'n0b'], W=['B3'], sig=False)
                            P(lambda: nc.tensor.matmul(B[3][0:64, 272 + h:273 + h], lhsT=Pt[:, h, 0:64], rhs=ones_bf[:, 0:1], start=False, stop=True),
                              R=['Pt', 'ones_bf'], W=['B3'], sig=True)
                    if ckpt('t%dh' % pos):
                        return
                    yield None
                    nS = numS[s2]
                    nSk = 'numS%d' % s2
                    yield ('act', ['B6'], [nSk])
                    A(lambda: nc.scalar.activation(out=nS[:], in_=B[6][:, :], func=AF.Copy), R=['B6'], W=[nSk])
                    c_ = cl_
                    yield ('dve', ['B3'], [ck])
                    V(lambda: nc.vector.tensor_copy(out=c_[:, 12:16], in_=B[3][:, 272:276]), R=['B3'], W=[ck])
                    if ckpt('t%di' % pos):
                        return
                    yield None
                    for h in range(4):
                        P(lambda: nc.tensor.transpose(out=B1b[:, 512 + h * 128:512 + (h + 1) * 128], in_=kT[s2][:, h, :], identity=ident_bf[:]),
                          R=['kT%d' % s2, 'ident_bf'], W=['B1'], sig=(h == 3))
                    if ckpt('t%di1' % pos):
                        return
                    yield None
                    B1k = B1b[:, 512:1024].rearrange("p (h t) -> p h t", h=4)
                    if spc:
                        for h in range(4):
                            yield ('dve', ['maskcols', ck], ['wsm'])
                            V(lambda: nc.vector.tensor_scalar(out=wsm[:, :, h], in0=maskcols[:, :], scalar1=c_[:, 8 + h:9 + h], scalar2=None,
                                                              op0=ALU.mult), R=['maskcols', ck], W=['wsm'])
                        yield ('act', ['B1'], ['ktm'])
                        A(lambda: nc.scalar.activation(out=ktm[:], in_=B1k, func=AF.Copy), R=['B1'], W=['ktm'])
                        wsx = wsm[:, 16, :]
                        wk = ['wsm']
                    else:
                        wsx = c_[:, 8:12]
                        wk = [ck]
                    if ckpt('t%di2' % pos):
                        return
                    yield None
                    yield ('dve', ['B1'] + wk, ['kw'])
                    V(lambda: nc.vector.tensor_tensor(out=kw[:], in0=B1k, in1=wsx.unsqueeze(2).to_broadcast([128, 4, 128]), op=ALU.mult),
                      R=['B1'] + wk, W=['kw'])
                    if ckpt('t%di3' % pos):
                        return
                    yield None
                    for h in range(4):
                        P(lambda: nc.tensor.matmul(B[7][:, h * 128:(h + 1) * 128], lhsT=kw[:, h, :], rhs=vt[s2][:, h * 128:(h + 1) * 128],
                                                   start=True, stop=True), R=['kw', 'vt%d' % s2], W=['B7'], sig=False)
                        P(lambda: nc.tensor.matmul(B[3][:, 280 + h:281 + h], lhsT=kw[:, h, :], rhs=ones_bf[:, 0:1], start=True, stop=True),
                          R=['kw', 'ones_bf'], W=['B3'], sig=(h == 3))
                    if ckpt('t%di4' % pos):
                        return
                    yield None
                    for h in range(4):
                        yield ('dve', ['Cf', 'interB', 'B7'], ['Cf'])
                        V(lambda: nc.vector.scalar_tensor_tensor(out=Cf[:, h, :], in0=Cf[:, h, :], scalar=interB[:, h, 127:128],
                                                                 in1=B[7][:, h * 128:(h + 1) * 128], op0=ALU.mult, op1=ALU.add),
                          R=['Cf', 'interB', 'B7'], W=['Cf'])
                    if ckpt('t%di5' % pos):
                        return
                    yield None
                    yield ('dve', ['nf', 'interB'], ['nf'])
                    V(lambda: nc.vector.tensor_tensor(out=nf[:], in0=nf[:], in1=interB[:, :, 127], op=ALU.mult), R=['nf', 'interB'], W=['nf'])
                    yield ('dve', ['nf', 'B3'], ['nf'])
                    V(lambda: nc.vector.tensor_tensor(out=nf[:], in0=nf[:], in1=B[3][:, 280:284], op=ALU.add), R=['nf', 'B3'], W=['nf'])
                    if ckpt('t%di6' % pos):
                        return
                    yield None
                    ncb = Cb[1 - s2]; nnb = nb[1 - s2]
                    yield ('act', ['Cf'], ['Cb%d' % (1 - s2)])
                    A(lambda: nc.scalar.activation(out=ncb[:], in_=Cf[:], func=AF.Copy), R=['Cf'], W=['Cb%d' % (1 - s2)])
                    yield ('act', ['nf'], ['Cb%d' % (1 - s2)])
                    A(lambda: nc.scalar.activation(out=nnb[:], in_=nf[:], func=AF.Copy), R=['nf'], W=['Cb%d' % (1 - s2)])
                    if i == 15:
                        yield ('sp', ['Cf'], ['C_p'])
                        tk.dma('sp', lambda: nc.sync.dma_start(out=C_p.rearrange("h k v -> k h v"), in_=Cf[:]), ['Cf'], ['C_p'], 'C_p')
                        P(lambda: nc.tensor.transpose(out=B[3][0:4, 384:512], in_=nf[:, :], identity=ident_f[:]), R=['nf', 'ident_f'], W=['B3'])
                        yield ('dve', ['B3'], ['npt'])
                        V(lambda: nc.vector.tensor_copy(out=npt[:], in_=B[3][0:4, 384:512]), R=['B3'], W=['npt'])
                        yield ('sp', ['npt'], ['n_p'])
                        tk.dma('sp', lambda: nc.sync.dma_start(out=n_p[:, :], in_=npt[:]), ['npt'], ['n_p'], 'n_p')
                    if ckpt('t%dj' % pos):
                        return
                    yield None
                    if spc:
                        for h in range(4):
                            P(lambda: nc.tensor.matmul(B[3][:, 400 + h * 16:416 + h * 16], lhsT=sel[:, h, :], rhs=decr[:, :], start=True, stop=True),
                              R=['sel', 'decr'], W=['B3'], sig=(h == 3))
                        yield ('dve', ['B3'], ['decS'])
                        V(lambda: nc.vector.tensor_copy(out=decS[:], in_=B[3][:, 400:464].rearrange("p (h j) -> p h j", h=4)), R=['B3'], W=['decS'])
                        sCv = sC.rearrange("j h k v -> k (j h) v")
                        Csv = C_s.rearrange("j h k v -> k (j h) v")
                        def ld_eighth(e_):
                            bf_ = C0f[e_ % 4]
                            bk_ = 'C0f%d' % (e_ % 4)
                            wk_ = [bk_] + (['usamp', 'Was', 'Wbs'] if (e_ % 4) >= 2 else [])
                            tk.dma('sp', lambda: nc.sync.dma_start(out=bf_[:], in_=sCv[:, 8 * e_:8 * e_ + 8, :]), [], wk_, 'ld' + bk_)
                        yield ('sp', [], ['C0f2', 'usamp', 'Was', 'Wbs'])
                        ld_eighth(2)
                        for ei in range(8):
                            cf = C0f[ei % 4]
                            cfk = 'C0f%d' % (ei % 4)
                            if ei + 3 < 8:
                                yield ('sp', [], ['C0f%d' % ((ei + 3) % 4)])
                                ld_eighth(ei + 3)
                            for jj in range(2):
                                j = 2 * ei + jj
                                kz = kwz[j % 2]
                                kzk = 'kwz%d' % (j % 2)
                                yield ('pool', ['ktm', 'wsm'], [kzk])
                                G(lambda: nc.gpsimd.tensor_tensor(out=kz[:], in0=ktm[:], in1=wsm[:, j, :].unsqueeze(2).to_broadcast([128, 4, 128]),
                                                                  op=ALU.mult), R=['ktm', 'wsm'], W=[kzk])
                                for h in range(4):
                                    P(lambda: nc.tensor.matmul(B[7][:, h * 128:(h + 1) * 128], lhsT=kz[:, h, :], rhs=vt[s2][:, h * 128:(h + 1) * 128],
                                                               start=True, stop=True), R=[kzk, 'vt%d' % s2], W=['B7'], sig=False)
                                    P(lambda: nc.tensor.matmul(B[3][:, 300 + h * 16 + j:301 + h * 16 + j], lhsT=kz[:, h, :], rhs=ones_bf[:, 0:1],
                                                               start=True, stop=True), R=[kzk, 'ones_bf'], W=['B3'], sig=(h == 3))
                                for h in range(4):
                                    yield ('dve', [cfk, 'decS', 'B7'], [cfk])
                                    V(lambda: nc.vector.scalar_tensor_tensor(out=cf[:, jj * 4 + h, :], in0=cf[:, jj * 4 + h, :],
                                                                             scalar=decS[:, h, j:j + 1], in1=B[7][:, h * 128:(h + 1) * 128],
                                                                             op0=ALU.mult, op1=ALU.add), R=[cfk, 'decS', 'B7'], W=[cfk])
                            yield ('sp', [cfk], ['C_s%d' % ei])
                            tk.dma('sp', lambda: nc.sync.dma_start(out=Csv[:, 8 * ei:8 * ei + 8, :], in_=cf[:]), [cfk], ['C_s%d' % ei], 'st' + cfk)
                        yield ('dve', ['n0f', 'decS'], ['n1s'])
                        V(lambda: nc.vector.tensor_tensor(out=n1s[:], in0=n0f[:].rearrange("p (j h) -> p h j", h=4), in1=decS[:], op=ALU.mult),
                          R=['n0f', 'decS'], W=['n1s'])
                        yield ('dve', ['n1s', 'B3'], ['n1s'])
                        V(lambda: nc.vector.tensor_tensor(out=n1s[:], in0=n1s[:], in1=B[3][:, 300:364].rearrange("p (h j) -> p h j", h=4), op=ALU.add),
                          R=['n1s', 'B3'], W=['n1s'])
                        P(lambda: nc.tensor.transpose(out=B[3][0:64, 384:512], in_=n1s[:].rearrange("p h j -> p (h j)"), identity=ident_f[:]),
                          R=['n1s', 'ident_f'], W=['B3'])
                        yield ('dve', ['B3'], ['n1t'])
                        V(lambda: nc.vector.tensor_copy(out=n1t[:], in_=B[3][0:64, 384:512]), R=['B3'], W=['n1t'])
                        n_s_v = n_s.rearrange("j h k -> h j k")
                        for h in range(4):
                            yield ('sp', ['n1t'], ['n_s%d' % h])
                            tk.dma('sp', lambda: nc.sync.dma_start(out=n_s_v[h], in_=n1t[16 * h:16 * h + 16, :]), ['n1t'], ['n_s%d' % h], 'n_s')

                    yield 'SPLIT'
                    xr_ = xr[pos % 2]
                    xrk = 'xr%d' % (pos % 2)
                    load_into(xr_, xrk, i, 'xr%d' % (pos % 2))
                    c_ = cl_
                    yield ('dve', [ck], [ck])
                    V(lambda: nc.vector.scalar_tensor_tensor(out=c_[:, 12:16], in0=c_[:, 12:16], scalar=-1.0, in1=c_[:, 12:16],
                                                             op0=ALU.mult, op1=ALU.max), R=[ck], W=[ck])
                    yield ('dve', [ck], [ck])
                    V(lambda: nc.vector.tensor_tensor(out=c_[:, 12:16], in0=c_[:, 12:16], in1=c_[:, 4:8], op=ALU.max), R=[ck], W=[ck])
                    yield ('dve', [ck], [ck])
                    V(lambda: nc.vector.reciprocal(out=c_[:, 12:16], in_=c_[:, 12:16]), R=[ck], W=[ck])
                    for h in range(4):
                        yield ('act', [nSk], ['hj', ck])
                        A(lambda: nc.scalar.activation(out=hj[:], in_=nS[:, h * 128:(h + 1) * 128], func=AF.Square,
                                                       accum_out=c_[:, 16 + h:17 + h]), R=[nSk], W=['hj', ck])
                    yield ('dve', [ck], [ck])
                    V(lambda: nc.vector.tensor_tensor(out=c_[:, 16:20], in0=c_[:, 16:20], in1=c_[:, 12:16], op=ALU.mult), R=[ck], W=[ck])
                    yield ('dve', [ck], [ck])
                    V(lambda: nc.vector.tensor_tensor(out=c_[:, 16:20], in0=c_[:, 16:20], in1=c_[:, 12:16], op=ALU.mult), R=[ck], W=[ck])
                    yield ('act', [ck], [ck])
                    A(lambda: nc.scalar.activation(out=c_[:, 20:24], in_=c_[:, 16:20], func=AF.Ln, scale=1.0 / 128, bias=EPS), R=[ck], W=[ck])
                    yield ('act', [ck], [ck])
                    A(lambda: nc.scalar.activation(out=c_[:, 20:24], in_=c_[:, 20:24], func=AF.Exp, scale=-0.5), R=[ck], W=[ck])
                    yield ('dve', [ck], [ck])
                    V(lambda: nc.vector.tensor_tensor(out=c_[:, 20:24], in0=c_[:, 20:24], in1=c_[:, 12:16], op=ALU.mult), R=[ck], W=[ck])
                    for h in range(4):
                        yield ('dve', [nSk, ck, ok_], ['mixh'])
                        V(lambda: nc.vector.scalar_tensor_tensor(out=mixh[:, h * 128:(h + 1) * 128], in0=nS[:, h * 128:(h + 1) * 128],
                                                                 scalar=c_[:, 20 + h:21 + h], in1=og[so][:, h * 128:(h + 1) * 128],
                                                                 op0=ALU.mult, op1=ALU.mult), R=[nSk, ck, ok_], W=['mixh'])
                    for h in range(4):
                        P(lambda: nc.tensor.transpose(out=B1b[:, h * 128:(h + 1) * 128], in_=mixh[:, h * 128:(h + 1) * 128], identity=ident_bf[:]),
                          R=['mixh', 'ident_bf'], W=['B1'], sig=(h == 3))
                    yield ('act', ['B1'], [mk])
                    A(lambda: nc.scalar.activation(out=mixT[s2][:, 4:8, :], in_=B1b[:, 0:512].rearrange("p (h t) -> p h t", h=4), func=AF.Copy),
                      R=['B1'], W=[mk])

                    if ckpt('t%dk' % pos):
                        return
                    yield 'SPLIT'
                    for g in range(4):
                        P(lambda: nc.tensor.matmul(B[2][:, g * 128:(g + 1) * 128], lhsT=w_pool_sb[:, g, :], rhs=Z[:, g, :], start=True, stop=True),
                          R=[zk, 'w_pool'], W=['B2'], sig=(g == 3))
                    yield ('dve', ['B2', 'pscol'], [mk])
                    V(lambda: nc.vector.tensor_tensor(out=mixT[s2][:, 0:4, :], in0=B2v, in1=pscol[:].unsqueeze(2).to_broadcast([128, 4, 128]),
                                                      op=ALU.mult), R=['B2', 'pscol'], W=[mk])

                    for half in range(2):
                        for k in range(8):
                            P(lambda: nc.tensor.matmul(B[4 + half][:, :], lhsT=mixT[s2][:, k, :], rhs=w_out_sb[:, k, half * 512:(half + 1) * 512],
                                                       start=(k == 0), stop=(k == 7)), R=[mk, 'w_out'], W=['B%d' % (4 + half)], sig=(k == 7))
                    for half in range(2):
                        yield ('dve', [xrk, 'B%d' % (4 + half)], [xrk])
                        V(lambda: nc.vector.tensor_tensor(out=xr_[:, half * 512:(half + 1) * 512], in0=xr_[:, half * 512:(half + 1) * 512],
                                                          in1=B[4 + half][:, :], op=ALU.add), R=[xrk, 'B%d' % (4 + half)], W=[xrk])
                    yield ('sp', [xrk], [x2key(i)])
                    tk.dma('sp', lambda: nc.sync.dma_start(out=x2s[128 * i:128 * i + 128, :], in_=xr_[:]), [xrk], [x2key(i)], 'x2st%d' % (pos % 2))

                gens = []
                pending = {}
                pos_next = 0
                load_x(1)
                tk.dma('sp', lambda: nc.sync.dma_start(out=spt[0][:, :], in_=spool.rearrange("j i c -> (j i) c")[0:128, :]),
                       [], ['Dt'], 'spt0')
                tk.dma('sp', lambda: nc.sync.dma_start(out=spt[1][0:112, :], in_=spool.rearrange("j i c -> (j i) c")[128:240, :]),
                       [], ['interB'], 'spt1')
                tk.dma('sp', lambda: nc.sync.dma_start(out=n0t[:, :], in_=sn[:, :]), [], ['n0t'], 'n0t')
                with nc.allow_non_contiguous_dma(reason="tiny"):
                    tk.dma('sp', lambda: nc.sync.dma_start(out=m0r[:, :], in_=sm_.rearrange("j h -> h j")), [], ['m0r'], 'm0r')
                tk.dma('pool', lambda: nc.gpsimd.dma_start(out=C0b[:], in_=sC.rearrange("j h k v -> k (j h) v")),
                       [], ['C0b', 'stg0', 'stg1', 'stg2', 'stg3'], 'C0b')
                tk.dma('sp', lambda: nc.sync.dma_start(out=C0f[0][:], in_=sC.rearrange("j h k v -> k (j h) v")[:, 0:8, :]), [], ['C0f0', 'stg1', 'stg2'], 'ldC0f0')
                tk.dma('sp', lambda: nc.sync.dma_start(out=C0f[1][:], in_=sC.rearrange("j h k v -> k (j h) v")[:, 8:16, :]), [], ['C0f1', 'stg1', 'stg2'], 'ldC0f1')
                tk.dma('sp', lambda: nc.sync.dma_start(out=pool_s[:, 0:11, :], in_=spool[:, 4:15, :]), [], ['pool_s_a'], 'pool_s_a')


                def advance(g_):
                    while True:
                        try:
                            v = next(g_)
                        except StopIteration:
                            v = 'END'
                        if v is None:
                            continue
                        pending[id(g_)] = v
                        return

                while gens or pos_next < NT:
                    if pos_next < NT:
                        if 0 < pos_next and pos_next + 1 < NT:
                            load_x(pos_next + 1)
                        g_new = mixer_tile(order[pos_next], pos_next)
                        gens.append(g_new)
                        pending[id(g_new)] = 'SPLIT'
                        pos_next += 1
                    if 7 <= pos_next <= 12:
                        w_up_v0 = w_up.rearrange("(k p) f -> p k f", p=128)
                        for blk in [pos_next - 7]:
                            dst = arena[:, blk * 2048:(blk + 1) * 2048].bitcast(BF16).rearrange("p (k c) -> p k c", k=8)
                            tk.dma('pool', lambda: nc.gpsimd.dma_start(out=dst, in_=w_up_v0[:, :, blk * 512:(blk + 1) * 512]),
                                   [], ['w_up%d' % blk, 'C0b', 'C0f0', 'C0f1', 'C0f2', 'C0f3', 'qz', 'usamp', 'Was', 'Wbs', 'kwz0'], 'w_up%d' % blk)
                    for g_ in gens:
                        advance(g_)
                    while True:
                        best = None
                        best_t = None
                        for gi_, g_ in enumerate(gens):
                            d_ = pending[id(g_)]
                            if isinstance(d_, tuple):
                                t_ = tk.estimate(d_[0], d_[1], d_[2]) + BETA * gi_
                                if best is None or t_ < best_t:
                                    best, best_t = g_, t_
                        if best is None:
                            break
                        if SCHED == 'rr':
                            for g_ in list(gens):
                                if isinstance(pending[id(g_)], tuple):
                                    advance(g_)
                                    junk_mm()
                        else:
                            advance(best)
                            junk_mm()
                    gens = [g_ for g_ in gens if pending[id(g_)] != 'END']
                    if stop[0]:
                        break

                P(lambda: nc.tensor.matmul(B[0][:, 0:128], lhsT=ident_bf[:], rhs=ident_bf[:], start=True, stop=True), R=['ident_bf'], W=['B0'], sig=True)
                tk.barrier()
                ckpt('mixer')
                if stop[0]:
                    tk.finish()
                    return nc

            with ExitStack() as fs:
                w_up_hi = sb(fs, "w_up_hi", [128, 2, 8, 512], BF16)

                def wup(blk):
                    if blk < 6:
                        return arena[:, blk * 2048:(blk + 1) * 2048].bitcast(BF16).rearrange("p (k c) -> p k c", k=8)
                    return w_up_hi[:, blk - 6, :, :]
                w_dn_sb = sb(fs, "w_dn_sb", [128, 32, D], BF16)
                w_up_v = w_up.rearrange("(k p) f -> p k f", p=128)
                w_dn_v = w_down.rearrange("(f p) d -> p f d", p=128)
                for blk in range(6, 8):
                    tk.dma('pool', lambda: nc.gpsimd.dma_start(out=w_up_hi[:, blk - 6, :, :], in_=w_up_v[:, :, blk * 512:(blk + 1) * 512]),
                           [], ['w_up%d' % blk], 'w_up%d' % blk)
                for blk in range(8):
                    tk.dma('pool', lambda: nc.gpsimd.dma_start(out=w_dn_sb[:, blk * 4:(blk + 1) * 4, :], in_=w_dn_v[:, blk * 4:(blk + 1) * 4, :]),
                           [], ['w_dn%d' % blk], 'w_dn%d' % blk)
                NXF = 6
                xf = [sb(fs, "xf%d" % i, [128, D]) for i in range(NXF)]
                junk2 = sb(fs, "junk2", [128, D], BF16)
                xn2 = [sb(fs, "xn2_%d" % i, [128, D], BF16) for i in range(2)]
                st2 = [sb(fs, "st2_%d" % i, [128, 8]) for i in range(2)]
                xn2T = [sb(fs, "xn2T%d" % i, [128, 8, 256], BF16) for i in range(2)]
                aT = [sb(fs, "aT%d" % i, [128, 32, 256], BF16) for i in range(2)]
                rr = [sb(fs, "rr%d" % i, [128, 2, 256]) for i in range(2)]

                groups = [[2 * g, 2 * g + 1] for g in range(8)] + [[SP]]
                tiles_in_order = [t for g in groups for t in g]

                slot_of = {}
                free_slots = list(range(NXF))
                load_queue = list(tiles_in_order)

                def prefetch():
                    while free_slots and load_queue:
                        i = load_queue.pop(0)
                        s = free_slots.pop(0)
                        slot_of[i] = s
                        tk.dma('sp', lambda: nc.sync.dma_start(out=xf[s][:], in_=x2s[128 * i:128 * i + 128, :]), [x2key(i)], ['xf%d' % s], 'xf%d' % s)

                prefetch()
                tcount = [0]
                evc = [0]

                xn_slot = {}

                def ffn_norm(gi, tl, only=None):
                    for ti, i in enumerate(tl):
                        if only is not None and ti != only:
                            continue
                        n = tcount[0]; tcount[0] += 1
                        s = slot_of[i]
                        s2 = n % 2
                        xn_slot[i] = s2
                        sk = 'st2_%d' % s2
                        xkey = 'xf%d' % s
                        A(lambda: nc.scalar.activation(out=junk2[:], in_=xf[s][:], func=AF.Square, accum_out=st2[s2][:, 0:1]),
                          R=[xkey], W=['junk2', sk])
                        A(lambda: nc.scalar.activation(out=st2[s2][:, 1:2], in_=st2[s2][:, 0:1], func=AF.Ln, scale=1.0 / D, bias=EPS), R=[sk], W=[sk])
                        A(lambda: nc.scalar.activation(out=st2[s2][:, 1:2], in_=st2[s2][:, 1:2], func=AF.Exp, scale=-0.5), R=[sk], W=[sk])
                        A(lambda: nc.scalar.activation(out=xn2[s2][:], in_=xf[s][:], func=AF.Copy, scale=st2[s2][:, 1:2]),
                          R=[xkey, sk], W=['xn2_%d' % s2])

                def ffn_tr(gi, tl, only=None):
                    gs = gi % 2
                    xTk = 'xn2T%d' % gs
                    for ti, i in enumerate(tl):
                        if only is not None and ti != only:
                            continue
                        s2 = xn_slot[i]
                        for k in range(8):
                            P(lambda: nc.tensor.transpose(out=B0b[:, k * 128:(k + 1) * 128], in_=xn2[s2][:, k * 128:(k + 1) * 128], identity=ident_bf[:]),
                              R=['xn2_%d' % s2, 'ident_bf'], W=['B0'], sig=(k == 7))
                        V(lambda: nc.vector.tensor_tensor(out=xn2T[gs][:, :, ti * 128:(ti + 1) * 128], in0=B0b.rearrange("p (k t) -> p k t", k=8),
                                                          in1=gcol2[:].unsqueeze(2).to_broadcast([128, 8, 128]), op=ALU.mult),
                          R=['B0', 'gcol2'], W=[xTk])

                def ffn_up(gi, tl, hook=None):
                    gs = gi % 2
                    ntok = 64 if tl == [SP] else 128 * len(tl)
                    xTk = 'xn2T%d' % gs
                    ak = 'aT%d' % gs
                    for fp in range(16):
                        if hook is not None and fp in (3, 9):
                            hook(0 if fp == 3 else 1)
                        e = evc[0]; evc[0] += 1
                        bank = 1 + e % 3
                        bkey = 'B%d' % bank
                        for hf in range(2):
                            f = 2 * fp + hf
                            for k in range(8):
                                P(lambda: nc.tensor.matmul(B[bank][:, hf * 256:hf * 256 + ntok], lhsT=wup(f // 4)[:, k, (f % 4) * 128:(f % 4 + 1) * 128],
                                                           rhs=xn2T[gs][:, k, 0:ntok], start=(k == 0), stop=(k == 7)),
                                  R=[xTk, 'w_up%d' % (f // 4)], W=[bkey], sig=(hf == 1 and k == 7))
                        r_ = rr[e % 2]
                        rk_ = 'rr%d' % (e % 2)
                        A(lambda: nc.scalar.activation(out=r_[:, :, 0:ntok], in_=B[bank][:].rearrange("p (a t) -> p a t", a=2)[:, :, 0:ntok],
                                                       func=AF.Relu), R=[bkey], W=[rk_])
                        V(lambda: nc.vector.tensor_tensor(out=aT[gs][:, 2 * fp:2 * fp + 2, 0:ntok], in0=r_[:, :, 0:ntok], in1=r_[:, :, 0:ntok],
                                                          op=ALU.mult), R=[rk_], W=[ak])

                def ffn_down(gi, tl, only=None):
                    gs = gi % 2
                    ak = 'aT%d' % gs
                    for ti, i in enumerate(tl):
                        if only is not None and ti != only:
                            continue
                        s = slot_of[i]
                        xkey = 'xf%d' % s
                        db = 4 + 2 * (ti % 2)
                        nr = 64 if i == SP else 128
                        for half in range(2):
                            for f in range(32):
                                P(lambda: nc.tensor.matmul(B[db + half][0:nr, :], lhsT=aT[gs][:, f, ti * 128:ti * 128 + nr],
                                                           rhs=w_dn_sb[:, f, half * 512:(half + 1) * 512], start=(f == 0), stop=(f == 31)),
                                  R=[ak, 'w_dn%d' % (f // 4)], W=['B%d' % (db + half)], sig=(f == 31))
                        for half in range(2):
                            V(lambda: nc.vector.tensor_tensor(out=xf[s][0:nr, half * 512:(half + 1) * 512], in0=xf[s][0:nr, half * 512:(half + 1) * 512],
                                                              in1=B[db + half][0:nr, :], op=ALU.add), R=[xkey, 'B%d' % (db + half)], W=[xkey])
                        s2 = ti % 2
                        sk = 'st2_%d' % s2
                        A(lambda: nc.scalar.activation(out=junk2[0:nr, :], in_=xf[s][0:nr, :], func=AF.Square, accum_out=st2[s2][0:nr, 2:3]), R=[xkey], W=['junk2', sk])
                        A(lambda: nc.scalar.activation(out=st2[s2][0:nr, 3:4], in_=st2[s2][0:nr, 2:3], func=AF.Ln, scale=1.0 / D, bias=EPS), R=[sk], W=[sk])
                        A(lambda: nc.scalar.activation(out=st2[s2][0:nr, 3:4], in_=st2[s2][0:nr, 3:4], func=AF.Exp, scale=-0.5), R=[sk], W=[sk])
                        V(lambda: nc.vector.scalar_tensor_tensor(out=xf[s][0:nr, :], in0=xf[s][0:nr, :], scalar=st2[s2][0:nr, 3:4], in1=gfB[0:nr, :],
                                                                 op0=ALU.mult, op1=ALU.mult), R=[xkey, sk, 'gfB'], W=[xkey])
                        if i == SP:
                            tk.dma('sp', lambda: nc.sync.dma_start(out=ys[:, :], in_=xf[s][0:64, :]), [xkey], ['ys'], 'yst%d' % s)
                        else:
                            tk.dma('sp', lambda: nc.sync.dma_start(out=yp[128 * i:128 * i + 128, :], in_=xf[s][:]), [xkey], ['yp%d' % i], 'yst%d' % s)
                        free_slots.append(s)
                        prefetch()

                ng = len(groups)

                def nhook(g_):
                    if g_ >= ng:
                        return None
                    return lambda t_: (ffn_norm(g_, groups[g_], only=t_) if t_ < len(groups[g_]) else None)

                def tr_down(g_next, g_cur):
                    for t_ in range(2):
                        if g_next < ng and t_ < len(groups[g_next]):
                            ffn_tr(g_next, groups[g_next], only=t_)
                        if t_ < len(groups[g_cur]):
                            ffn_down(g_cur, groups[g_cur], only=t_)

                ffn_norm(0, groups[0]); ffn_tr(0, groups[0])
                ffn_up(0, groups[0], nhook(1)); ffn_tr(1, groups[1])
                ffn_up(1, groups[1], nhook(2))
                tr_down(2, 0)
                ffn_down(1, groups[1])
                for gi in range(2, ng):
                    ffn_up(gi, groups[gi], nhook(gi + 1))
                    tr_down(gi + 1, gi)


                pass
        except _Stop:
            ms.close()
            tk.barrier()
        tk.finish()
    return nc


_NC_CACHE = {}


def kernel(x_prompt, x_sample, state_pool, state_C, state_n, state_m, meta_tokens, norm1, w_in,
           b_gate, w_pool, pool_scale, head_gain, w_out, norm2, w_up, w_down, norm_f):
    f = lambda a: np.ascontiguousarray(np.asarray(a, dtype=np.float32))
    if 'nc' not in _NC_CACHE:
        _NC_CACHE['nc'] = build_nc()
    nc = _NC_CACHE['nc']
    x_prompt = f(x_prompt); x_sample = f(x_sample)
    shared = {
        "meta": f(meta_tokens), "norm1": f(norm1).reshape(D), "w_in": f(w_in).reshape(D, INC),
        "b_gate": f(b_gate).reshape(8), "w_pool": f(w_pool).reshape(4, 128, 128),
        "pool_scale": f(pool_scale).reshape(512), "head_gain": f(head_gain).reshape(512),
        "w_out": f(w_out).reshape(D, D), "norm2": f(norm2).reshape(D), "w_up": f(w_up).reshape(D, DFF),
        "w_down": f(w_down).reshape(DFF, D), "norm_f": f(norm_f).reshape(D),
    }
    sp_ = f(state_pool)[0]; sc_ = f(state_C)[0]; sn_ = f(state_n)[0]; sm_ = f(state_m)[0]
    in_maps = []
    for c in range(NCORES):
        m = dict(shared)
        m["xp"] = x_prompt[c]
        m["xs"] = f(x_sample[16 * c:16 * c + 16].reshape(64, D))
        m["spool"] = f(sp_[16 * c:16 * c + 16])
        m["sC"] = f(sc_[16 * c:16 * c + 16])
        m["sn"] = f(sn_[16 * c:16 * c + 16].reshape(64, 128))
        m["sm"] = f(sm_[16 * c:16 * c + 16])
        in_maps.append(m)
    res = run_bass_kernel_spmd(nc, in_maps, core_ids=list(range(NCORES)))
    rs = res.results
    cat = lambda k: np.stack([np.asarray(r[k], dtype=np.float32) for r in rs], axis=0)
    y_prompt = cat("yp")
    y_sample = np.concatenate([np.asarray(r["ys"], dtype=np.float32).reshape(16, 4, D) for r in rs], axis=0)
    pool_prompt = cat("pool_p")[None]
    C_prompt = cat("C_p")[None]
    n_prompt = cat("n_p")[None]
    m_prompt = cat("m_p").reshape(NCORES, 4)[None]
    pool_sample = np.concatenate([np.asarray(r["pool_s"], dtype=np.float32) for r in rs], axis=0)[None]
    C_sample = np.concatenate([np.asarray(r["C_s"], dtype=np.float32) for r in rs], axis=0)[None]
    n_sample = np.concatenate([np.asarray(r["n_s"], dtype=np.float32) for r in rs], axis=0)[None]
    m_sample = np.concatenate([np.asarray(r["m_s"], dtype=np.float32) for r in rs], axis=0)[None]
    return (y_prompt, y_sample, pool_prompt, C_prompt, n_prompt, m_prompt,
            pool_sample, C_sample, n_sample, m_sample)
```
